# Optimizing a Trainium2 kernel written in Bass

```python
import math
import jax, jax.numpy as jnp
from jax import lax
import numpy as np

D_MODEL = 1024
BATCH = 16
SEQ = 256
DEPTH = 2
DEC_BATCH = 4
DEC_SEQ = 1024
PAST_LEN = 512

GRID_W = 64
N_EVEN = (DEPTH + 1) // 2
N_ODD = DEPTH // 2
S5_WIDTH = D_MODEL // 2
S5_GROUP_CH = 16
S5_GROUPS = S5_WIDTH // S5_GROUP_CH
S5_STATE = 64
GLA_HEADS = 4
GLA_VW = D_MODEL // 2
GLA_DV = GLA_VW // GLA_HEADS
GLA_DK = GLA_DV // 2
GLA_QK = GLA_HEADS * GLA_DK
GLA_RANK = 16
GLA_TAU = 16.0
GLA_CHUNK = 64
IN_EVEN = S5_WIDTH + 2 * GLA_QK + 2 * GLA_VW + 2 * GLA_RANK
MIX_EVEN = S5_WIDTH + GLA_VW
HEAD_DIM = 128
N_HEADS = D_MODEL // HEAD_DIM
KV_HEADS = 2
QKV_W = (N_HEADS + 2 * KV_HEADS) * HEAD_DIM
Q_BLOCK = 128
AXIS_DIM = HEAD_DIM // 2
ROPE_THETA = 10000.0
D_FF = 4 * D_MODEL
EPS = 1e-6

kernel_name = "hybrid_s5_gla_gqa_context_prefix_step"


def rms_norm(x, g):
    xf = x.astype(jnp.float32)
    y = xf * lax.rsqrt(jnp.mean(xf * xf, axis=-1, keepdims=True) + EPS)
    return (y * g.astype(jnp.float32)).astype(x.dtype)


def ada_mod(cond, w, b):
    m = jax.nn.silu(cond) @ w + b
    return jnp.split(m[:, None, :], 6, axis=-1)


def modulate(h, shift, scale):
    return h * (1 + scale) + shift


def _lin_rec(e1, e2):
    a1, b1 = e1
    a2, b2 = e2
    return a1 * a2, a2 * b1 + b2


def s5_bidir(u, h0_re, h0_im, lam_re, lam_im, log_dt, b_re, b_im, c_re, c_im, d_skip, w_glu, b_glu):
    f32 = jnp.float32
    B, L, _ = u.shape
    uf = u.astype(f32).reshape(B, L, S5_GROUPS, S5_GROUP_CH)
    uc = uf.astype(jnp.complex64)
    Bm = lax.complex(b_re.astype(f32), b_im.astype(f32))
    Cm = lax.complex(c_re.astype(f32), c_im.astype(f32))
    y = uf * d_skip.astype(f32).reshape(S5_GROUPS, S5_GROUP_CH)
    fin_re, fin_im = [], []
    for d, rev in ((0, False), (1, True)):
        lam = lax.complex(lam_re[d].astype(f32), lam_im[d].astype(f32))
        dt = jnp.exp(log_dt[d].astype(f32))[:, None]
        lam_bar = jnp.exp(lam * dt)
        b_bar = ((lam_bar - 1.0) / lam)[..., None] * Bm
        bu = jnp.einsum('blgc,gpc->blgp', uc, b_bar)
        h0 = lax.complex(h0_re[:, d].astype(f32), h0_im[:, d].astype(f32))
        edge = L - 1 if rev else 0
        bu = bu.at[:, edge].add(lam_bar * h0)
        a = jnp.broadcast_to(lam_bar, bu.shape)
        _, hs = lax.associative_scan(_lin_rec, (a, bu), reverse=rev, axis=1)
        fin = hs[:, edge]
        fin_re.append(jnp.real(fin))
        fin_im.append(jnp.imag(fin))
        y = y + jnp.real(jnp.einsum('blgp,gcp->blgc', hs, Cm))
    g = jax.nn.gelu(y.reshape(B, L, S5_WIDTH))
    out = g * jax.nn.sigmoid(g @ w_glu.astype(f32) + b_glu.astype(f32))
    return out.astype(u.dtype), jnp.stack(fin_re, axis=1), jnp.stack(fin_im, axis=1)


def gla_chunked(q, k, v, g, s0):
    B, L, H, K = q.shape
    n = L // GLA_CHUNK

    def to_chunks(t):
        return jnp.moveaxis(t.reshape(B, n, GLA_CHUNK, H, t.shape[-1]), 1, 0)

    mask = jnp.tril(jnp.ones((GLA_CHUNK, GLA_CHUNK), dtype=bool))

    def step(S, inp):
        qc, kc, vc, gc = inp
        b = jnp.cumsum(gc, axis=1)
        b_last = b[:, -1]
        qd = qc * jnp.exp(b)
        kd = kc * jnp.exp(-b)
        att = jnp.where(mask, jnp.einsum('bthk,bshk->bhts', qd, kd), 0.0)
        o = jnp.einsum('bthk,bhkv->bthv', qd, S) + jnp.einsum('bhts,bshv->bthv', att, vc)
        S = S * jnp.exp(b_last)[..., None] + jnp.einsum('bshk,bshv->bhkv', kc * jnp.exp(b_last[:, None] - b), vc)
        return S, o

    S, o = lax.scan(step, s0, (to_chunks(q), to_chunks(k), to_chunks(v), to_chunks(g)))
    o = jnp.moveaxis(o, 0, 1).reshape(B, L, H, v.shape[-1])
    return o, S


def gla_bidir(q, k, v, r, glr, s0, w_gate2, b_gate, gla_norm):
    f32 = jnp.float32
    B, L, _ = q.shape
    qf = q.astype(f32).reshape(B, L, GLA_HEADS, GLA_DK) * (GLA_DK ** -0.5)
    kf = k.astype(f32).reshape(B, L, GLA_HEADS, GLA_DK)
    vf = v.astype(f32).reshape(B, L, GLA_HEADS, GLA_DV)
    glr = glr.astype(f32).reshape(B, L, 2, GLA_RANK)
    o_sum = jnp.zeros((B, L, GLA_HEADS, GLA_DV), f32)
    finals = []
    for d in (0, 1):
        gd = jax.nn.log_sigmoid(glr[:, :, d] @ w_gate2[d].astype(f32) + b_gate[d].astype(f32))
        gd = gd.reshape(B, L, GLA_HEADS, GLA_DK) / GLA_TAU
        if d == 0:
            od, sd = gla_chunked(qf, kf, vf, gd, s0[:, d].astype(f32))
        else:
            od, sd = gla_chunked(jnp.flip(qf, 1), jnp.flip(kf, 1), jnp.flip(vf, 1), jnp.flip(gd, 1), s0[:, d].astype(f32))
            od = jnp.flip(od, 1)
        o_sum = o_sum + od
        finals.append(sd)
    o = rms_norm(o_sum, gla_norm) * jax.nn.silu(r.astype(f32)).reshape(B, L, GLA_HEADS, GLA_DV)
    return o.reshape(B, L, GLA_VW).astype(q.dtype), jnp.stack(finals, axis=1)


def even_mixer(h, h0_re, h0_im, gla_s0, w_in, w_out, lam_re, lam_im, log_dt, b_re, b_im, c_re, c_im,
               d_skip, w_glu, b_glu, w_gate2, b_gate, gla_norm):
    z = h @ w_in
    cuts = np.cumsum([S5_WIDTH, GLA_QK, GLA_QK, GLA_VW, GLA_VW]).tolist()
    u, q, k, v, r, glr = jnp.split(z, cuts, axis=-1)
    s5_out, s5_re, s5_im = s5_bidir(u, h0_re, h0_im, lam_re, lam_im, log_dt, b_re, b_im, c_re, c_im,
                                    d_skip, w_glu, b_glu)
    gla_out, gla_fin = gla_bidir(q, k, v, r, glr, gla_s0, w_gate2, b_gate, gla_norm)
    out = jnp.concatenate([s5_out, gla_out.astype(s5_out.dtype)], axis=-1) @ w_out
    return out, s5_re, s5_im, gla_fin


def attn_project(h, w_qkv, q_norm, k_norm):
    B, L, _ = h.shape
    z = h @ w_qkv
    q, k, v = jnp.split(z, [N_HEADS * HEAD_DIM, (N_HEADS + KV_HEADS) * HEAD_DIM], axis=-1)
    q = rms_norm(q.reshape(B, L, N_HEADS, HEAD_DIM), q_norm)
    k = rms_norm(k.reshape(B, L, KV_HEADS, HEAD_DIM), k_norm)
    v = v.reshape(B, L, KV_HEADS, HEAD_DIM)
    return q, k, v


def grid_rope(L):
    rows = L // GRID_W
    row = jnp.repeat(jnp.arange(rows, dtype=jnp.float32), GRID_W)
    col = jnp.tile(jnp.arange(GRID_W, dtype=jnp.float32), rows)
    inv = ROPE_THETA ** (-jnp.arange(0, AXIS_DIM, 2, dtype=jnp.float32) / AXIS_DIM)
    ang = jnp.concatenate([row[:, None] * inv, col[:, None] * inv], axis=-1)
    return jnp.cos(ang), jnp.sin(ang)


def apply_rope(x, cos, sin):
    xf = x.astype(jnp.float32).reshape(*x.shape[:-1], HEAD_DIM // 2, 2)
    x1, x2 = xf[..., 0], xf[..., 1]
    c = cos[None, :, None, :]
    s = sin[None, :, None, :]
    out = jnp.stack([x1 * c - x2 * s, x1 * s + x2 * c], axis=-1)
    return out.reshape(x.shape).astype(x.dtype)


def attend(q, k, v):
    B, Lq, H, Dh = q.shape
    rep = H // KV_HEADS
    nb = Lq // Q_BLOCK
    qb = jnp.moveaxis(q.reshape(B, nb, Q_BLOCK, KV_HEADS, rep, Dh), 1, 0)
    scale = Dh ** -0.5

    def block(qc):
        s = jnp.einsum('bqgrd,bkgd->bgrqk', qc, k).astype(jnp.float32) * scale
        p = jax.nn.softmax(s, axis=-1).astype(v.dtype)
        return jnp.einsum('bgrqk,bkgd->bqgrd', p, v)

    o = lax.map(block, qb)
    return jnp.moveaxis(o, 0, 1).reshape(B, Lq, H * Dh)


def sq_relu_mlp(h, w1, w2):
    return jnp.square(jax.nn.relu(h @ w1)) @ w2


def setup_inputs(seed: int = 0) -> dict:
    key = jax.random.key(seed)
    ks = iter(jax.random.split(key, 48))
    f32 = jnp.float32

    def nrm(shape, scale=1.0):
        return jax.random.normal(next(ks), shape, f32) * scale

    lam_im_base = jnp.pi * jnp.arange(S5_STATE, dtype=f32)
    return {
        "x_prompt": nrm((BATCH, SEQ, D_MODEL)),
        "x_sample": nrm((DEC_BATCH, DEC_SEQ, D_MODEL)),
        "state_s5_re": nrm((DEC_BATCH, N_EVEN, 2, S5_GROUPS, S5_STATE), 0.1),
        "state_s5_im": nrm((DEC_BATCH, N_EVEN, 2, S5_GROUPS, S5_STATE), 0.1),
        "state_gla": nrm((DEC_BATCH, N_EVEN, 2, GLA_HEADS, GLA_DK, GLA_DV), 0.1),
        "cache_k": nrm((DEC_BATCH, N_ODD, PAST_LEN, KV_HEADS, HEAD_DIM)),
        "cache_v": nrm((DEC_BATCH, N_ODD, PAST_LEN, KV_HEADS, HEAD_DIM)),
        "c": nrm((DEC_BATCH, D_MODEL)),
        "c_ctx": nrm((D_MODEL,)),
        "norm_mix": 1.0 + nrm((DEPTH, D_MODEL), 0.01),
        "norm_mlp": 1.0 + nrm((DEPTH, D_MODEL), 0.01),
        "w_ada": nrm((DEPTH, D_MODEL, 6 * D_MODEL), 0.5 * D_MODEL ** -0.5),
        "b_ada": nrm((DEPTH, 6 * D_MODEL), 0.01),
        "w_mlp_in": nrm((DEPTH, D_MODEL, D_FF), D_MODEL ** -0.5),
        "w_mlp_out": nrm((DEPTH, D_FF, D_MODEL), D_FF ** -0.5),
        "w_in_e": nrm((N_EVEN, D_MODEL, IN_EVEN), D_MODEL ** -0.5),
        "w_out_e": nrm((N_EVEN, MIX_EVEN, D_MODEL), MIX_EVEN ** -0.5),
        "s5_lambda_re": -0.5 + nrm((N_EVEN, 2, S5_GROUPS, S5_STATE), 0.01),
        "s5_lambda_im": lam_im_base + nrm((N_EVEN, 2, S5_GROUPS, S5_STATE), 0.01),
        "s5_log_dt": jax.random.uniform(next(ks), (N_EVEN, 2, S5_GROUPS), f32,
                                        minval=math.log(1e-3), maxval=math.log(1e-1)),
        "s5_b_re": nrm((N_EVEN, S5_GROUPS, S5_STATE, S5_GROUP_CH), (2 * S5_GROUP_CH) ** -0.5),
        "s5_b_im": nrm((N_EVEN, S5_GROUPS, S5_STATE, S5_GROUP_CH), (2 * S5_GROUP_CH) ** -0.5),
        "s5_c_re": nrm((N_EVEN, S5_GROUPS, S5_GROUP_CH, S5_STATE), 0.5 ** 0.5),
        "s5_c_im": nrm((N_EVEN, S5_GROUPS, S5_GROUP_CH, S5_STATE), 0.5 ** 0.5),
        "s5_d": nrm((N_EVEN, S5_WIDTH)),
        "s5_w_glu": nrm((N_EVEN, S5_WIDTH, S5_WIDTH), S5_WIDTH ** -0.5),
        "s5_b_glu": nrm((N_EVEN, S5_WIDTH), 0.01),
        "gla_w_gate2": nrm((N_EVEN, 2, GLA_RANK, GLA_QK), GLA_RANK ** -0.5),
        "gla_b_gate": nrm((N_EVEN, 2, GLA_QK), 0.01),
        "gla_norm": 1.0 + nrm((N_EVEN, GLA_DV), 0.01),
        "w_qkv_o": nrm((N_ODD, D_MODEL, QKV_W), D_MODEL ** -0.5),
        "w_o_o": nrm((N_ODD, N_HEADS * HEAD_DIM, D_MODEL), (N_HEADS * HEAD_DIM) ** -0.5),
        "q_norm": 1.0 + nrm((N_ODD, HEAD_DIM), 0.01),
        "k_norm": 1.0 + nrm((N_ODD, HEAD_DIM), 0.01),
    }


def reference(x_prompt, x_sample, state_s5_re, state_s5_im, state_gla, cache_k, cache_v, c, c_ctx,
              norm_mix, norm_mlp, w_ada, b_ada, w_mlp_in, w_mlp_out, w_in_e, w_out_e,
              s5_lambda_re, s5_lambda_im, s5_log_dt, s5_b_re, s5_b_im, s5_c_re, s5_c_im, s5_d,
              s5_w_glu, s5_b_glu, gla_w_gate2, gla_b_gate, gla_norm, w_qkv_o, w_o_o, q_norm, k_norm):
    yp, ys = x_prompt, x_sample
    bp = x_prompt.shape[0]
    new_s5_re, new_s5_im, new_gla, new_k, new_v = [], [], [], [], []
    for i in range(DEPTH):
        j = i // 2
        sp1, cp1, gp1, sp2, cp2, gp2 = ada_mod(c_ctx[None, :], w_ada[i], b_ada[i])
        sl1, cl1, gl1, sl2, cl2, gl2 = ada_mod(c, w_ada[i], b_ada[i])
        hp = modulate(rms_norm(yp, norm_mix[i]), sp1, cp1)
        hs = modulate(rms_norm(ys, norm_mix[i]), sl1, cl1)
        if i % 2 == 0:
            ev = (w_in_e[j], w_out_e[j], s5_lambda_re[j], s5_lambda_im[j], s5_log_dt[j], s5_b_re[j],
                  s5_b_im[j], s5_c_re[j], s5_c_im[j], s5_d[j], s5_w_glu[j], s5_b_glu[j],
                  gla_w_gate2[j], gla_b_gate[j], gla_norm[j])
            z_s5 = jnp.zeros((bp, 2, S5_GROUPS, S5_STATE), jnp.float32)
            z_gla = jnp.zeros((bp, 2, GLA_HEADS, GLA_DK, GLA_DV), jnp.float32)
            mp, s5r, s5i, gfin = even_mixer(hp, z_s5, z_s5, z_gla, *ev)
            new_s5_re.append(s5r)
            new_s5_im.append(s5i)
            new_gla.append(gfin)
            ms, _, _, _ = even_mixer(hs, state_s5_re[:, j], state_s5_im[:, j], state_gla[:, j], *ev)
        else:
            qp, kp, vp = attn_project(hp, w_qkv_o[j], q_norm[j], k_norm[j])
            mp = attend(qp, kp, vp) @ w_o_o[j]
            new_k.append(kp)
            new_v.append(vp)
            qs, ks_, vs = attn_project(hs, w_qkv_o[j], q_norm[j], k_norm[j])
            cos, sin = grid_rope(hs.shape[1])
            qs = apply_rope(qs, cos, sin)
            ks_ = apply_rope(ks_, cos, sin)
            k_all = jnp.concatenate([ks_, cache_k[:, j].astype(ks_.dtype)], axis=1)
            v_all = jnp.concatenate([vs, cache_v[:, j].astype(vs.dtype)], axis=1)
            ms = attend(qs, k_all, v_all) @ w_o_o[j]
        yp = yp + gp1 * mp
        ys = ys + gl1 * ms
        hp = modulate(rms_norm(yp, norm_mlp[i]), sp2, cp2)
        hs = modulate(rms_norm(ys, norm_mlp[i]), sl2, cl2)
        yp = yp + gp2 * sq_relu_mlp(hp, w_mlp_in[i], w_mlp_out[i])
        ys = ys + gl2 * sq_relu_mlp(hs, w_mlp_in[i], w_mlp_out[i])
    s5_re_out = jnp.stack(new_s5_re, axis=1)
    s5_im_out = jnp.stack(new_s5_im, axis=1)
    gla_out = jnp.stack(new_gla, axis=1)
    k_out = jnp.stack(new_k, axis=1)
    v_out = jnp.stack(new_v, axis=1)
    return (yp, ys, s5_re_out, s5_im_out, gla_out, k_out, v_out)
```

```python
import os
import contextlib
import numpy as np
import concourse.bass as bass
import concourse.mybir as mybir
from concourse.bass_utils import run_bass_kernel_spmd

F32 = mybir.dt.float32
BF16 = mybir.dt.bfloat16
I32 = mybir.dt.int32
ALU = mybir.AluOpType
AF = mybir.ActivationFunctionType
AX = mybir.AxisListType

D = 1024
NT = 1024
DFF = 4096
EPS = 1e-6
DEPTH = 2
KDMA = 16
NSLOT = 3
STRICT_SAME_ENGINE = True

DBG_SKIP_EVEN = int(os.environ.get("SKIP_EVEN", "0"))
DBG_SKIP_ODD = int(os.environ.get("SKIP_ODD", "0"))


class Prog:
    ENGS = ("pe", "act", "dve", "pool", "sp")

    def __init__(self):
        self.ops = []
        self.streams = {e: [] for e in self.ENGS}
        self.last_writer = {}
        self.readers = {}
        self.ndma = {e: 0 for e in self.ENGS}

    def add(self, eng, fn, reads=(), writes=(), dma=False):
        oid = len(self.ops)
        deps = set()
        pr = [r for r in reads if isinstance(r, tuple) and r[0] in ("pf", "pb")]
        if pr:
            writes = list(writes) + [r for r in pr if r not in writes]
        for r in reads:
            lw = self.last_writer.get(r)
            if lw is not None:
                deps.add(lw)
        for w in writes:
            lw = self.last_writer.get(w)
            if lw is not None:
                deps.add(lw)
            for rd in self.readers.get(w, {}).values():
                deps.update(rd)
        deps.discard(oid)
        op = dict(id=oid, eng=eng, fn=fn, dma=dma, signal=False, pos=len(self.streams[eng]))
        if dma:
            op["dslot"] = self.ndma[eng] % KDMA
            op["dval"] = 16 * (self.ndma[eng] // KDMA + 1)
            self.ndma[eng] += 1
        keep = set()
        for d in deps:
            a = self.ops[d]
            if (not a["dma"]) and (not dma) and a["eng"] == eng:
                if eng == "pe":
                    continue
                if (not STRICT_SAME_ENGINE) and op["pos"] - a["pos"] > 2:
                    continue
            keep.add(d)
        op["deps"] = keep
        for d in keep:
            if not self.ops[d]["dma"]:
                self.ops[d]["signal"] = True
        self.ops.append(op)
        self.streams[eng].append(oid)
        for r in reads:
            rr = self.readers.setdefault(r, {})
            if dma:
                rr.setdefault("dma", []).append(oid)
            else:
                rr[eng] = [oid]
        for w in writes:
            self.last_writer[w] = oid
            self.readers[w] = {}
        return oid

    def emit(self, nc, final_wait_ops=()):
        for oid in final_wait_ops:
            if not self.ops[oid]["dma"]:
                self.ops[oid]["signal"] = True
        for e in self.ENGS:
            cnt = 0
            for oid in self.streams[e]:
                op = self.ops[oid]
                if (not op["dma"]) and op["signal"]:
                    cnt += 1
                    op["sval"] = cnt
        with contextlib.ExitStack() as st:
            esem = {e: st.enter_context(nc.semaphore("s_" + e)) for e in self.ENGS}
            dsem = {e: [st.enter_context(nc.semaphore("d_%s%d" % (e, i))) for i in range(KDMA)]
                    for e in ("sp", "pool", "act")}
            block = st.enter_context(nc.Block())
            ops = self.ops

            def run_stream(e, h):
                seen = {}
                for oid in self.streams[e]:
                    op = ops[oid]
                    need = {}
                    for d in op["deps"]:
                        a = ops[d]
                        if a["dma"]:
                            s = dsem[a["eng"]][a["dslot"]]
                            v = a["dval"]
                        else:
                            s = esem[a["eng"]]
                            v = a["sval"]
                        k = id(s)
                        if k not in need or need[k][1] < v:
                            need[k] = (s, v)
                    if op["dma"]:
                        s = dsem[e][op["dslot"]]
                        v = op["dval"] - 16
                        if v > 0:
                            k = id(s)
                            if k not in need or need[k][1] < v:
                                need[k] = (s, v)
                    for k, (s, v) in need.items():
                        if seen.get(k, 0) >= v:
                            continue
                        seen[k] = v
                        h.wait_ge(s, v)
                    ins = op["fn"](h)
                    if op["dma"]:
                        ins.then_inc(dsem[e][op["dslot"]], 16)
                    elif op["signal"]:
                        ins.then_inc(esem[e], 1)
                if e == "sp":
                    for oid in final_wait_ops:
                        a = ops[oid]
                        if a["dma"]:
                            h.wait_ge(dsem[a["eng"]][a["dslot"]], a["dval"])
                        else:
                            h.wait_ge(esem[a["eng"]], a["sval"])

            @block.sync
            def _(h):
                run_stream("sp", h)

            @block.gpsimd
            def _(h):
                run_stream("pool", h)

            @block.scalar
            def _(h):
                run_stream("act", h)

            @block.vector
            def _(h):
                run_stream("dve", h)

            @block.tensor
            def _(h):
                run_stream("pe", h)


def keys(name, *dims):
    out = [(name,)]
    for d in dims:
        if isinstance(d, int):
            d = [d]
        out = [k + (i,) for k in out for i in d]
    return out


class Builder:
    def __init__(self, nc, st):
        self.nc = nc
        self.st = st
        self.P = Prog()
        self.fin = []

    def sb(self, name, shape, dt=F32):
        return self.st.enter_context(self.nc.sbuf_tensor(name, shape, dt))

    def psum(self, name, shape, dt=F32):
        return self.st.enter_context(self.nc.psum_tensor(name, shape, dt))

    def dram_in(self, name, shape, dt=F32):
        return self.nc.dram_tensor(name, list(shape), dt, kind="ExternalInput").ap()

    def dram_out(self, name, shape, dt=F32):
        return self.nc.dram_tensor(name, list(shape), dt, kind="ExternalOutput").ap()

    def mm(self, out, lhsT, rhs, start, stop, reads, writes):
        self.P.add("pe", lambda h: h.matmul(out, lhsT=lhsT, rhs=rhs, start=start, stop=stop), reads, writes)

    def tr(self, out, in_, ident, reads, writes):
        self.P.add("pe", lambda h: h.transpose(out=out, in_=in_, identity=ident), reads, writes)

    def act(self, out, in_, func, reads, writes, scale=None, bias=None):
        kw = {}
        if scale is not None:
            kw["scale"] = scale
        if bias is not None:
            kw["bias"] = bias
        self.P.add("act", lambda h: h.activation(out=out, in_=in_, func=func, **kw), reads, writes)

    def tt(self, out, in0, in1, op, reads, writes, eng="dve"):
        self.P.add(eng, lambda h: h.tensor_tensor(out=out, in0=in0, in1=in1, op=op), reads, writes)

    def ts(self, out, in0, s1, op0, reads, writes, s2=None, op1=None, eng="dve"):
        if op1 is None:
            self.P.add(eng, lambda h: h.tensor_single_scalar(out=out, in_=in0, scalar=s1, op=op0), reads, writes)
        else:
            self.P.add(eng, lambda h: h.tensor_scalar(out=out, in0=in0, scalar1=s1, scalar2=s2, op0=op0, op1=op1),
                       reads, writes)

    def stt(self, out, in0, scalar, in1, op0, op1, reads, writes, eng="dve"):
        self.P.add(eng, lambda h: h.scalar_tensor_tensor(out=out, in0=in0, scalar=scalar, in1=in1, op0=op0, op1=op1),
                   reads, writes)

    def copy(self, out, in_, reads, writes, eng="dve"):
        if eng == "act":
            self.P.add("act", lambda h: h.activation(out=out, in_=in_, func=AF.Copy), reads, writes)
        else:
            self.P.add(eng, lambda h: h.tensor_copy(out=out, in_=in_), reads, writes)

    def memset(self, out, val, writes, eng="dve"):
        self.P.add(eng, lambda h: h.memset(out, val), (), writes)

    def recip(self, out, in_, reads, writes):
        self.P.add("dve", lambda h: h.reciprocal(out=out, in_=in_), reads, writes)

    def dma(self, q, out, in_, reads, writes, final=False):
        oid = self.P.add(q, lambda h: h.dma_start(out=out, in_=in_), reads, writes, dma=True)
        if final:
            self.fin.append(oid)
        return oid


class WStream:
    def __init__(self, B, slots):
        self.B = B
        self.slots = slots
        self.loads = []
        self.recorded = 0
        self.slot_of = {}
        self.pinned = set()
        self.rr = 0
        self.occ = {}

    def plan(self, view_fn, src):
        self.loads.append((view_fn, src))
        return len(self.loads) - 1

    def pin(self, i):
        self.pinned.add(self.slot_of[i])

    def unpin(self, i):
        self.pinned.discard(self.slot_of[i])

    def acquire(self, i):
        while self.recorded < len(self.loads):
            k = self.recorded
            rr = self.rr
            while rr % NSLOT in self.pinned:
                rr += 1
            sidx = rr % NSLOT
            prev = self.occ.get(sidx, -1)
            if k > i and prev >= i:
                break
            assert prev < i or prev < 0, (k, i, prev)
            self.rr = rr + 1
            view_fn, src = self.loads[k]
            self.B.dma("pool", view_fn(self.slots[sidx]), src, (), [("w", sidx)])
            self.slot_of[k] = sidx
            self.occ[sidx] = k
            self.recorded += 1
        sidx = self.slot_of[i]
        return self.slots[sidx], ("w", sidx)


def build_program():
    nc = bass.Bass("TRN2", target_bir_lowering=False)
    with contextlib.ExitStack() as st:
        B = Builder(nc, st)
        P = B.P
        x_d = B.dram_in("x", [NT, D])
        cond_d = B.dram_in("condT", [128, 8])
        ident_d = B.dram_in("ident", [128, 128])
        w_ada_d = B.dram_in("w_ada", [DEPTH, D, 6 * D])
        b_ada_d = B.dram_in("b_adaT", [128, DEPTH, 48])
        gmix_d = B.dram_in("gmixT", [128, DEPTH, 8])
        gmlp_d = B.dram_in("gmlpT", [128, DEPTH, 8])
        w1_d = B.dram_in("w_mlp_in", [DEPTH, D, DFF])
        w2_d = B.dram_in("w_mlp_out", [DEPTH, DFF, D])
        y_d = B.dram_out("y", [NT, D])
        win_d = B.dram_in("w_in", [D, 2080])
        wout_d = B.dram_in("w_out", [D, D])
        wg2_d = B.dram_in("wg2", [32, 2, 256])
        bgT_d = B.dram_in("bgT", [128, 4])
        gnorm_d = B.dram_in("gnorm", [128, 1])
        tri2_d = B.dram_in("tri2", [128, 256])
        cm_d = B.dram_in("cm", [128, 1])
        glainit_d = B.dram_in("gla_init", [2, 256, 128])
        glaout_d = B.dram_out("gla_out", [4, 2, 256, 128])
        s5in_d = B.dram_in("s5in", [128, 1392])
        initP_d = B.dram_in("initP", [128, 2, 2, 16])
        jv_d = B.dram_in("jv", [128, 128])
        dS_d = B.dram_in("dS", [128, 32])
        E8_d = B.dram_in("E8", [128, 8 * 240])
        wglu_d = B.dram_in("w_glu", [512, 512])
        bglu_d = B.dram_in("bgluT", [128, 4])
        s5re_d = B.dram_out("s5re_out", [4, 2, 32, 64])
        s5im_d = B.dram_out("s5im_out", [4, 2, 32, 64])
        wqkv_d = B.dram_in("w_qkv", [D, 1536])
        wo_d = B.dram_in("w_o", [D, D])
        gqk_d = B.dram_in("gqk", [128, 1280])
        ck_d = B.dram_in("cache_k", [512, 256])
        cv_d = B.dram_in("cache_v", [512, 256])
        maskb_d = B.dram_in("maskb", [128, 48])
        cos_d = B.dram_in("ropecos", [128, 8, 64])
        sin_d = B.dram_in("ropesin", [128, 8, 64])
        kout_d = B.dram_out("k_out", [NT, 256])
        vout_d = B.dram_out("v_out", [NT, 256])

        yT = B.sb("yT", [128, 8, NT], F32)
        hT = B.sb("hT", [128, 8, NT], BF16)
        big = B.sb("big", [128, 32 * NT], BF16)
        slots = [B.sb("wslot%d" % i, [128, 8192], BF16) for i in range(NSLOT)]
        xin = [B.sb("xin%d" % i, [128, D], F32) for i in range(2)]
        sq = B.sb("sq", [128, 8, 512], BF16)
        sd = B.sb("sd", [128, 512], F32)
        rstd = B.sb("rstd", [128, 512], F32)
        tmpn = [B.sb("tmpn%d" % i, [128, 512], F32) for i in range(2)]
        rl = [B.sb("rl%d" % i, [128, 512], BF16) for i in range(2)]
        ident = B.sb("ident_sb", [128, 128], F32)
        identb = B.sb("identb", [128, 128], BF16)
        onesd = B.sb("onesd", [128, 128], BF16)
        epsc = B.sb("epsc", [128, 1], F32)
        condT = B.sb("condT_sb", [128, 8], F32)
        silc = B.sb("silc", [128, 8], BF16)
        b_adaT = B.sb("b_adaT_sb", [128, DEPTH, 48], F32)
        gmixT = B.sb("gmixT_sb", [128, DEPTH, 8], F32)
        gmlpT = B.sb("gmlpT_sb", [128, DEPTH, 8], F32)
        adaT = B.sb("adaT", [128, DEPTH, 48], F32)
        A1 = B.sb("A1", [128, DEPTH, 8], F32)
        A2 = B.sb("A2", [128, DEPTH, 8], F32)

        onesb = B.sb("onesb", [128, 128], BF16)
        onesdv = B.sb("onesdv", [128, 128], BF16)
        onec = B.sb("onec", [128, 1], F32)
        halfpi = B.sb("halfpi", [128, 1], F32)
        wg2 = B.sb("wg2b", [32, 512], BF16)
        nbg = B.sb("nbg", [128, 4], F32)
        gnorm = B.sb("gnorm_sb", [128, 1], F32)
        tri2 = B.sb("tri2_sb", [128, 256], F32)
        cm = B.sb("cm_sb", [128, 1], F32)
        m01 = B.sb("m01", [128, NT], F32)
        cosT = m01[:, 0:512].rearrange("p (i f) -> p i f", i=8)
        sinT = m01[:, 512:1024].rearrange("p (i f) -> p i f", i=8)
        rho8 = B.sb("rho8", [128, 2, 16], F32)
        th8 = B.sb("th8", [128, 2, 16], F32)
        L8r = B.sb("L8r", [128, 2, 16], F32)
        L8i = B.sb("L8i", [128, 2, 16], F32)
        LIr = B.sb("LIr", [128, 2, 16], F32)
        LIi = B.sb("LIi", [128, 2, 16], F32)
        LIt = B.sb("LIt", [128, 16], F32)
        th8n = B.sb("th8n", [128, 2, 16], F32)
        itile = B.sb("itile", [128, 512], I32)
        initP = B.sb("initP_sb", [128, 2, 2, 16], F32)
        smf = B.sb("smf", [128, 128], F32)
        smb = B.sb("smb", [128, 128], F32)
        jv = B.sb("jv_sb", [128, 128], F32)
        Hfin = B.sb("Hfin", [128, 2, 128], F32)
        P0r = B.sb("P0r", [128, 16], F32)
        P0i = B.sb("P0i", [128, 16], F32)
        bglu = B.sb("bglu_sb", [128, 4], F32)
        dS = B.sb("dS_sb", [128, 32], F32)
        E8 = B.sb("E8_sb", [128, 8 * 240], BF16)
        ebl = B.sb("ebl", [128, 4, 8], F32)
        Sst = [B.sb("Sst%d" % i, [128, 128], F32) for i in range(2)]
        Stmp = B.sb("Stmp", [128, 128], F32)
        maskb = B.sb("maskb_sb", [128, 48], F32)

        class Arena:
            def __init__(self):
                self.off = 0

            def reset(self):
                self.off = 0

            def alloc(self, nelem, dt):
                nb = nelem * (4 if dt == F32 else 2)
                nb = (nb + 63) // 64 * 64
                o = self.off
                self.off += nb
                assert self.off <= 65536, self.off
                v = big[:, o // 2:(o + nb) // 2]
                if dt == F32:
                    v = v.bitcast(F32)
                return v[:, 0:nelem]

        arena = Arena()
        BIGK = keys("big", range(32), range(2))

        ps2 = [B.psum("ps2_%d" % i, [128, 1024], F32) for i in range(2)]
        pf = [ps2[0][:, 0:512], ps2[0][:, 512:1024], ps2[1][:, 0:512], ps2[1][:, 512:1024]] + \
             [B.psum("pf%d" % i, [128, 512], F32)[:, :] for i in (4, 5)]
        pb = [B.psum("pb%d" % i, [128, 1024], BF16) for i in range(2)]
        pbf = [pb[i][:, :].bitcast(F32) for i in range(2)]

        ws = WStream(B, slots)

        def v_kc(ncols):
            return lambda slot: slot[:, 0:8 * ncols].rearrange("p (kc n) -> p kc n", kc=8)

        LD = {}

        def plan_ada(l):
            for j in range(6):
                LD["ada", l, j] = ws.plan(v_kc(1024),
                                          w_ada_d[l, :, j * 1024:(j + 1) * 1024].rearrange("(kc p) n -> p kc n", p=128))

        def plan_mlp(l):
            for b in range(4):
                LD["w1", l, b] = ws.plan(v_kc(1024),
                                         w1_d[l, :, b * 1024:(b + 1) * 1024].rearrange("(kc p) n -> p kc n", p=128))
            for b in range(4):
                LD["w2", l, b] = ws.plan(lambda slot: slot[:, 0:8192].rearrange("p (fc n) -> p fc n", fc=32),
                                         w2_d[l, :, b * 256:(b + 1) * 256].rearrange("(fc p) n -> p fc n", p=128))

        if not DBG_SKIP_EVEN:
            LD["s5in"] = ws.plan(lambda slot: slot[:, 0:2784].bitcast(F32), s5in_d[:, :])
        plan_ada(0)
        if not DBG_SKIP_EVEN:
            LD["winU"] = ws.plan(v_kc(512), win_d[:, 0:512].rearrange("(kc p) n -> p kc n", p=128))
            plan_ada(1)
            LD["wglu"] = ws.plan(lambda slot: slot[:, 0:2048].rearrange("p (kc n) -> p kc n", kc=4),
                                 wglu_d[:, :].rearrange("(kc p) n -> p kc n", p=128))
            LD["winQK"] = ws.plan(v_kc(512), win_d[:, 512:1024].rearrange("(kc p) n -> p kc n", p=128))
            LD["winVR"] = ws.plan(v_kc(1024), win_d[:, 1024:2048].rearrange("(kc p) n -> p kc n", p=128))
            LD["winG"] = ws.plan(v_kc(32), win_d[:, 2048:2080].rearrange("(kc p) n -> p kc n", p=128))
            LD["wout"] = ws.plan(v_kc(1024), wout_d[:, :].rearrange("(kc p) n -> p kc n", p=128))
        plan_mlp(0)
        if DBG_SKIP_EVEN:
            plan_ada(1)
        LD["qkvA"] = ws.plan(v_kc(1024), wqkv_d[:, 0:1024].rearrange("(kc p) n -> p kc n", p=128))
        LD["qkvB"] = ws.plan(v_kc(512), wqkv_d[:, 1024:1536].rearrange("(kc p) n -> p kc n", p=128))
        LD["wo"] = ws.plan(v_kc(1024), wo_d[:, :].rearrange("(kc p) n -> p kc n", p=128))
        plan_mlp(1)

        B.dma("sp", maskb[:], maskb_d[:, :], (), ["maskb"])
        B.memset(onesb[:], 1.0, ["onesb"])
        B.memset(onesdv[:], 1.0 / 128.0, ["onesdv"])
        B.memset(onec[:], 1.0, ["onec"])
        B.memset(halfpi[:], float(np.pi / 2) * 0.99999, ["halfpi"])
        B.dma("pool", wg2[:], wg2_d[:, :, :].rearrange("r d c -> r (d c)"), (), ["wg2"])
        B.dma("sp", nbg[:], bgT_d[:, :], (), ["nbg"])
        B.ts(nbg[:], nbg[:], -1.0, ALU.mult, ["nbg"], ["nbg"])
        B.dma("sp", gnorm[:], gnorm_d[:, :], (), ["gnorm"])
        B.dma("sp", tri2[:], tri2_d[:, :], (), ["tri2"])
        B.dma("sp", cm[:], cm_d[:, :], (), ["cm"])
        B.dma("sp", initP[:], initP_d[:, :, :, :], (), ["initP"])
        B.dma("sp", jv[:], jv_d[:, :], (), ["jv"])
        B.dma("sp", bglu[:], bglu_d[:, :], (), ["bglu"])
        B.dma("sp", dS[:], dS_d[:, :], (), ["dS"])
        B.dma("pool", E8[:], E8_d[:, :], (), ["E8"])
        B.memset(smf[:], 1.0, ["smf"])
        B.memset(smb[:], 1.0, ["smb"])
        B.copy(smf[:, 32:128:32], cm[:, 0:1].to_broadcast([128, 3]), ["cm", "smf"], ["smf"])
        B.copy(smb[:, 31:127:32], cm[:, 0:1].to_broadcast([128, 3]), ["cm", "smb"], ["smb"])
        B.memset(smf[:, 0:1], 0.0, ["smf"])
        B.memset(smb[:, 127:128], 0.0, ["smb"])
        B.dma("sp", ident[:], ident_d[:, :], (), ["ident"])
        B.dma("sp", condT[:], cond_d[:, :], (), ["condT"])
        B.dma("sp", b_adaT[:], b_ada_d[:, :, :], (), ["b_adaT"])
        B.dma("sp", gmixT[:], gmix_d[:, :, :], (), ["gmixT"])
        B.dma("sp", gmlpT[:], gmlp_d[:, :, :], (), ["gmlpT"])
        B.copy(identb[:], ident[:], ["ident"], ["identb"])
        B.memset(onesd[:], 1.0 / 1024.0, ["onesd"])
        B.memset(epsc[:], EPS, ["epsc"])
        B.act(silc[:], condT[:], AF.Silu, ["condT"], ["silc"])

        def yk(cs, tiles):
            return keys("yT", cs, tiles)

        def input_transposes():
            for i in range(8):
                xb_ = xin[i % 2]
                xk = ("xin", i % 2)
                B.dma("sp", xb_[:], x_d[i * 128:(i + 1) * 128, :], (), [xk])
                for half in range(2):
                    ps = pf[half]
                    pk = ("pf", half)
                    for cc in range(4):
                        c = half * 4 + cc
                        B.tr(ps[:, cc * 128:(cc + 1) * 128], xb_[:, c * 128:(c + 1) * 128], ident[:],
                             [xk, "ident"], [pk])
                    B.copy(yT[:, half * 4:half * 4 + 4, i * 128:(i + 1) * 128],
                           ps[:, :].rearrange("p (c t) -> p c t", c=4), [pk], yk(range(half * 4, half * 4 + 4), i),
                           eng="act")


        def ada_step(l, j):
            aps = pf[5]
            slot, sk = ws.acquire(LD["ada", l, j])
            for c in range(8):
                col = j * 8 + c
                for kc in range(8):
                    B.mm(aps[:, col:col + 1], slot[:, kc * 1024 + c * 128: kc * 1024 + (c + 1) * 128],
                         silc[:, kc:kc + 1], kc == 0, kc == 7, [sk, "silc"], [("pf", 5)])

        def ada_finish(l):
            B.tt(adaT[:, l, :], pf[5][:, 0:48], b_adaT[:, l, :], ALU.add, [("pf", 5), "b_adaT"], [("ada", l)])
            B.stt(A1[:, l, :], adaT[:, l, 8:16], 1.0, gmixT[:, l, :], ALU.add, ALU.mult,
                  [("ada", l), "gmixT"], [("A1", l)])
            B.stt(A2[:, l, :], adaT[:, l, 32:40], 1.0, gmlpT[:, l, :], ALU.add, ALU.mult,
                  [("ada", l), "gmlpT"], [("A2", l)])

        def ada(l):
            for j in range(6):
                ada_step(l, j)
            ada_finish(l)

        def norm_mod(l, which):
            Amat = A1 if which == 0 else A2
            sh0 = 0 if which == 0 else 24
            ak = ("A1", l) if which == 0 else ("A2", l)
            for n in range(2):
                tl = range(4 * n, 4 * n + 4)
                for c in range(8):
                    B.act(sq[:, c, :], yT[:, c, n * 512:(n + 1) * 512], AF.Square, yk(c, tl), [("sq", c)])
                for c in range(8):
                    B.mm(pf[2][:, :], onesd[:], sq[:, c, :], c == 0, c == 7, ["onesd", ("sq", c)], [("pf", 2)])
                B.act(sd[:], pf[2][:, :], AF.Ln, [("pf", 2), "epsc"], ["sd"], bias=epsc[:, 0:1])
                B.act(rstd[:], sd[:], AF.Exp, ["sd"], ["rstd"], scale=-0.5)
                for c in range(8):
                    t_ = tmpn[c % 2]
                    tk = ("tmpn", c % 2)
                    B.tt(t_[:], yT[:, c, n * 512:(n + 1) * 512], rstd[:], ALU.mult, yk(c, tl) + ["rstd"], [tk])
                    B.act(hT[:, c, n * 512:(n + 1) * 512], t_[:], AF.Identity, [tk, ak, ("ada", l)],
                          [("hT", c, n)], scale=Amat[:, l, c:c + 1], bias=adaT[:, l, sh0 + c:sh0 + c + 1])

        def mlp(l):
            h1 = big[:, :].rearrange("p (f t) -> p f t", f=32)
            cnt = 0
            for b in range(4):
                slot, sk = ws.acquire(LD["w1", l, b])
                for fl in range(8):
                    f = b * 8 + fl
                    for n in range(2):
                        pi = cnt % 2
                        cnt += 1
                        ps = pf[pi]
                        for kc in range(8):
                            B.mm(ps[:, :], slot[:, kc * 1024 + fl * 128: kc * 1024 + (fl + 1) * 128],
                                 hT[:, kc, n * 512:(n + 1) * 512], kc == 0, kc == 7,
                                 [sk, ("hT", kc, n)], [("pf", pi)])
                        B.act(rl[pi][:], ps[:, :], AF.Relu, [("pf", pi)], [("rl", pi)])
                        B.tt(h1[:, f, n * 512:(n + 1) * 512], rl[pi][:], rl[pi][:], ALU.mult,
                             [("rl", pi)], [("big", f, n)])
            for b in range(4):
                slot, sk = ws.acquire(LD["w2", l, b])
                for cl in range(2):
                    c = 2 * b + cl
                    for n in range(2):
                        pi = cnt % 2
                        cnt += 1
                        ps = pf[pi]
                        for fc in range(32):
                            B.mm(ps[:, :], slot[:, fc * 256 + cl * 128: fc * 256 + (cl + 1) * 128],
                                 h1[:, fc, n * 512:(n + 1) * 512], fc == 0, fc == 31,
                                 [sk, ("big", fc, n)], [("pf", pi)])
                        tl = range(4 * n, 4 * n + 4)
                        B.stt(yT[:, c, n * 512:(n + 1) * 512], ps[:, :], adaT[:, l, 40 + c:41 + c],
                              yT[:, c, n * 512:(n + 1) * 512], ALU.mult, ALU.add,
                              [("pf", pi), ("ada", l)] + yk(c, tl), yk(c, tl))


        def arena_barrier(old, new):
            B.memset(epsc[:], EPS, ["epsc"] + list(old) + list(new))

        def attention(l):
            arena.reset()
            qT = arena.alloc(8 * NT, BF16).rearrange("p (h t) -> p h t", h=8)
            kT = arena.alloc(2 * 1536, BF16).rearrange("p (h t) -> p h t", h=2)
            vtok = arena.alloc(12 * 256, BF16).rearrange("p (i f) -> p i f", i=12)
            qkv_tok = arena.alloc(1536, F32)
            qkn = arena.alloc(1280, F32).rearrange("p (h d) -> p h d", h=10)
            rt01 = arena.alloc(1280, F32)
            sqh = rt01.rearrange("p (h d) -> p h d", h=10)
            rt = [rt01[:, 0:640].rearrange("p (h d) -> p h d", h=10), rt01[:, 640:1280].rearrange("p (h d) -> p h d", h=10)] + \
                 [arena.alloc(640, F32).rearrange("p (h d) -> p h d", h=10) for _ in range(2)]
            gqk = arena.alloc(1280, F32).rearrange("p (h d) -> p h d", h=10)
            qr = arena.alloc(1280, BF16).rearrange("p (h d) -> p h d", h=10)
            PTall = arena.alloc(3072, BF16)
            PTa = PTall[:, 0:1024].rearrange("p (k t) -> p k t", k=2)
            PTb = PTall[:, 1024:2048].rearrange("p (k t) -> p k t", k=2)
            PTc = PTall[:, 2048:3072].rearrange("p (k t) -> p k t", k=2)
            sqflat = sq[:, :, :].rearrange("p c t -> p (c t)")
            qkv_toks = [qkv_tok, PTall.bitcast(F32)]
            qkv_keys = [["qkv_tok"], ["PT0", "PT1", "PT2"]]
            qkns = [qkn, sqflat[:, 0:2560].bitcast(F32).rearrange("p (h d) -> p h d", h=10)]
            qkn_keys = [["qkn"], keys("sq", range(5))]
            qrs = [qr, sqflat[:, 2560:3840].rearrange("p (h d) -> p h d", h=10)]
            qr_keys = [["qr"], keys("sq", range(5, 8))]
            rden = rstd
            ssq = arena.alloc(16, F32)
            rsq = arena.alloc(16, F32)
            ck_tok = sq[:, :, :].rearrange("p c t -> p (c t)").bitcast(F32)[:, 0:1024].rearrange("p (i f) -> p i f", i=4)
            CKK = keys("sq", range(8))
            AK = ["qT", "kTl", "kTc", "vtokc", "qkv_tok", "gqk", "qkn", "rt0", "rt1", "rt2", "rt3", "qr",
                  "PT0", "PT1", "PT2", "rstd", "ssq", "rsq"] + keys("vtok", range(8)) + keys("qTi", range(8)) + keys("kTi", range(8))
            arena_barrier(BIGK, AK)
            attnT = hT
            B.dma("sp", gqk[:, :, :], gqk_d[:, :].rearrange("p (h d) -> p h d", h=10), (), ["gqk"])
            B.dma("sp", cosT, cos_d[:, :, :], (), ["m01"])
            B.dma("sp", sinT, sin_d[:, :, :], (), ["m01"])

            B.dma("sp", ck_tok[:, :, :], ck_d[:, :].rearrange("(i p) f -> p i f", p=128), (), CKK)
            B.dma("pool", vtok[:, 8:12, :], cv_d[:, :].rearrange("(i p) f -> p i f", p=128), (), ["vtokc"])
            for kvh in range(2):
                ps = pf[kvh]
                for i in range(4):
                    B.tr(ps[:, i * 128:(i + 1) * 128], ck_tok[:, i, kvh * 128:(kvh + 1) * 128], ident[:],
                         CKK + ["ident"], [("pf", kvh)])
                B.copy(kT[:, kvh, 1024:1536], ps[:, :], [("pf", kvh)], ["kTc"], eng="act")

            slotA, skA = ws.acquire(LD["qkvA"])
            ws.pin(LD["qkvA"])
            slotB, skB = ws.acquire(LD["qkvB"])
            def pa_proj(i):
                    tsl = slice(i * 128, (i + 1) * 128)
                    n = i // 4
                    bsel = i % 2
                    qkv_t, qkv_k = qkv_toks[bsel], qkv_keys[bsel]
                    qkn_t, qkn_k = qkns[bsel], qkn_keys[bsel]
                    qr_t, qr_k = qrs[bsel], qr_keys[bsel]
                    for cb in range(3):
                        ps = pf[2 + cb]
                        pk = ("pf", 2 + cb)
                        for kc in range(8):
                            if cb < 2:
                                rhs = slotA[:, kc * 1024 + cb * 512: kc * 1024 + (cb + 1) * 512]
                                sk = skA
                            else:
                                rhs = slotB[:, kc * 512:(kc + 1) * 512]
                                sk = skB
                            B.mm(ps[:, :], hT[:, kc, tsl], rhs, kc == 0, kc == 7, [("hT", kc, n), sk], [pk])

            def pa_evac(i):
                    tsl = slice(i * 128, (i + 1) * 128)
                    bsel = i % 2
                    qkv_t, qkv_k = qkv_toks[bsel], qkv_keys[bsel]
                    for cb in range(3):
                        B.copy(qkv_t[:, cb * 512:(cb + 1) * 512], pf[2 + cb][:, :], [("pf", 2 + cb)], qkv_k, eng="act")
                    B.dma("sp", vout_d[tsl, :], qkv_t[:, 1280:1536], qkv_k, (), final=True)
                    B.copy(vtok[:, i, :], qkv_t[:, 1280:1536], qkv_k, [("vtok", i)], eng="act")

            def pa_rest(i):
                    tsl = slice(i * 128, (i + 1) * 128)
                    n = i // 4
                    bsel = i % 2
                    qkv_t, qkv_k = qkv_toks[bsel], qkv_keys[bsel]
                    qkn_t, qkn_k = qkns[bsel], qkn_keys[bsel]
                    qr_t, qr_k = qrs[bsel], qr_keys[bsel]
                    xq = qkv_t[:, 0:1280].rearrange("p (h d) -> p h d", h=10)
                    B.tt(sqh[:, :, :], xq, xq, ALU.mult, qkv_k, ["rt0", "rt1"])
                    B.P.add("dve", lambda h, o=ssq[:, 0:10], a=sqh[:, :, :]: h.tensor_reduce(out=o, in_=a, axis=AX.X, op=ALU.add),
                            ["rt0", "rt1"], ["ssq"])
                    B.act(rsq[:, 0:10], ssq[:, 0:10], AF.Sqrt, ["ssq", "epsc"], ["rsq"], scale=1.0 / 128.0, bias=epsc[:, 0:1])

            def pa_tail(i):
                    tsl = slice(i * 128, (i + 1) * 128)
                    n = i // 4
                    bsel = i % 2
                    qkv_t, qkv_k = qkv_toks[bsel], qkv_keys[bsel]
                    qkn_t, qkn_k = qkns[bsel], qkn_keys[bsel]
                    qr_t, qr_k = qrs[bsel], qr_keys[bsel]
                    xq = qkv_t[:, 0:1280].rearrange("p (h d) -> p h d", h=10)
                    B.recip(ssq[:, 0:10], rsq[:, 0:10], ["rsq"], ["ssq"])
                    B.tt(qkn_t[:, :, :], xq, ssq[:, 0:10].unsqueeze(2).to_broadcast([128, 10, 128]), ALU.mult,
                         qkv_k + ["ssq"], qkn_k)
                    B.tt(qkn_t[:, :, :], qkn_t[:, :, :], gqk[:, :, :], ALU.mult, qkn_k + ["gqk"], qkn_k)
                    B.dma("sp", kout_d[tsl, :], qkn_t[:, 8:10, :], qkn_k, (), final=True)
                    x1 = qkn_t[:, :, 0::2]
                    x2 = qkn_t[:, :, 1::2]
                    cb_ = cosT[:, i, :].unsqueeze(1).to_broadcast([128, 10, 64])
                    sb_ = sinT[:, i, :].unsqueeze(1).to_broadcast([128, 10, 64])
                    B.tt(rt[0][:, :, :], x1, cb_, ALU.mult, qkn_k + ["m01"], ["rt0"])
                    B.tt(rt[1][:, :, :], x2, sb_, ALU.mult, qkn_k + ["m01"], ["rt1"])
                    B.tt(qr_t[:, :, 0::2], rt[0][:, :, :], rt[1][:, :, :], ALU.subtract, ["rt0", "rt1"], qr_k)
                    B.tt(rt[2][:, :, :], x1, sb_, ALU.mult, qkn_k + ["m01"], ["rt2"])
                    B.tt(rt[3][:, :, :], x2, cb_, ALU.mult, qkn_k + ["m01"], ["rt3"])
                    B.tt(qr_t[:, :, 1::2], rt[2][:, :, :], rt[3][:, :, :], ALU.add, ["rt2", "rt3"], qr_k)
                    for hh in range(8):
                        B.tr(pb[0][:, hh * 128:(hh + 1) * 128], qr_t[:, hh, :], identb[:], qr_k + ["identb"], [("pb", 0)])
                    for hh in range(2):
                        B.tr(pb[1][:, hh * 128:(hh + 1) * 128], qr_t[:, 8 + hh, :], identb[:], qr_k + ["identb"], [("pb", 1)])
                    B.copy(qT[:, :, tsl], pb[0][:, :].rearrange("p (h t) -> p h t", h=8), [("pb", 0)], [("qTi", i)], eng="act")
                    B.copy(kT[:, :, tsl], pb[1][:, 0:256].rearrange("p (h t) -> p h t", h=2), [("pb", 1)], [("kTi", i)], eng="act")

            pa_proj(0)
            pa_evac(0)
            for i in range(8):
                if i + 1 < 8:
                    pa_proj(i + 1)
                pa_rest(i)
                if i + 1 < 8:
                    pa_evac(i + 1)
                pa_tail(i)

            ws.unpin(LD["qkvA"])
            sc = 1.0 / np.sqrt(128.0)
            QK_ALL = keys("qTi", range(8)) + keys("kTi", range(8)) + ["kTc"]
            PT2 = [PTa, PTb, PTc]
            units = [(h_, n, pr) for h_ in range(8) for n in range(2) for pr in range(6)]

            def score(g):
                h_, n, pr = units[g]
                kvh = h_ // 4
                nsl = slice(n * 512, (n + 1) * 512)
                bi = g % 2
                for k in range(2):
                    kc = 2 * pr + k
                    B.mm(pf[2 * bi + k][:, :], kT[:, kvh, kc * 128:(kc + 1) * 128], qT[:, h_, nsl], True, True,
                         QK_ALL, [("pf", 2 * bi + k)])
                ti = g % 3
                for qb in range(2):
                    col = (2 * pr) * 4 + n * 2 + qb
                    B.act(PT2[ti][:, :, qb * 256:(qb + 1) * 256],
                          ps2[bi][:, :].rearrange("p (k t) -> p k t", k=2)[:, :, qb * 256:(qb + 1) * 256], AF.Exp,
                          [("pf", 2 * bi), ("pf", 2 * bi + 1), "maskb"], ["PT%d" % ti], scale=sc,
                          bias=maskb[:, col:col + 1])

            def pv(g):
                h_, n, pr = units[g]
                kvh = h_ // 4
                nsl = slice(n * 512, (n + 1) * 512)
                ti = g % 3
                for k in range(2):
                    kc = 2 * pr + k
                    vk = ("vtok", kc) if kc < 8 else "vtokc"
                    B.mm(pf[4][:, :], vtok[:, kc, kvh * 128:(kvh + 1) * 128], PT2[ti][:, k, :], kc == 0, kc == 11,
                         [vk, "PT%d" % ti], [("pf", 4)])
                    B.mm(pf[5][:, :], onesb[:], PT2[ti][:, k, :], kc == 0, kc == 11, ["onesb", "PT%d" % ti], [("pf", 5)])
                if pr == 5:
                    B.copy(tmpn[0][:], pf[5][:, :], [("pf", 5)], [("tmpn", 0)])
                    B.copy(tmpn[1][:], pf[4][:, :], [("pf", 4)], [("tmpn", 1)])
                    B.recip(rden[:], tmpn[0][:], [("tmpn", 0)], ["rstd"])
                    B.tt(attnT[:, h_, nsl], tmpn[1][:], rden[:], ALU.mult, [("tmpn", 1), "rstd"], [("hT", h_, n)])

            score(0)
            score(1)
            for g in range(len(units)):
                if g + 2 < len(units):
                    score(g + 2)
                pv(g)

            slot, sk = ws.acquire(LD["wo"])
            cnt = 0
            for c in range(8):
                for n in range(2):
                    pi = cnt % 2
                    cnt += 1
                    for kc in range(8):
                        B.mm(pf[pi][:, :], slot[:, kc * 1024 + c * 128: kc * 1024 + (c + 1) * 128],
                             attnT[:, kc, n * 512:(n + 1) * 512], kc == 0, kc == 7, [sk, ("hT", kc, n)], [("pf", pi)])
                    tl = range(4 * n, 4 * n + 4)
                    B.stt(yT[:, c, n * 512:(n + 1) * 512], pf[pi][:, :], adaT[:, l, 16 + c:17 + c],
                          yT[:, c, n * 512:(n + 1) * 512], ALU.mult, ALU.add,
                          [("pf", pi), ("ada", l)] + yk(c, tl), yk(c, tl))
            arena_barrier(AK, BIGK)


        TWO_PI = float(2.0 * np.pi)
        PI_LO = 3.1415925

        def ar_view(off, nelem, dt):
            v = big[:, off // 2:(off // 2) + nelem * (2 if dt == F32 else 1)]
            if dt == F32:
                v = v.bitcast(F32)
            return v

        Toep = ar_view(0, 4096, BF16).rearrange("p (g m) -> p g m", g=32)
        WendT = ar_view(8192, 8192, BF16).rearrange("p (q r m) -> p q r m", q=4, r=16)
        HinB = WendT
        Wout = ar_view(24576, 8192, BF16).rearrange("p (r q m) -> p r q m", r=16, q=4)
        Ubuf = ar_view(40960, 4096, BF16).rearrange("p (g m) -> p g m", g=32)
        S5K = keys("TO", range(32)) + keys("WT", range(4), range(16)) + keys("WO", range(16)) + \
            keys("U", range(32)) + ["Dreg", "E0", "E1", "E2", "E3"]

        def s5_prep(hooks):
            Dv = ar_view(40960, 2048, F32)
            PR = Dv[:, 0:256].rearrange("p (r m) -> p r m", r=16)
            PIm = Dv[:, 256:512].rearrange("p (r m) -> p r m", r=16)
            tA = Dv[:, 512:768].rearrange("p (r m) -> p r m", r=16)
            tB = Dv[:, 768:1024].rearrange("p (r m) -> p r m", r=16)
            tI = itile[:, 0:256].rearrange("p (r m) -> p r m", r=16)
            Bbr = Dv[:, 1280:1536].rearrange("p (r c) -> p r c", r=16)
            Bbi = Dv[:, 1536:1792].rearrange("p (r c) -> p r c", r=16)
            sm_ = Dv[:, 1792:2048]
            sv = lambda i: sm_[:, i * 16:(i + 1) * 16]
            AFr = ar_view(49152, 2048, BF16).rearrange("p (r s c) -> p r s c", r=16, s=8)
            nAFi = ar_view(49152 + 4096, 2048, BF16).rearrange("p (r s c) -> p r s c", r=16, s=8)
            CFr = ar_view(49152 + 8192, 2048, BF16).rearrange("p (r s c) -> p r s c", r=16, s=8)
            CFi = ar_view(49152 + 12288, 2048, BF16).rearrange("p (r s c) -> p r s c", r=16, s=8)
            t1 = xin[0][:, :].rearrange("p (r s c) -> p r s c", r=8, s=8)
            t2 = xin[1][:, :].rearrange("p (r s c) -> p r s c", r=8, s=8)
            slot_in, SK_IN = ws.acquire(LD["s5in"])
            ws.pin(LD["s5in"])
            sqf = slot_in[:, 0:2784].bitcast(F32)
            lamP = sqf[:, 0:64].rearrange("p (x d r) -> p x d r", x=2, d=2)
            ldtP = sqf[:, 64:96].rearrange("p (d r) -> p d r", d=2)
            BPt = sqf[:, 96:608].rearrange("p (x r c) -> p x r c", x=2, r=16)
            CPt = sqf[:, 608:1120].rearrange("p (x r c) -> p x r c", x=2, r=16)
            nv = sqf[:, 1120:1136]
            Mf1 = sqf[:, 1136:1264]
            Mb1 = sqf[:, 1264:1392]
            SQK = [SK_IN]
            XK = [("xin", 0), ("xin", 1)]
            D_ = ["Dreg"]

            def reduce_sin(out, ang, shift, kk):
                B.ts(tB, ang, 1.0 / TWO_PI, ALU.mult, kk, D_, s2=shift / TWO_PI, op1=ALU.add)
                B.copy(tI, tB, D_, D_ + ["itile"])
                B.copy(tB, tI, D_ + ["itile"], D_)
                B.stt(tB, tB, -TWO_PI, ang, ALU.mult, ALU.add, D_ + kk, D_)
                B.ts(tB, tB, shift, ALU.add, D_, D_, s2=PI_LO, op1=ALU.min)
                B.ts(tB, tB, -PI_LO, ALU.max, D_, D_)
                B.act(out, tB, AF.Sin, D_, D_)

            B.tt(Toep[:, :, :], ident[:, :].unsqueeze(1).to_broadcast([128, 32, 128]),
                 dS[:, :].unsqueeze(2).to_broadcast([128, 32, 128]), ALU.mult, ["ident", "dS"], keys("TO", range(32)))

            for d in range(2):
                dt_, ar_, ai_ = sv(0), sv(1), sv(2)
                B.act(dt_, ldtP[:, d, :], AF.Exp, SQK, D_)
                B.tt(ar_, lamP[:, 0, d, :], dt_, ALU.mult, SQK + D_, D_)
                B.tt(ai_, lamP[:, 1, d, :], dt_, ALU.mult, SQK + D_, D_)
                nvb = nv.unsqueeze(1).to_broadcast([128, 16, 16])
                B.tt(tA, ai_.unsqueeze(2).to_broadcast([128, 16, 16]), nvb, ALU.mult, D_ + SQK, D_)
                reduce_sin(PIm, tA, 0.0, D_)
                reduce_sin(PR, tA, float(np.pi / 2), D_)
                B.tt(tA, ar_.unsqueeze(2).to_broadcast([128, 16, 16]), nvb, ALU.mult, D_ + SQK, D_)
                B.act(tA, tA, AF.Exp, D_, D_)
                B.tt(PR, PR, tA, ALU.mult, D_, D_)
                B.tt(PIm, PIm, tA, ALU.mult, D_, D_)
                if d == 0:
                    B.copy(P0r[:, :], PR[:, :, 0], D_, ["P0"])
                    B.copy(P0i[:, :], PIm[:, :, 0], D_, ["P0"])
                B.copy(rho8[:, d, :], tA[:, :, 15], D_, ["rho8"])
                B.copy(L8r[:, d, :], PR[:, :, 15], D_, ["L8"])
                B.copy(L8i[:, d, :], PIm[:, :, 15], D_, ["L8"])
                t8 = sv(3)
                B.ts(t8, ai_, 8.0, ALU.mult, D_, D_)
                B.ts(sv(4), t8, 1.0 / TWO_PI, ALU.mult, D_, D_)
                B.copy(tI[:, 0, :], sv(4), D_, D_ + ["itile"])
                B.copy(sv(4), tI[:, 0, :], D_ + ["itile"], D_)
                B.stt(th8[:, d, :], sv(4), -TWO_PI, t8, ALU.mult, ALU.add, D_, ["th8"])
                nr, ni, l2, kr, ki, tq = sv(5), sv(6), sv(7), sv(8), sv(9), sv(10)
                lr, li = lamP[:, 0, d, :], lamP[:, 1, d, :]
                B.ts(nr, PR[:, :, 8], -1.0, ALU.add, D_, D_)
                B.copy(ni, PIm[:, :, 8], D_, D_)
                B.tt(l2, lr, lr, ALU.mult, SQK, D_)
                B.tt(tq, li, li, ALU.mult, SQK, D_)
                B.tt(l2, l2, tq, ALU.add, D_, D_)
                B.recip(l2, l2, D_, D_)
                B.tt(kr, nr, lr, ALU.mult, D_ + SQK, D_)
                B.tt(tq, ni, li, ALU.mult, D_ + SQK, D_)
                B.tt(kr, kr, tq, ALU.add, D_, D_)
                B.tt(kr, kr, l2, ALU.mult, D_, D_)
                B.tt(ki, ni, lr, ALU.mult, D_ + SQK, D_)
                B.tt(tq, nr, li, ALU.mult, D_ + SQK, D_)
                B.tt(ki, ki, tq, ALU.subtract, D_, D_)
                B.tt(ki, ki, l2, ALU.mult, D_, D_)
                krb = kr.unsqueeze(2).to_broadcast([128, 16, 16])
                kib = ki.unsqueeze(2).to_broadcast([128, 16, 16])
                tAc = tA
                B.tt(Bbr, BPt[:, 0, :, :], krb, ALU.mult, SQK + D_, D_)
                B.tt(tAc, BPt[:, 1, :, :], kib, ALU.mult, SQK + D_, D_)
                B.tt(Bbr, Bbr, tAc, ALU.subtract, D_, D_)
                B.tt(Bbi, BPt[:, 1, :, :], krb, ALU.mult, SQK + D_, D_)
                B.tt(tAc, BPt[:, 0, :, :], kib, ALU.mult, SQK + D_, D_)
                B.tt(Bbi, Bbi, tAc, ALU.add, D_, D_)

                def cplx(outr, outi, Ar, Ai, msl, negi, ok):
                    for hf in range(2):
                        rs = slice(hf * 8, (hf + 1) * 8)
                        Arb = Ar[:, rs, :].unsqueeze(2).to_broadcast([128, 8, 8, 16])
                        Aib = Ai[:, rs, :].unsqueeze(2).to_broadcast([128, 8, 8, 16])
                        Prb = PR[:, rs, msl].unsqueeze(3).to_broadcast([128, 8, 8, 16])
                        Pib = PIm[:, rs, msl].unsqueeze(3).to_broadcast([128, 8, 8, 16])
                        B.tt(t1, Arb, Prb, ALU.mult, D_ + SQK, XK[0:1])
                        B.tt(t2, Aib, Pib, ALU.mult, D_ + SQK, XK[1:2])
                        B.tt(outr[:, rs, :, :], t1, t2, ALU.subtract, XK, ok)
                        B.tt(t1, Arb, Pib, ALU.mult, D_ + SQK, XK[0:1])
                        B.tt(t2, Aib, Prb, ALU.mult, D_ + SQK, XK[1:2])
                        if negi:
                            B.stt(outi[:, rs, :, :], t1, -1.0, t2, ALU.mult, ALU.subtract, XK, ok)
                        else:
                            B.tt(outi[:, rs, :, :], t1, t2, ALU.add, XK, ok)

                if d == 0:
                    m_af = slice(7, None, -1)
                    m_cf = slice(7, 15)
                    m_we = slice(14, 6, -1)
                    m_wo = slice(8, 16)
                else:
                    m_af = slice(7, 15)
                    m_cf = slice(7, None, -1)
                    m_we = slice(7, 15)
                    m_wo = slice(15, 7, -1)
                cplx(AFr, nAFi, Bbr, Bbi, m_af, True, ["E0", "E1"])
                cplx(CFr, CFi, CPt[:, 0, :, :], CPt[:, 1, :, :], m_cf, False, ["E2", "E3"])
                hooks[2 * d]()
                for gb in range(8):
                    pi = gb % 2
                    for k in range(4):
                        g = gb * 4 + k
                        half, pair = g // 16, g % 16
                        rs_ = slice(half * 64, (half + 1) * 64)
                        afr = AFr[rs_, pair, :, :].rearrange("p s c -> p (s c)")
                        afi = nAFi[rs_, pair, :, :].rearrange("p s c -> p (s c)")
                        cfr = CFr[rs_, pair, :, :].rearrange("p s c -> p (s c)")
                        cfi = CFi[rs_, pair, :, :].rearrange("p s c -> p (s c)")
                        B.mm(pf[pi][:, k * 128:(k + 1) * 128], afr, cfr, True, False, ["E0", "E2"], [("pf", pi)])
                        B.mm(pf[pi][:, k * 128:(k + 1) * 128], afi, cfi, False, True, ["E1", "E3"], [("pf", pi)])
                    tm = tmpn[pi]
                    B.tt(tm[:].rearrange("p (g m) -> p g m", g=4), pf[pi][:, :].rearrange("p (g m) -> p g m", g=4),
                         (Mf1 if d == 0 else Mb1).unsqueeze(1).to_broadcast([128, 4, 128]), ALU.mult,
                         [("pf", pi)] + SQK, [("tmpn", pi)])
                    tg = Toep[:, gb * 4:(gb + 1) * 4, :]
                    B.tt(tg, tg, tm[:].rearrange("p (g m) -> p g m", g=4), ALU.add,
                         [("tmpn", pi)] + keys("TO", range(gb * 4, gb * 4 + 4)), keys("TO", range(gb * 4, gb * 4 + 4)))
                hooks[2 * d + 1]()
                if d == 0:
                    cplx(AFr, nAFi, Bbr, Bbi, m_we, False, ["E0", "E1"])
                else:
                    B.ts(nAFi[:, :, :, :], nAFi[:, :, :, :], -1.0, ALU.mult, ["E1"], ["E1"])
                for x, src in enumerate((AFr, nAFi)):
                    q = d * 2 + x
                    for hb in range(2):
                        for k in range(8):
                            pair = hb * 8 + k
                            B.tr(pb[hb][:, k * 128:(k + 1) * 128], src[:, pair, :, :].rearrange("p s c -> p (s c)"),
                                 identb[:], ["E%d" % x, "identb"], [("pb", hb)])
                        B.copy(WendT[:, q, hb * 8:(hb + 1) * 8, :], pb[hb][:, :].rearrange("p (r m) -> p r m", r=8),
                               [("pb", hb)], keys("WT", q, range(hb * 8, hb * 8 + 8)), eng="act")
                cplx(Wout[:, :, d * 2 + 0, :].rearrange("p r (t c) -> p r t c", t=8),
                     Wout[:, :, d * 2 + 1, :].rearrange("p r (t c) -> p r t c", t=8),
                     CPt[:, 0, :, :], CPt[:, 1, :, :], m_wo, True, keys("WO", range(16)))

        def s5_phase(l):
            ws.unpin(LD["s5in"])
            uTs = sq[:, :, :].rearrange("p c t -> p (c t)").rearrange("p (cc s j) -> p cc s j", cc=4, s=8)
            E8v = E8[:, :].rearrange("p (a x) -> p a x", a=8)
            Xb = ar_view(49152, 1024, F32).rearrange("p (x k j) -> p x k j", x=2, k=4)
            tab = ar_view(49152 + 4096, 1024, F32).rearrange("p (x k j) -> p x k j", x=2, k=4)
            tmp = [ar_view(49152 + 8192 + 2048 * i, 512, F32).rearrange("p (k j) -> p k j", k=4) for i in range(4)]
            EK = ["E0", "E1", "E2", "E3"]
            for gb in range(8):
                pi = gb % 2
                for k in range(4):
                    g = gb * 4 + k
                    cc, gl = g // 8, g % 8
                    for s_ in range(8):
                        B.mm(pf[pi][:, k * 128:(k + 1) * 128], E8v[:, gl, 112 - s_ * 16: 112 - s_ * 16 + 128],
                             uTs[:, cc, s_, :], s_ == 0, s_ == 7, ["E8"] + keys("sq", [2 * cc, 2 * cc + 1]), [("pf", pi)])
                B.copy(Ubuf[:, gb * 4:(gb + 1) * 4, :], pf[pi][:, :].rearrange("p (g m) -> p g m", g=4), [("pf", pi)],
                       keys("U", range(gb * 4, gb * 4 + 4)), eng=("act" if gb % 2 == 0 else "dve"))
            Ue = ar_view(49152, 256, BF16)
            for g in range(32):
                cc, gl = g // 8, g % 8
                B.mm(pf[2][:, g * 8:g * 8 + 4], E8v[:, gl, 112:240], uTs[:, cc, 0, 0:128:32], True, True,
                     ["E8"] + keys("sq", [2 * cc, 2 * cc + 1]), [("pf", 2)])
                B.mm(pf[2][:, g * 8 + 4:g * 8 + 8], E8v[:, gl, 112:240], uTs[:, cc, 7, 31:128:32], True, True,
                     ["E8"] + keys("sq", [2 * cc, 2 * cc + 1]), [("pf", 2)])
            B.copy(Ue[:, :], pf[2][:, 0:256], [("pf", 2)], ["E0"])
            Uev = Ue[:, :].rearrange("p (g d s) -> p g d s", g=32, d=2)
            x0v = pf[3][:, :].rearrange("p (q r ab s) -> p q r ab s", q=4, r=16, ab=2)
            for q in range(4):
                d = q // 2
                for pair in range(16):
                    B.mm(x0v[:, q, pair, 0, :], WendT[0:32, q, pair, :], Uev[0:32, pair, d, :], True, True,
                         [("WT", q, pair), "E0"], [("pf", 3)])
                    B.mm(x0v[:, q, pair, 1, :], WendT[0:32, q, pair, :], Uev[0:32, 16 + pair, d, :], True, True,
                         [("WT", q, pair), "E0"], [("pf", 3)])
            xe = ar_view(49152 + 4096, 256, F32).rearrange("p (x d s r) -> p x d s r", x=2, d=2, s=4)
            for q in range(4):
                d, x = q // 2, q % 2
                for half in range(2):
                    rs_ = slice(half * 64, (half + 1) * 64)
                    B.copy(xe[rs_, x, d, :, :], x0v[rs_, q, :, half, :].rearrange("p r s -> p s r"), [("pf", 3)], ["E1"])
            hv0 = Hfin[:, 0, :].rearrange("p (d s r) -> p d s r", d=2, s=4)
            hv1 = Hfin[:, 1, :].rearrange("p (d s r) -> p d s r", d=2, s=4)
            B.copy(hv0[:, 1, :, :], xe[:, 0, 1, :, :], ["E1"], ["Hfin"])
            B.copy(hv1[:, 1, :, :], xe[:, 1, 1, :, :], ["E1"], ["Hfin"])
            p0r = P0r[:, :].unsqueeze(1).to_broadcast([128, 4, 16])
            p0i = P0i[:, :].unsqueeze(1).to_broadcast([128, 4, 16])
            e1 = ar_view(49152 + 8192, 64, F32).rearrange("p (s r) -> p s r", s=4)
            e2 = ar_view(49152 + 8192 + 2048, 64, F32).rearrange("p (s r) -> p s r", s=4)
            B.tt(e1, xe[:, 0, 0, :, :], p0r, ALU.mult, ["E1", "P0"], ["E2"])
            B.tt(e2, xe[:, 1, 0, :, :], p0i, ALU.mult, ["E1", "P0"], ["E2"])
            B.tt(hv0[:, 0, :, :], e1, e2, ALU.subtract, ["E2"], ["Hfin"])
            B.tt(e1, xe[:, 1, 0, :, :], p0r, ALU.mult, ["E1", "P0"], ["E2"])
            B.tt(e2, xe[:, 0, 0, :, :], p0i, ALU.mult, ["E1", "P0"], ["E2"])
            B.tt(hv1[:, 0, :, :], e1, e2, ALU.add, ["E2"], ["Hfin"])
            B.ts(th8n[:, :, :], th8[:, :, :], 1.0 / TWO_PI, ALU.mult, ["th8"], ["th8n"])
            for d in range(2):
                B.tt(LIr[:, d, :], L8r[:, d, :], initP[:, 0, d, :], ALU.mult, ["L8", "initP"], ["LI"])
                B.tt(LIt[:, :], L8i[:, d, :], initP[:, 1, d, :], ALU.mult, ["L8", "initP"], ["LIt"])
                B.tt(LIr[:, d, :], LIr[:, d, :], LIt[:, :], ALU.subtract, ["LI", "LIt"], ["LI"])
                B.tt(LIi[:, d, :], L8r[:, d, :], initP[:, 1, d, :], ALU.mult, ["L8", "initP"], ["LI"])
                B.tt(LIt[:, :], L8i[:, d, :], initP[:, 0, d, :], ALU.mult, ["L8", "initP"], ["LIt"])
                B.tt(LIi[:, d, :], LIi[:, d, :], LIt[:, :], ALU.add, ["LI", "LIt"], ["LI"])
            cntb = 0
            for d in range(2):
                for bt in range(4):
                    prs = range(bt * 4, bt * 4 + 4)
                    for x in range(2):
                        q = d * 2 + x
                        for k2 in range(2):
                            pi = 2 + (cntb % 2)
                            cntb += 1
                            for kk in range(2):
                                pair = bt * 4 + k2 * 2 + kk
                                B.mm(pf[pi][:, (2 * kk) * 128:(2 * kk + 1) * 128], WendT[:, q, pair, :], Ubuf[:, pair, :],
                                     True, True, [("WT", q, pair), ("U", pair)], [("pf", pi)])
                                B.mm(pf[pi][:, (2 * kk + 1) * 128:(2 * kk + 2) * 128], WendT[:, q, pair, :], Ubuf[:, 16 + pair, :],
                                     True, True, [("WT", q, pair), ("U", 16 + pair)], [("pf", pi)])
                            pv = pf[pi][:, :].rearrange("p (k ab j) -> p k ab j", k=2, ab=2)
                            B.copy(Xb[0:64, x, k2 * 2:k2 * 2 + 2, :], pv[0:64, :, 0, :], [("pf", pi)], ["E0"], eng="act")
                            B.copy(Xb[64:128, x, k2 * 2:k2 * 2 + 2, :], pv[64:128, :, 1, :], [("pf", pi)], ["E0"], eng="act")
                    j0 = 0 if d == 0 else 127
                    B.tt(Xb[:, 0, :, j0], Xb[:, 0, :, j0], LIr[:, d, bt * 4:bt * 4 + 4], ALU.add, ["E0", "LI"], ["E0"])
                    B.tt(Xb[:, 1, :, j0], Xb[:, 1, :, j0], LIi[:, d, bt * 4:bt * 4 + 4], ALU.add, ["E0", "LI"], ["E0"])
                    tq_ = tmp[3]
                    B.tt(tq_[:, :, :], th8n[:, d, bt * 4:bt * 4 + 4].unsqueeze(2).to_broadcast([128, 4, 128]),
                         jv[:, :].unsqueeze(1).to_broadcast([128, 4, 128]), ALU.mult, ["th8n", "jv"], ["E3"])
                    tb_ = tmp[2]
                    ti_ = itile[:, :].rearrange("p (k j) -> p k j", k=4)
                    B.copy(ti_, tq_[:, :, :], ["E3"], ["E2", "itile"])
                    B.copy(tb_[:, :, :], ti_, ["E2", "itile"], ["E3"])
                    B.tt(tb_[:, :, :], tq_[:, :, :], tb_[:, :, :], ALU.subtract, ["E3"], ["E3"])
                    B.act(tab[:, 1, :, :], tb_[:, :, :], AF.Sin, ["E3"], ["E1"], scale=TWO_PI * 0.99999)
                    B.stt(tb_[:, :, :], tb_[:, :, :], -1.0, tb_[:, :, :], ALU.mult, ALU.max, ["E3"], ["E3"])
                    B.act(tab[:, 0, :, :], tb_[:, :, :], AF.Sin, ["E3", "halfpi"], ["E1"], scale=-TWO_PI * 0.99999,
                          bias=halfpi[:, 0:1])
                    cs, sn = tab[:, 0, :, :], tab[:, 1, :, :]
                    Xr, Xi = Xb[:, 0, :, :], Xb[:, 1, :, :]
                    B.tt(tmp[0][:, :, :], Xr, cs, ALU.mult, ["E0", "E1"], ["E2"])
                    B.tt(tmp[1][:, :, :], Xi, sn, ALU.mult, ["E0", "E1"], ["E2"])
                    B.tt(tmp[2][:, :, :], Xi, cs, ALU.mult, ["E0", "E1"], ["E3"])
                    B.tt(tmp[3][:, :, :], Xr, sn, ALU.mult, ["E0", "E1"], ["E3"])
                    B.tt(Xr, tmp[0][:, :, :], tmp[1][:, :, :], ALU.add if d == 0 else ALU.subtract, ["E2"], ["E0"])
                    B.tt(Xi, tmp[2][:, :, :], tmp[3][:, :, :], ALU.subtract if d == 0 else ALU.add, ["E3"], ["E0"])
                    mult = tmp[0]
                    smk = smf if d == 0 else smb
                    B.tt(mult[:, :, :], smk[:, :].unsqueeze(1).to_broadcast([128, 4, 128]),
                         rho8[:, d, bt * 4:bt * 4 + 4].unsqueeze(2).to_broadcast([128, 4, 128]), ALU.mult,
                         ["smf", "smb", "rho8"], ["E2"])
                    mf = mult[:, :, :].rearrange("p k j -> p (k j)")
                    for x, dst in ((0, tmp[2]), (1, tmp[3])):
                        src = Xb[:, x, :, :].rearrange("p k j -> p (k j)")
                        dfl = dst[:, :, :].rearrange("p k j -> p (k j)")
                        if d == 0:
                            B.P.add("dve", lambda h, o=dfl, m=mf, s_=src: h.tensor_tensor_scan(
                                out=o, data0=m, data1=s_, initial=0.0, op0=ALU.mult, op1=ALU.add), ["E0", "E2"], ["E3"])
                        else:
                            B.P.add("dve", lambda h, o=dfl[:, ::-1], m=mf[:, ::-1], s_=src[:, ::-1]: h.tensor_tensor_scan(
                                out=o, data0=m, data1=s_, initial=0.0, op0=ALU.mult, op1=ALU.add), ["E0", "E2"], ["E3"])
                    kr_, ki_ = tmp[2][:, :, :], tmp[3][:, :, :]
                    B.tt(tmp[0][:, :, :], kr_, cs, ALU.mult, ["E3", "E1"], ["E2"])
                    B.tt(tmp[1][:, :, :], ki_, sn, ALU.mult, ["E3", "E1"], ["E2"])
                    B.tt(Xr, tmp[0][:, :, :], tmp[1][:, :, :], ALU.subtract if d == 0 else ALU.add, ["E2"], ["E0"])
                    B.tt(tmp[0][:, :, :], ki_, cs, ALU.mult, ["E3", "E1"], ["E2"])
                    B.tt(tmp[1][:, :, :], kr_, sn, ALU.mult, ["E3", "E1"], ["E2"])
                    B.tt(Xi, tmp[0][:, :, :], tmp[1][:, :, :], ALU.add if d == 0 else ALU.subtract, ["E2"], ["E0"])
                    for x in range(2):
                        q = d * 2 + x
                        wk = keys("WT", q, prs)
                        if d == 0:
                            B.tt(HinB[:, q, bt * 4:bt * 4 + 4, 1:128], Xb[:, x, :, 0:127],
                                 smf[:, 1:128].unsqueeze(1).to_broadcast([128, 4, 127]), ALU.mult, ["E0", "smf"], wk)
                            B.copy(HinB[:, q, bt * 4:bt * 4 + 4, 0], initP[:, x, d, bt * 4:bt * 4 + 4], ["initP"], wk)
                        else:
                            B.tt(HinB[:, q, bt * 4:bt * 4 + 4, 0:127], Xb[:, x, :, 1:128],
                                 smb[:, 0:127].unsqueeze(1).to_broadcast([128, 4, 127]), ALU.mult, ["E0", "smb"], wk)
                            B.copy(HinB[:, q, bt * 4:bt * 4 + 4, 127], initP[:, x, d, bt * 4:bt * 4 + 4], ["initP"], wk)
            ada(l + 1)
            for x, od in ((0, s5re_d), (1, s5im_d)):
                B.tr(pf[0][:, x * 128:(x + 1) * 128], Hfin[:, x, :], ident[:], ["Hfin", "ident"], [("pf", 0)])
            hst = tmpn[0]
            B.copy(hst[:, 0:256], pf[0][:, 0:256], [("pf", 0)], [("tmpn", 0)])
            for x, od in ((0, s5re_d), (1, s5im_d)):
                for d in range(2):
                    for sg in range(4):
                        r0 = d * 64 + sg * 16
                        B.dma("sp", od[sg, d, :, :].rearrange("(h r) p -> r h p", h=2),
                              hst[r0:r0 + 16, x * 128:(x + 1) * 128].rearrange("r (h p) -> r h p", h=2),
                              [("tmpn", 0)], (), final=True)
            for gb in range(8):
                pi = gb % 2
                for k in range(4):
                    g = gb * 4 + k
                    half, pair = g // 16, g % 16
                    rs_ = slice(half * 64, (half + 1) * 64)
                    B.mm(pf[pi][:, k * 128:(k + 1) * 128], Toep[:, g, :], Ubuf[:, g, :], True, False,
                         [("TO", g), ("U", g)], [("pf", pi)])
                    for q in range(4):
                        B.mm(pf[pi][:, k * 128:(k + 1) * 128], Wout[rs_, pair, q, :], HinB[rs_, q, pair, :], False, q == 3,
                             [("WO", pair), ("WT", q, pair)], [("pf", pi)])
                B.copy(Ubuf[:, gb * 4:(gb + 1) * 4, :], pf[pi][:, :].rearrange("p (g m) -> p g m", g=4), [("pf", pi)],
                       keys("U", range(gb * 4, gb * 4 + 4)), eng=("act" if gb % 2 == 0 else "dve"))
            y5 = ar_view(8192, 4096, F32).rearrange("p (c t) -> p c t", c=4)
            Y5K = keys("WT", range(4), range(16))
            gT = ar_view(24576, 4096, BF16).rearrange("p (c t) -> p c t", c=4)
            GTK = keys("WO", range(16))
            for cc in range(4):
                for hb in range(2):
                    pi = 2 + hb
                    for k in range(4):
                        tp = hb * 4 + k
                        for gl in range(8):
                            B.mm(pf[pi][:, k * 128:(k + 1) * 128], E8v[:, tp, 112 - gl * 16: 112 - gl * 16 + 128],
                                 Ubuf[:, cc * 8 + gl, :], gl == 0, gl == 7, ["E8", ("U", cc * 8 + gl)], [("pf", pi)])
                    B.copy(y5[:, cc, :].rearrange("p (j t) -> p t j", t=8)[:, hb * 4:(hb + 1) * 4, :],
                           pf[pi][:, :].rearrange("p (t j) -> p t j", t=4), [("pf", pi)], Y5K,
                           eng=("act" if hb == 0 else "dve"))
                B.act(gT[:, cc, :], y5[:, cc, :], AF.Gelu_apprx_tanh, Y5K, GTK)
            s5o = sq[:, :, :].rearrange("p c t -> p (c t)").rearrange("p (c t) -> p c t", c=4)
            slot, sk = ws.acquire(LD["wglu"])
            cg = 0
            for m in range(4):
                for n in range(2):
                    pi = cg % 2
                    cg += 1
                    for kc in range(4):
                        B.mm(pf[pi][:, :], slot[:, kc * 512 + m * 128: kc * 512 + (m + 1) * 128],
                             gT[:, kc, n * 512:(n + 1) * 512], kc == 0, kc == 3, [sk] + GTK, [("pf", pi)])
                    B.act(tmpn[pi][:], pf[pi][:, :], AF.Sigmoid, [("pf", pi), "bglu"], [("tmpn", pi)], bias=bglu[:, m:m + 1])
                    B.tt(s5o[:, m, n * 512:(n + 1) * 512], gT[:, m, n * 512:(n + 1) * 512], tmpn[pi][:], ALU.mult,
                         GTK + [("tmpn", pi)], keys("sq", [2 * m, 2 * m + 1]))

        cnt = [0]
        u_done = []

        def proj_fm(slot, sk, col0, evac):
            for n in range(2):
                pi = cnt[0] % 2
                cnt[0] += 1
                for kc in range(8):
                    B.mm(pf[pi][:, :], col0(kc), hT[:, kc, n * 512:(n + 1) * 512], kc == 0, kc == 7,
                         [sk, ("hT", kc, n)], [("pf", pi)])
                evac(pf[pi], ("pf", pi), n)

        def u_proj():
            uTs = sq[:, :, :].rearrange("p c t -> p (c t)").rearrange("p (cc s j) -> p cc s j", cc=4, s=8)
            slot, sk = ws.acquire(LD["winU"])
            for cc in range(4):
                def ev(ps, pk, n, cc=cc):
                    B.copy(uTs[:, cc, :, n * 64:(n + 1) * 64],
                           ps[:, :].rearrange("p (j s) -> p s j", s=8), [pk], keys("sq", [2 * cc, 2 * cc + 1]), eng="act")
                proj_fm(slot, sk, lambda kc, cc=cc, slot=slot: slot[:, kc * 512 + cc * 128: kc * 512 + (cc + 1) * 128], ev)
            u_done.append(1)

        def even_mixer(l):
            if not u_done:
                u_proj()
            s5_phase(l)
            s5o = sq[:, :, :].rearrange("p c t -> p (c t)").rearrange("p (c t) -> p c t", c=4)

            arena.reset()
            qT = arena.alloc(2 * NT, BF16).rearrange("p (h t) -> p h t", h=2)
            kT = arena.alloc(2 * NT, BF16).rearrange("p (h t) -> p h t", h=2)
            vtok = arena.alloc(8 * 512, BF16).rearrange("p (i f) -> p i f", i=8)
            srT = arena.alloc(4 * NT, BF16).rearrange("p (h t) -> p h t", h=4)
            glrT = arena.alloc(NT, BF16)
            qd = arena.alloc(4 * NT, BF16).rearrange("p (a t) -> p a t", a=4)
            kd = arena.alloc(4 * NT, BF16).rearrange("p (a t) -> p a t", a=4)
            Sin = arena.alloc(32 * 128, BF16).rearrange("p (a v) -> p a v", a=32)
            spt = arena.alloc(NT, F32)
            cbt = arena.alloc(NT, F32)
            ebt = spt
            kdT = arena.alloc(8 * 128, BF16).rearrange("p (n k) -> p n k", n=8)
            attb0 = arena.alloc(512, BF16)
            sqo = arena.alloc(512, BF16)
            t1o = arena.alloc(512, F32)
            GK = ["gq", "gk", "srT", "glrT", "spt", "cbt", "kdT", "att0", "sqo", "t1o"] + \
                keys("vtokg", range(8)) + keys("qd", range(4)) + keys("kd", range(4)) + keys("Sin", range(32))
            arena_barrier(S5K, GK)

            slot, sk = ws.acquire(LD["winQK"])
            for m in range(4):
                def ev(ps, pk, n, m=m):
                    if m < 2:
                        B.act(qT[:, m, n * 512:(n + 1) * 512], ps[:, :], AF.Copy, [pk], ["gq"], scale=0.125)
                    else:
                        B.copy(kT[:, m - 2, n * 512:(n + 1) * 512], ps[:, :], [pk], ["gk"], eng="act")
                proj_fm(slot, sk, lambda kc, m=m, slot=slot: slot[:, kc * 512 + m * 128: kc * 512 + (m + 1) * 128], ev)
            slot, sk = ws.acquire(LD["winVR"])
            for m in range(4):
                def ev(ps, pk, n, m=m):
                    B.act(srT[:, m, n * 512:(n + 1) * 512], ps[:, :], AF.Silu, [pk], ["srT"])
                proj_fm(slot, sk, lambda kc, m=m, slot=slot: slot[:, kc * 1024 + 512 + m * 128: kc * 1024 + 512 + (m + 1) * 128], ev)
            for i in range(8):
                pi = cnt[0] % 2
                cnt[0] += 1
                for kc in range(8):
                    B.mm(pf[pi][:, :], hT[:, kc, i * 128:(i + 1) * 128], slot[:, kc * 1024: kc * 1024 + 512],
                         kc == 0, kc == 7, [sk, ("hT", kc, i // 4)], [("pf", pi)])
                B.copy(vtok[:, i, :], pf[pi][:, :], [("pf", pi)], [("vtokg", i)], eng="act")
            slot, sk = ws.acquire(LD["winG"])
            for n in range(2):
                pi = cnt[0] % 2
                cnt[0] += 1
                for kc in range(8):
                    B.mm(pf[pi][0:32, :], slot[:, kc * 32:(kc + 1) * 32], hT[:, kc, n * 512:(n + 1) * 512],
                         kc == 0, kc == 7, [sk, ("hT", kc, n)], [("pf", pi)])
                B.copy(glrT[0:32, n * 512:(n + 1) * 512], pf[pi][0:32, :], [("pf", pi)], ["glrT"], eng="act")

            B.memset(m01[:], 1.0, ["m01"])
            B.memset(m01[:, 0::128], 0.0, ["m01"])
            for d in range(2):
                for hp in range(2):
                    a = d * 2 + hp
                    for n in range(2):
                        pi = cnt[0] % 2
                        cnt[0] += 1
                        B.mm(pf[pi][:, :], wg2[:, d * 256 + hp * 128: d * 256 + (hp + 1) * 128],
                             glrT[0:32, n * 512:(n + 1) * 512], True, True, ["wg2", "glrT"], [("pf", pi)])
                        B.act(ebt[:, n * 512:(n + 1) * 512], pf[pi][:, :], AF.Exp, [("pf", pi), "nbg"], ["spt"],
                              scale=-1.0, bias=nbg[:, a:a + 1])
                        B.act(spt[:, n * 512:(n + 1) * 512], ebt[:, n * 512:(n + 1) * 512], AF.Ln, ["spt", "onec"], ["spt"],
                              bias=onec[:, 0:1])
                    if d == 0:
                        B.P.add("dve", lambda h, o=cbt[:, :], m=m01[:, :], x=spt[:, :]: h.tensor_tensor_scan(
                            out=o, data0=m, data1=x, initial=0.0, op0=ALU.mult, op1=ALU.add), ["m01", "spt"], ["cbt"])
                    else:
                        B.P.add("dve", lambda h, o=cbt[:, ::-1], m=m01[:, :], x=spt[:, ::-1]: h.tensor_tensor_scan(
                            out=o, data0=m, data1=x, initial=0.0, op0=ALU.mult, op1=ALU.add), ["m01", "spt"], ["cbt"])
                    B.act(ebt[:, :], cbt[:, :], AF.Exp, ["cbt"], ["spt"], scale=-1.0 / 16.0)
                    B.tt(qd[:, a, :], qT[:, hp, :], ebt[:, :], ALU.mult, ["gq", "spt"], [("qd", a)])
                    last = 127 if d == 0 else 0
                    B.copy(ebl[:, a, :], ebt[:, last::128], ["spt"], [("ebl", a)])
                    B.act(ebt[:, :], cbt[:, :], AF.Exp, ["cbt"], ["spt"], scale=1.0 / 16.0)
                    B.tt(kd[:, a, :], kT[:, hp, :], ebt[:, :], ALU.mult, ["gk", "spt"], [("kd", a)])

            sptb = spt.bitcast(BF16)
            cbtb = cbt.bitcast(BF16)
            kdTs = [kdT,
                    sptb[:, 0:1024].rearrange("p (n k) -> p n k", n=8),
                    sptb[:, 1024:2048].rearrange("p (n k) -> p n k", n=8),
                    cbtb[:, 0:1024].rearrange("p (n k) -> p n k", n=8)]
            Sbuf = [[Sst[0][:, :], Sst[1][:, :]],
                    [cbt[:, 512:640], cbt[:, 640:768]],
                    [cbt[:, 768:896], cbt[:, 896:1024]],
                    [t1o[:, 0:128], t1o[:, 128:256]]]
            CHK = keys("kdTs", range(4)) + keys("Sc", range(4), range(2))
            arena_barrier(["spt", "cbt", "t1o", "kdT"] + keys("S", range(2)), CHK)
            for a in range(4):
                for n in range(8):
                    B.tr(pb[a % 2][:, n * 128:(n + 1) * 128], kd[:, a, n * 128:(n + 1) * 128], identb[:],
                         [("kd", a), "identb"], [("pb", a % 2)])
                B.copy(kdTs[a][:, :, :], pb[a % 2][:, :].rearrange("p (n k) -> p n k", n=8), [("pb", a % 2)],
                       [("kdTs", a)], eng="act")
                d, hp = a // 2, a % 2
                B.dma("sp", Sbuf[a][0], glainit_d[d, hp * 128:(hp + 1) * 128, :], (), [("Sc", a, 0)])
            cur = [0, 0, 0, 0]
            for idx in range(8):
                for a in range(4):
                    d, hp = a // 2, a % 2
                    n = idx if d == 0 else 7 - idx
                    c_ = cur[a]
                    Sc, Sn = Sbuf[a][c_], Sbuf[a][1 - c_]
                    pA = pf[4 + (a % 2)]
                    pk = ("pf", 4 + (a % 2))
                    if idx > 0 and idx % 2 == 0:
                        B.ts(Sc, Sc, cm[:, 0:1], ALU.mult, [("Sc", a, c_), "cm"], [("Sc", a, c_)])
                    B.copy(Sin[:, a * 8 + n, :], Sc, [("Sc", a, c_)], [("Sin", a * 8 + n)], eng="act")
                    for hh in range(2):
                        h_ = hp * 2 + hh
                        B.mm(pA[:, hh * 128:(hh + 1) * 128], kdTs[a][:, n, :], vtok[:, n, h_ * 128:(h_ + 1) * 128],
                             True, True, [("kdTs", a), ("vtokg", n)], [pk])
                    B.ts(Stmp[:], Sc, ebl[:, a, n:n + 1], ALU.mult, [("Sc", a, c_), ("ebl", a)], ["Stmp"])
                    for hh in range(2):
                        rs_ = slice(hh * 64, (hh + 1) * 64)
                        B.stt(Sn[rs_, :], pA[rs_, hh * 128:(hh + 1) * 128], ebl[rs_, a, n:n + 1], Stmp[rs_, :],
                              ALU.mult, ALU.add, [pk, ("ebl", a), "Stmp"], [("Sc", a, 1 - c_)])
                    cur[a] = 1 - c_
                    if idx % 2 == 1:
                        B.dma("sp", glaout_d[n // 2, d, hp * 128:(hp + 1) * 128, :], Sn, [("Sc", a, 1 - c_)], (),
                              final=True)
            arena_barrier(CHK, ["spt", "cbt", "t1o", "kdT"] + keys("S", range(2)))

            attb = [rl[0][:, 0:256], rl[1][:, 0:256], attb0[:, 0:256]]
            attk = [("rl", 0), ("rl", 1), "att0"]
            gunits = [(n, h_) for n in range(8) for h_ in range(4)]

            def g_att(u):
                n, h_ = gunits[u]
                hp, hh = h_ // 2, h_ % 2
                rs_ = slice(hh * 64, (hh + 1) * 64)
                csl = slice(n * 128, (n + 1) * 128)
                pi = u % 2
                for d in range(2):
                    a = d * 2 + hp
                    B.mm(pf[pi][:, d * 128:(d + 1) * 128], kd[rs_, a, csl], qd[rs_, a, csl], True, True,
                         [("kd", a), ("qd", a)], [("pf", pi)])
                B.tt(attb[u % 3], pf[pi][:, 0:256], tri2[:, :], ALU.mult, [("pf", pi), "tri2"], [attk[u % 3]])

            def g_out(u):
                n, h_ = gunits[u]
                hp, hh = h_ // 2, h_ % 2
                rs_ = slice(hh * 64, (hh + 1) * 64)
                csl = slice(n * 128, (n + 1) * 128)
                ob = 2 + (n % 2)
                for d in range(2):
                    a = d * 2 + hp
                    B.mm(pf[ob][:, h_ * 128:(h_ + 1) * 128], vtok[:, n, h_ * 128:(h_ + 1) * 128],
                         attb[u % 3][:, d * 128:(d + 1) * 128], d == 0, False, [("vtokg", n), attk[u % 3]], [("pf", ob)])
                    B.mm(pf[ob][:, h_ * 128:(h_ + 1) * 128], Sin[rs_, a * 8 + n, :], qd[rs_, a, csl],
                         False, d == 1, [("Sin", a * 8 + n), ("qd", a)], [("pf", ob)])
                if h_ == 3:
                    B.act(sqo[:], pf[ob][:, :], AF.Square, [("pf", ob)], ["sqo"])
                    B.mm(pf[4][:, :], onesdv[:], sqo[:], True, True, ["onesdv", "sqo"], [("pf", 4)])
                    B.act(sd[:], pf[4][:, :], AF.Ln, [("pf", 4), "epsc"], ["sd"], bias=epsc[:, 0:1])
                    B.act(rstd[:], sd[:], AF.Exp, ["sd"], ["rstd"], scale=-0.5)
                    B.tt(t1o[:], pf[ob][:, :], rstd[:], ALU.mult, [("pf", ob), "rstd"], ["t1o"])
                    B.stt(hT[:, 4:8, csl], t1o[:].rearrange("p (h t) -> p h t", h=4), gnorm[:, 0:1], srT[:, :, csl],
                          ALU.mult, ALU.mult, ["t1o", "gnorm", "srT"], keys("hT", range(4, 8), n // 4))

            g_att(0)
            g_att(1)
            for u in range(len(gunits)):
                if u + 2 < len(gunits):
                    g_att(u + 2)
                g_out(u)

            slot, sk = ws.acquire(LD["wout"])
            for c in range(8):
                for n in range(2):
                    pi = cnt[0] % 2
                    cnt[0] += 1
                    for kc in range(8):
                        if kc < 4:
                            rhs_, rk = s5o[:, kc, n * 512:(n + 1) * 512], keys("sq", [2 * kc, 2 * kc + 1])
                        else:
                            rhs_, rk = hT[:, kc, n * 512:(n + 1) * 512], [("hT", kc, n)]
                        B.mm(pf[pi][:, :], slot[:, kc * 1024 + c * 128: kc * 1024 + (c + 1) * 128],
                             rhs_, kc == 0, kc == 7, [sk] + rk, [("pf", pi)])
                    tl = range(4 * n, 4 * n + 4)
                    B.stt(yT[:, c, n * 512:(n + 1) * 512], pf[pi][:, :], adaT[:, l, 16 + c:17 + c],
                          yT[:, c, n * 512:(n + 1) * 512], ALU.mult, ALU.add,
                          [("pf", pi), ("ada", l)] + yk(c, tl), yk(c, tl))
            arena_barrier(GK, BIGK)

        if DBG_SKIP_EVEN:
            input_transposes()
        if not DBG_SKIP_EVEN:
            s5_prep([lambda: (input_transposes(), ada_step(0, 0), ada_step(0, 1)),
                     lambda: ada_step(0, 2),
                     lambda: (ada_step(0, 3), ada_step(0, 4), ada_step(0, 5), ada_finish(0)),
                     lambda: None])
            norm_mod(0, 0)
            u_proj()
        for l in range(DEPTH):
            if DBG_SKIP_EVEN:
                ada(l)
            if l != 0 or DBG_SKIP_EVEN:
                norm_mod(l, 0)
            if l % 2 == 0 and not DBG_SKIP_EVEN:
                even_mixer(l)
            if l % 2 == 1 and not DBG_SKIP_ODD:
                attention(l)
            norm_mod(l, 1)
            mlp(l)

        for i in range(8):
            xb_ = xin[i % 2]
            xk = ("xin", i % 2)
            for half in range(2):
                ps = pf[half]
                pk = ("pf", half)
                for cc in range(4):
                    c = half * 4 + cc
                    B.tr(ps[:, cc * 128:(cc + 1) * 128], yT[:, c, i * 128:(i + 1) * 128], ident[:],
                         yk(c, i) + ["ident"], [pk])
                B.copy(xb_[:, half * 512:(half + 1) * 512], ps[:, :], [pk], [xk],
                       eng=("act" if half == 0 else "dve"))
            B.dma("sp", y_d[i * 128:(i + 1) * 128, :], xb_[:], [xk], (), final=True)

        P.emit(nc, B.fin)
    return nc


_NC_CACHE = {}


def _get_program():
    if "nc" not in _NC_CACHE:
        _NC_CACHE["nc"] = build_program()
    return _NC_CACHE["nc"]


def _fm(v, ncol):
    return np.ascontiguousarray(np.asarray(v, np.float32).reshape(ncol, 128).T)


def kernel(x_prompt, x_sample, state_s5_re, state_s5_im, state_gla, cache_k, cache_v, c, c_ctx,
           norm_mix, norm_mlp, w_ada, b_ada, w_mlp_in, w_mlp_out, w_in_e, w_out_e,
           s5_lambda_re, s5_lambda_im, s5_log_dt, s5_b_re, s5_b_im, s5_c_re, s5_c_im, s5_d,
           s5_w_glu, s5_b_glu, gla_w_gate2, gla_b_gate, gla_norm, w_qkv_o, w_o_o, q_norm, k_norm):
    f32 = np.float32
    x_prompt = np.asarray(x_prompt, f32)
    x_sample = np.asarray(x_sample, f32)
    nc = _get_program()
    ident = np.eye(128, dtype=f32)
    b_adaT = np.ascontiguousarray(np.stack([_fm(np.asarray(b_ada)[l], 48) for l in range(DEPTH)], axis=1))
    gmixT = np.ascontiguousarray(np.stack([_fm(np.asarray(norm_mix)[l], 8) for l in range(DEPTH)], axis=1))
    gmlpT = np.ascontiguousarray(np.stack([_fm(np.asarray(norm_mlp)[l], 8) for l in range(DEPTH)], axis=1))
    shared = {
        "ident": ident,
        "w_ada": np.ascontiguousarray(np.asarray(w_ada, f32)),
        "b_adaT": b_adaT, "gmixT": gmixT, "gmlpT": gmlpT,
        "w_mlp_in": np.ascontiguousarray(np.asarray(w_mlp_in, f32)),
        "w_mlp_out": np.ascontiguousarray(np.asarray(w_mlp_out, f32)),
    }
    inv = (10000.0 ** (-np.arange(0, 64, 2, dtype=f32) / f32(64))).astype(f32)
    row = np.repeat(np.arange(16, dtype=f32), 64)
    col = np.tile(np.arange(64, dtype=f32), 16)
    ang = np.concatenate([row[:, None] * inv, col[:, None] * inv], axis=-1).astype(f32)
    tok_pm = lambda a: np.ascontiguousarray(a.reshape(8, 128, -1).transpose(1, 0, 2))
    cos_s, sin_s = tok_pm(np.cos(ang).astype(f32)), tok_pm(np.sin(ang).astype(f32))
    cos_p, sin_p = np.ones_like(cos_s), np.zeros_like(sin_s)
    mask_s = np.zeros((128, 48), f32)
    mask_p = np.full((128, 48), -30000.0, f32)
    for kc in range(8):
        mask_p[:, kc * 4 + kc // 2] = 0.0
    gqk = np.concatenate([np.tile(np.asarray(q_norm, f32).reshape(1, 128), (1, 8)),
                          np.tile(np.asarray(k_norm, f32).reshape(1, 128), (1, 2))], axis=1)
    shared.update({
        "w_qkv": np.ascontiguousarray(np.asarray(w_qkv_o, f32)[0]),
        "w_o": np.ascontiguousarray(np.asarray(w_o_o, f32)[0]),
        "gqk": np.ascontiguousarray(np.tile(gqk, (128, 1))),
    })
    wg2 = np.zeros((32, 2, 256), f32)
    for d in range(2):
        wg2[d * 16:(d + 1) * 16, d, :] = np.asarray(gla_w_gate2, f32)[0, d]
    bg = np.asarray(gla_b_gate, f32)[0]
    bgT = np.ascontiguousarray(bg.reshape(2, 2, 128).transpose(2, 0, 1).reshape(128, 4))
    ii = np.arange(128)
    shared.update({
        "w_in": np.ascontiguousarray(np.asarray(w_in_e, f32)[0]),
        "w_out": np.ascontiguousarray(np.asarray(w_out_e, f32)[0]),
        "wg2": wg2, "bgT": bgT,
        "gnorm": np.ascontiguousarray(np.asarray(gla_norm, f32)[0].reshape(128, 1)),
        "tri2": np.ascontiguousarray(np.concatenate([(ii[:, None] <= ii[None, :]).astype(f32),
                                                      (ii[:, None] >= ii[None, :]).astype(f32)], axis=1)),
    })
    def p_lay(a):
        a = np.asarray(a, f32)
        dd = a.shape[0]
        return np.ascontiguousarray(a.reshape(dd, 2, 16, 64).transpose(1, 3, 0, 2).reshape(128, dd, 16))
    lamP = np.ascontiguousarray(np.stack([p_lay(np.asarray(s5_lambda_re)[0]), p_lay(np.asarray(s5_lambda_im)[0])], axis=1))
    ldt = np.asarray(s5_log_dt, f32)[0]
    ldtP = np.ascontiguousarray(np.broadcast_to(ldt.reshape(2, 2, 1, 16).transpose(1, 2, 0, 3), (2, 64, 2, 16)).reshape(128, 2, 16))
    def bp_lay(a):
        return np.asarray(a, f32).reshape(2, 16, 64, 16).transpose(0, 2, 1, 3).reshape(128, 16, 16)
    def cp_lay(a):
        return np.asarray(a, f32).reshape(2, 16, 16, 64).transpose(0, 3, 1, 2).reshape(128, 16, 16)
    BPh = np.ascontiguousarray(np.stack([bp_lay(np.asarray(s5_b_re)[0]), bp_lay(np.asarray(s5_b_im)[0])], axis=1))
    CPh = np.ascontiguousarray(np.stack([cp_lay(np.asarray(s5_c_re)[0]), cp_lay(np.asarray(s5_c_im)[0])], axis=1))
    sc_i = np.arange(128)
    s_of, c_of = sc_i // 16, sc_i % 16
    dS = np.ascontiguousarray(np.asarray(s5_d, f32)[0].reshape(32, 16)[:, c_of].T)
    Mf = (s_of[None, :] >= s_of[:, None]).astype(f32)
    Mb = (s_of[:, None] >= s_of[None, :]).astype(f32)
    E8 = np.zeros((128, 8, 240), f32)
    for a_ in range(8):
        for cc_ in range(16):
            E8[a_ * 16 + cc_, a_, cc_ + 112] = 1.0
    s5in = np.concatenate([lamP.reshape(128, 64), ldtP.reshape(128, 32), BPh.reshape(128, 512), CPh.reshape(128, 512),
                           np.tile(np.arange(-7, 9, dtype=f32)[None, :], (128, 1)), Mf, Mb], axis=1).astype(f32)
    shared.update({
        "s5in": np.ascontiguousarray(s5in),
        "jv": np.ascontiguousarray(np.tile(np.arange(128, dtype=f32)[None, :], (128, 1))),
        "dS": dS,
        "E8": np.ascontiguousarray(E8.reshape(128, 8 * 240)),
        "w_glu": np.ascontiguousarray(np.asarray(s5_w_glu, f32)[0]),
        "bgluT": _fm(np.asarray(s5_b_glu)[0], 4),
    })
    state_s5_re = np.asarray(state_s5_re, f32)
    state_s5_im = np.asarray(state_s5_im, f32)
    state_gla = np.asarray(state_gla, f32)
    cache_k = np.asarray(cache_k, f32)
    cache_v = np.asarray(cache_v, f32)
    in_maps = []
    for core in range(8):
        m = dict(shared)
        if core < 4:
            m["x"] = np.ascontiguousarray(x_prompt[4 * core:4 * core + 4].reshape(NT, D))
            m["condT"] = _fm(c_ctx, 8)
            m["cache_k"] = np.zeros((512, 256), f32)
            m["cache_v"] = np.zeros((512, 256), f32)
            m["maskb"], m["ropecos"], m["ropesin"] = mask_p, cos_p, sin_p
            m["cm"] = np.zeros((128, 1), f32)
            m["initP"] = np.zeros((128, 2, 2, 16), f32)
            m["gla_init"] = np.zeros((2, 256, 128), f32)
        else:
            b = core - 4
            m["x"] = np.ascontiguousarray(x_sample[b])
            m["condT"] = _fm(np.asarray(c)[b], 8)
            m["cache_k"] = np.ascontiguousarray(cache_k[b, 0].reshape(512, 256))
            m["cache_v"] = np.ascontiguousarray(cache_v[b, 0].reshape(512, 256))
            m["maskb"], m["ropecos"], m["ropesin"] = mask_s, cos_s, sin_s
            m["cm"] = np.ones((128, 1), f32)
            m["initP"] = np.ascontiguousarray(np.stack([p_lay(state_s5_re[b, 0]), p_lay(state_s5_im[b, 0])], axis=1))
            m["gla_init"] = np.ascontiguousarray(state_gla[b, 0].reshape(2, 256, 128))
        in_maps.append(m)
    res = run_bass_kernel_spmd(nc, in_maps, core_ids=list(range(8)))
    r = res.results
    y_prompt = np.concatenate([r[i]["y"].reshape(4, 256, D) for i in range(4)], axis=0)
    y_sample = np.stack([r[4 + i]["y"] for i in range(4)], axis=0)
    new_k = np.concatenate([r[i]["k_out"].reshape(4, 1, 256, 2, 128) for i in range(4)], axis=0)
    new_v = np.concatenate([r[i]["v_out"].reshape(4, 1, 256, 2, 128) for i in range(4)], axis=0)
    new_gla = np.concatenate([r[i]["gla_out"].reshape(4, 1, 2, 4, 64, 128) for i in range(4)], axis=0)
    new_re = np.concatenate([r[i]["s5re_out"].reshape(4, 1, 2, 32, 64) for i in range(4)], axis=0)
    new_im = np.concatenate([r[i]["s5im_out"].reshape(4, 1, 2, 32, 64) for i in range(4)], axis=0)
    return (y_prompt, y_sample, new_re, new_im, new_gla, new_k, new_v)
```

```python
import os
import contextlib
import numpy as np
import concourse.bass as bass
import concourse.mybir as mybir
from concourse.bass_utils import run_bass_kernel_spmd

F32 = mybir.dt.float32
BF16 = mybir.dt.bfloat16
I32 = mybir.dt.int32
ALU = mybir.AluOpType
AF = mybir.ActivationFunctionType
AX = mybir.AxisListType

D = 1024
NT = 1024
DFF = 4096
EPS = 1e-6
DEPTH = 2
KDMA = 32
NSLOT = 3
STRICT_SAME_ENGINE = True

DBG_SKIP_EVEN = int(os.environ.get("SKIP_EVEN", "0"))
DBG_SKIP_ODD = int(os.environ.get("SKIP_ODD", "0"))


class Prog:
    ENGS = ("pe", "act", "dve", "pool", "sp")

    def __init__(self):
        self.ops = []
        self.streams = {e: [] for e in self.ENGS}
        self.last_writer = {}
        self.readers = {}
        self.ndma = {e: 0 for e in self.ENGS}

    def add(self, eng, fn, reads=(), writes=(), dma=False):
        oid = len(self.ops)
        deps = set()
        pr = [r for r in reads if isinstance(r, tuple) and r[0] in ("pf", "pb")]
        if pr:
            writes = list(writes) + [r for r in pr if r not in writes]
        for r in reads:
            lw = self.last_writer.get(r)
            if lw is not None:
                deps.add(lw)
        for w in writes:
            lw = self.last_writer.get(w)
            if lw is not None:
                deps.add(lw)
            for rd in self.readers.get(w, {}).values():
                deps.update(rd)
        deps.discard(oid)
        op = dict(id=oid, eng=eng, fn=fn, dma=dma, signal=False, pos=len(self.streams[eng]))
        if dma:
            op["dslot"] = self.ndma[eng] % KDMA
            op["dval"] = 16 * (self.ndma[eng] // KDMA + 1)
            self.ndma[eng] += 1
        keep = set()
        for d in deps:
            a = self.ops[d]
            if (not a["dma"]) and (not dma) and a["eng"] == eng:
                if eng == "pe":
                    continue
                if (not STRICT_SAME_ENGINE) and op["pos"] - a["pos"] > 2:
                    continue
            keep.add(d)
        op["deps"] = keep
        for d in keep:
            if not self.ops[d]["dma"]:
                self.ops[d]["signal"] = True
        self.ops.append(op)
        self.streams[eng].append(oid)
        for r in reads:
            rr = self.readers.setdefault(r, {})
            if dma:
                rr.setdefault("dma", []).append(oid)
            else:
                rr[eng] = [oid]
        for w in writes:
            self.last_writer[w] = oid
            self.readers[w] = {}
        return oid

    def emit(self, nc, final_wait_ops=()):
        for oid in final_wait_ops:
            if not self.ops[oid]["dma"]:
                self.ops[oid]["signal"] = True
        for e in self.ENGS:
            cnt = 0
            for oid in self.streams[e]:
                op = self.ops[oid]
                if (not op["dma"]) and op["signal"]:
                    cnt += 1
                    op["sval"] = cnt
        with contextlib.ExitStack() as st:
            esem = {e: st.enter_context(nc.semaphore("s_" + e)) for e in self.ENGS}
            dsem = {e: [st.enter_context(nc.semaphore("d_%s%d" % (e, i))) for i in range(KDMA)]
                    for e in ("sp", "pool", "act")}
            block = st.enter_context(nc.Block())
            ops = self.ops

            def run_stream(e, h):
                seen = {}
                for oid in self.streams[e]:
                    op = ops[oid]
                    need = {}
                    for d in op["deps"]:
                        a = ops[d]
                        if a["dma"]:
                            s = dsem[a["eng"]][a["dslot"]]
                            v = a["dval"]
                        else:
                            s = esem[a["eng"]]
                            v = a["sval"]
                        k = id(s)
                        if k not in need or need[k][1] < v:
                            need[k] = (s, v)
                    if op["dma"]:
                        s = dsem[e][op["dslot"]]
                        v = op["dval"] - 16
                        if v > 0:
                            k = id(s)
                            if k not in need or need[k][1] < v:
                                need[k] = (s, v)
                    for k, (s, v) in need.items():
                        if seen.get(k, 0) >= v:
                            continue
                        seen[k] = v
                        h.wait_ge(s, v)
                    ins = op["fn"](h)
                    if op["dma"]:
                        ins.then_inc(dsem[e][op["dslot"]], 16)
                    elif op["signal"]:
                        ins.then_inc(esem[e], 1)
                if e == "sp":
                    for oid in final_wait_ops:
                        a = ops[oid]
                        if a["dma"]:
                            h.wait_ge(dsem[a["eng"]][a["dslot"]], a["dval"])
                        else:
                            h.wait_ge(esem[a["eng"]], a["sval"])

            @block.sync
            def _(h):
                run_stream("sp", h)

            @block.gpsimd
            def _(h):
                run_stream("pool", h)

            @block.scalar
            def _(h):
                run_stream("act", h)

            @block.vector
            def _(h):
                run_stream("dve", h)

            @block.tensor
            def _(h):
                run_stream("pe", h)


def keys(name, *dims):
    out = [(name,)]
    for d in dims:
        if isinstance(d, int):
            d = [d]
        out = [k + (i,) for k in out for i in d]
    return out


class Builder:
    def __init__(self, nc, st):
        self.nc = nc
        self.st = st
        self.P = Prog()
        self.fin = []

    def sb(self, name, shape, dt=F32):
        return self.st.enter_context(self.nc.sbuf_tensor(name, shape, dt))

    def psum(self, name, shape, dt=F32):
        return self.st.enter_context(self.nc.psum_tensor(name, shape, dt))

    def dram_in(self, name, shape, dt=F32):
        return self.nc.dram_tensor(name, list(shape), dt, kind="ExternalInput").ap()

    def dram_out(self, name, shape, dt=F32):
        return self.nc.dram_tensor(name, list(shape), dt, kind="ExternalOutput").ap()

    def mm(self, out, lhsT, rhs, start, stop, reads, writes):
        self.P.add("pe", lambda h: h.matmul(out, lhsT=lhsT, rhs=rhs, start=start, stop=stop), reads, writes)

    def tr(self, out, in_, ident, reads, writes):
        self.P.add("pe", lambda h: h.transpose(out=out, in_=in_, identity=ident), reads, writes)

    def act(self, out, in_, func, reads, writes, scale=None, bias=None):
        kw = {}
        if scale is not None:
            kw["scale"] = scale
        if bias is not None:
            kw["bias"] = bias
        self.P.add("act", lambda h: h.activation(out=out, in_=in_, func=func, **kw), reads, writes)

    def tt(self, out, in0, in1, op, reads, writes, eng="dve"):
        self.P.add(eng, lambda h: h.tensor_tensor(out=out, in0=in0, in1=in1, op=op), reads, writes)

    def ts(self, out, in0, s1, op0, reads, writes, s2=None, op1=None, eng="dve"):
        if op1 is None:
            self.P.add(eng, lambda h: h.tensor_single_scalar(out=out, in_=in0, scalar=s1, op=op0), reads, writes)
        else:
            self.P.add(eng, lambda h: h.tensor_scalar(out=out, in0=in0, scalar1=s1, scalar2=s2, op0=op0, op1=op1),
                       reads, writes)

    def stt(self, out, in0, scalar, in1, op0, op1, reads, writes, eng="dve"):
        self.P.add(eng, lambda h: h.scalar_tensor_tensor(out=out, in0=in0, scalar=scalar, in1=in1, op0=op0, op1=op1),
                   reads, writes)

    def copy(self, out, in_, reads, writes, eng="dve"):
        if eng == "act":
            self.P.add("act", lambda h: h.activation(out=out, in_=in_, func=AF.Copy), reads, writes)
        else:
            self.P.add(eng, lambda h: h.tensor_copy(out=out, in_=in_), reads, writes)

    def memset(self, out, val, writes, eng="dve"):
        self.P.add(eng, lambda h: h.memset(out, val), (), writes)

    def recip(self, out, in_, reads, writes):
        self.P.add("dve", lambda h: h.reciprocal(out=out, in_=in_), reads, writes)

    def dma(self, q, out, in_, reads, writes, final=False):
        oid = self.P.add(q, lambda h: h.dma_start(out=out, in_=in_), reads, writes, dma=True)
        if final:
            self.fin.append(oid)
        return oid


class WStream:
    def __init__(self, B, slots):
        self.B = B
        self.slots = slots
        self.loads = []
        self.recorded = 0
        self.slot_of = {}
        self.pinned = set()
        self.rr = 0
        self.occ = {}

    def plan(self, view_fn, src):
        self.loads.append((view_fn, src))
        return len(self.loads) - 1

    def pin(self, i):
        self.pinned.add(self.slot_of[i])

    def unpin(self, i):
        self.pinned.discard(self.slot_of[i])

    def acquire(self, i):
        while self.recorded < len(self.loads):
            k = self.recorded
            rr = self.rr
            while rr % NSLOT in self.pinned:
                rr += 1
            sidx = rr % NSLOT
            prev = self.occ.get(sidx, -1)
            if k > i and prev >= i:
                break
            assert prev < i or prev < 0, (k, i, prev)
            self.rr = rr + 1
            view_fn, src = self.loads[k]
            self.B.dma("pool", view_fn(self.slots[sidx]), src, (), [("w", sidx)])
            self.slot_of[k] = sidx
            self.occ[sidx] = k
            self.recorded += 1
        sidx = self.slot_of[i]
        return self.slots[sidx], ("w", sidx)


def build_program():
    nc = bass.Bass("TRN2", target_bir_lowering=False)
    with contextlib.ExitStack() as st:
        B = Builder(nc, st)
        P = B.P
        x_d = B.dram_in("x", [NT, D])
        cond_d = B.dram_in("condT", [128, 8])
        ident_d = B.dram_in("ident", [128, 128])
        w_ada_d = B.dram_in("w_ada", [DEPTH, D, 6 * D])
        b_ada_d = B.dram_in("b_adaT", [128, DEPTH, 48])
        gmix_d = B.dram_in("gmixT", [128, DEPTH, 8])
        gmlp_d = B.dram_in("gmlpT", [128, DEPTH, 8])
        w1_d = B.dram_in("w_mlp_in", [DEPTH, D, DFF])
        w2_d = B.dram_in("w_mlp_out", [DEPTH, DFF, D])
        y_d = B.dram_out("y", [NT, D])
        win_d = B.dram_in("w_in", [D, 2080])
        wout_d = B.dram_in("w_out", [D, D])
        wg2_d = B.dram_in("wg2", [32, 2, 256])
        bgT_d = B.dram_in("bgT", [128, 4])
        gnorm_d = B.dram_in("gnorm", [128, 1])
        tri2_d = B.dram_in("tri2", [128, 256])
        cm_d = B.dram_in("cm", [128, 1])
        glainit_d = B.dram_in("gla_init", [2, 256, 128])
        glaout_d = B.dram_out("gla_out", [4, 2, 256, 128])
        s5in_d = B.dram_in("s5in", [128, 1392])
        initP_d = B.dram_in("initP", [128, 2, 2, 16])
        jv_d = B.dram_in("jv", [128, 128])
        dS_d = B.dram_in("dS", [128, 32])
        E8_d = B.dram_in("E8", [128, 8 * 240])
        wglu_d = B.dram_in("w_glu", [512, 512])
        bglu_d = B.dram_in("bgluT", [128, 4])
        s5re_d = B.dram_out("s5re_out", [4, 2, 32, 64])
        s5im_d = B.dram_out("s5im_out", [4, 2, 32, 64])
        wqkv_d = B.dram_in("w_qkv", [D, 1536])
        wo_d = B.dram_in("w_o", [D, D])
        gqk_d = B.dram_in("gqk", [128, 1280])
        ck_d = B.dram_in("cache_k", [512, 256])
        cv_d = B.dram_in("cache_v", [512, 256])
        maskb_d = B.dram_in("maskb", [128, 48])
        cos_d = B.dram_in("ropecos", [128, 8, 64])
        sin_d = B.dram_in("ropesin", [128, 8, 64])
        kout_d = B.dram_out("k_out", [NT, 256])
        vout_d = B.dram_out("v_out", [NT, 256])

        yT = B.sb("yT", [128, 8, NT], F32)
        hT = B.sb("hT", [128, 8, NT], BF16)
        big = B.sb("big", [128, 32 * NT], BF16)
        slots = [B.sb("wslot%d" % i, [128, 8192], BF16) for i in range(NSLOT)]
        xin = [B.sb("xin%d" % i, [128, D], F32) for i in range(2)]
        sq = B.sb("sq", [128, 8, 512], BF16)
        sd = B.sb("sd", [128, 512], F32)
        rstd = B.sb("rstd", [128, 512], F32)
        tmpn = [B.sb("tmpn%d" % i, [128, 512], F32) for i in range(2)]
        rl = [B.sb("rl%d" % i, [128, 512], BF16) for i in range(2)]
        ident = B.sb("ident_sb", [128, 128], F32)
        identb = B.sb("identb", [128, 128], BF16)
        onesd = B.sb("onesd", [128, 128], BF16)
        epsc = B.sb("epsc", [128, 1], F32)
        condT = B.sb("condT_sb", [128, 8], F32)
        silc = B.sb("silc", [128, 8], BF16)
        b_adaT = B.sb("b_adaT_sb", [128, DEPTH, 48], F32)
        gmixT = B.sb("gmixT_sb", [128, DEPTH, 8], F32)
        gmlpT = B.sb("gmlpT_sb", [128, DEPTH, 8], F32)
        adaT = B.sb("adaT", [128, DEPTH, 48], F32)
        A1 = B.sb("A1", [128, DEPTH, 8], F32)
        A2 = B.sb("A2", [128, DEPTH, 8], F32)

        onesb = B.sb("onesb", [128, 128], BF16)
        onesdv = B.sb("onesdv", [128, 128], BF16)
        onec = B.sb("onec", [128, 1], F32)
        halfpi = B.sb("halfpi", [128, 1], F32)
        wg2 = B.sb("wg2b", [32, 512], BF16)
        nbg = B.sb("nbg", [128, 4], F32)
        gnorm = B.sb("gnorm_sb", [128, 1], F32)
        tri2 = B.sb("tri2_sb", [128, 256], F32)
        cm = B.sb("cm_sb", [128, 1], F32)
        m01 = B.sb("m01", [128, NT], F32)
        cosT = m01[:, 0:512].rearrange("p (i f) -> p i f", i=8)
        sinT = m01[:, 512:1024].rearrange("p (i f) -> p i f", i=8)
        rho8 = B.sb("rho8", [128, 2, 16], F32)
        th8 = B.sb("th8", [128, 2, 16], F32)
        L8r = B.sb("L8r", [128, 2, 16], F32)
        L8i = B.sb("L8i", [128, 2, 16], F32)
        LIr = B.sb("LIr", [128, 2, 16], F32)
        LIi = B.sb("LIi", [128, 2, 16], F32)
        LIt = B.sb("LIt", [128, 16], F32)
        th8n = B.sb("th8n", [128, 2, 16], F32)
        itile = B.sb("itile", [128, 512], I32)
        initP = B.sb("initP_sb", [128, 2, 2, 16], F32)
        smf = B.sb("smf", [128, 128], F32)
        smb = B.sb("smb", [128, 128], F32)
        jv = B.sb("jv_sb", [128, 128], F32)
        Hfin = B.sb("Hfin", [128, 2, 128], F32)
        P0r = B.sb("P0r", [128, 16], F32)
        P0i = B.sb("P0i", [128, 16], F32)
        bglu = B.sb("bglu_sb", [128, 4], F32)
        dS = B.sb("dS_sb", [128, 32], F32)
        E8 = B.sb("E8_sb", [128, 8 * 240], BF16)
        ebl = B.sb("ebl", [128, 4, 8], F32)
        Sst = [B.sb("Sst%d" % i, [128, 128], F32) for i in range(2)]
        Stmp = B.sb("Stmp", [128, 128], F32)
        maskb = B.sb("maskb_sb", [128, 48], F32)

        class Arena:
            def __init__(self):
                self.off = 0

            def reset(self):
                self.off = 0

            def alloc(self, nelem, dt):
                nb = nelem * (4 if dt == F32 else 2)
                nb = (nb + 63) // 64 * 64
                o = self.off
                self.off += nb
                assert self.off <= 65536, self.off
                v = big[:, o // 2:(o + nb) // 2]
                if dt == F32:
                    v = v.bitcast(F32)
                return v[:, 0:nelem]

        arena = Arena()
        BIGK = keys("big", range(32), range(2))

        ps2 = [B.psum("ps2_%d" % i, [128, 1024], F32) for i in range(2)]
        pf = [ps2[0][:, 0:512], ps2[0][:, 512:1024], ps2[1][:, 0:512], ps2[1][:, 512:1024]] + \
             [B.psum("pf%d" % i, [128, 512], F32)[:, :] for i in (4, 5)]
        pb = [B.psum("pb%d" % i, [128, 1024], BF16) for i in range(2)]
        pbf = [pb[i][:, :].bitcast(F32) for i in range(2)]

        ws = WStream(B, slots)

        def v_kc(ncols):
            return lambda slot: slot[:, 0:8 * ncols].rearrange("p (kc n) -> p kc n", kc=8)

        LD = {}

        def plan_ada(l):
            for j in range(6):
                LD["ada", l, j] = ws.plan(v_kc(1024),
                                          w_ada_d[l, :, j * 1024:(j + 1) * 1024].rearrange("(kc p) n -> p kc n", p=128))

        def plan_mlp(l):
            for b in range(4):
                LD["w1", l, b] = ws.plan(v_kc(1024),
                                         w1_d[l, :, b * 1024:(b + 1) * 1024].rearrange("(kc p) n -> p kc n", p=128))
            for b in range(4):
                LD["w2", l, b] = ws.plan(lambda slot: slot[:, 0:8192].rearrange("p (fc n) -> p fc n", fc=32),
                                         w2_d[l, :, b * 256:(b + 1) * 256].rearrange("(fc p) n -> p fc n", p=128))

        if not DBG_SKIP_EVEN:
            LD["s5in"] = ws.plan(lambda slot: slot[:, 0:2784].bitcast(F32), s5in_d[:, :])
        plan_ada(0)
        if not DBG_SKIP_EVEN:
            LD["winU"] = ws.plan(v_kc(512), win_d[:, 0:512].rearrange("(kc p) n -> p kc n", p=128))
            plan_ada(1)
            LD["wglu"] = ws.plan(lambda slot: slot[:, 0:2048].rearrange("p (kc n) -> p kc n", kc=4),
                                 wglu_d[:, :].rearrange("(kc p) n -> p kc n", p=128))
            LD["winQK"] = ws.plan(v_kc(512), win_d[:, 512:1024].rearrange("(kc p) n -> p kc n", p=128))
            LD["winVR"] = ws.plan(v_kc(1024), win_d[:, 1024:2048].rearrange("(kc p) n -> p kc n", p=128))
            LD["winG"] = ws.plan(v_kc(32), win_d[:, 2048:2080].rearrange("(kc p) n -> p kc n", p=128))
            LD["wout"] = ws.plan(v_kc(1024), wout_d[:, :].rearrange("(kc p) n -> p kc n", p=128))
        plan_mlp(0)
        if DBG_SKIP_EVEN:
            plan_ada(1)
        LD["qkvA"] = ws.plan(v_kc(1024), wqkv_d[:, 0:1024].rearrange("(kc p) n -> p kc n", p=128))
        LD["qkvB"] = ws.plan(v_kc(512), wqkv_d[:, 1024:1536].rearrange("(kc p) n -> p kc n", p=128))
        LD["wo"] = ws.plan(v_kc(1024), wo_d[:, :].rearrange("(kc p) n -> p kc n", p=128))
        plan_mlp(1)

        B.dma("sp", maskb[:], maskb_d[:, :], (), ["maskb"])
        B.memset(onesb[:], 1.0, ["onesb"])
        B.memset(onesdv[:], 1.0 / 128.0, ["onesdv"])
        B.memset(onec[:], 1.0, ["onec"])
        B.memset(halfpi[:], float(np.pi / 2) * 0.99999, ["halfpi"])
        B.dma("pool", wg2[:], wg2_d[:, :, :].rearrange("r d c -> r (d c)"), (), ["wg2"])
        B.dma("sp", nbg[:], bgT_d[:, :], (), ["nbg"])
        B.ts(nbg[:], nbg[:], -1.0, ALU.mult, ["nbg"], ["nbg"])
        B.dma("sp", gnorm[:], gnorm_d[:, :], (), ["gnorm"])
        B.dma("sp", tri2[:], tri2_d[:, :], (), ["tri2"])
        B.dma("sp", cm[:], cm_d[:, :], (), ["cm"])
        B.dma("sp", initP[:], initP_d[:, :, :, :], (), ["initP"])
        B.dma("sp", jv[:], jv_d[:, :], (), ["jv"])
        B.dma("sp", bglu[:], bglu_d[:, :], (), ["bglu"])
        B.dma("sp", dS[:], dS_d[:, :], (), ["dS"])
        B.dma("pool", E8[:], E8_d[:, :], (), ["E8"])
        B.memset(smf[:], 1.0, ["smf"])
        B.memset(smb[:], 1.0, ["smb"])
        B.copy(smf[:, 32:128:32], cm[:, 0:1].to_broadcast([128, 3]), ["cm", "smf"], ["smf"])
        B.copy(smb[:, 31:127:32], cm[:, 0:1].to_broadcast([128, 3]), ["cm", "smb"], ["smb"])
        B.memset(smf[:, 0:1], 0.0, ["smf"])
        B.memset(smb[:, 127:128], 0.0, ["smb"])
        B.dma("sp", ident[:], ident_d[:, :], (), ["ident"])
        B.dma("sp", condT[:], cond_d[:, :], (), ["condT"])
        B.dma("sp", b_adaT[:], b_ada_d[:, :, :], (), ["b_adaT"])
        B.dma("sp", gmixT[:], gmix_d[:, :, :], (), ["gmixT"])
        B.dma("sp", gmlpT[:], gmlp_d[:, :, :], (), ["gmlpT"])
        B.copy(identb[:], ident[:], ["ident"], ["identb"])
        B.memset(onesd[:], 1.0 / 1024.0, ["onesd"])
        B.memset(epsc[:], EPS, ["epsc"])
        B.act(silc[:], condT[:], AF.Silu, ["condT"], ["silc"])

        def yk(cs, tiles):
            return keys("yT", cs, tiles)

        def input_transposes():
            for i in range(8):
                xb_ = xin[i % 2]
                xk = ("xin", i % 2)
                B.dma("sp", xb_[:], x_d[i * 128:(i + 1) * 128, :], (), [xk])
                for half in range(2):
                    ps = pf[half]
                    pk = ("pf", half)
                    for cc in range(4):
                        c = half * 4 + cc
                        B.tr(ps[:, cc * 128:(cc + 1) * 128], xb_[:, c * 128:(c + 1) * 128], ident[:],
                             [xk, "ident"], [pk])
                    B.copy(yT[:, half * 4:half * 4 + 4, i * 128:(i + 1) * 128],
                           ps[:, :].rearrange("p (c t) -> p c t", c=4), [pk], yk(range(half * 4, half * 4 + 4), i),
                           eng="act")


        def ada_step(l, j):
            aps = pf[5]
            slot, sk = ws.acquire(LD["ada", l, j])
            for c in range(8):
                col = j * 8 + c
                for kc in range(8):
                    B.mm(aps[:, col:col + 1], slot[:, kc * 1024 + c * 128: kc * 1024 + (c + 1) * 128],
                         silc[:, kc:kc + 1], kc == 0, kc == 7, [sk, "silc"], [("pf", 5)])

        def ada_finish(l):
            B.tt(adaT[:, l, :], pf[5][:, 0:48], b_adaT[:, l, :], ALU.add, [("pf", 5), "b_adaT"], [("ada", l)])
            B.stt(A1[:, l, :], adaT[:, l, 8:16], 1.0, gmixT[:, l, :], ALU.add, ALU.mult,
                  [("ada", l), "gmixT"], [("A1", l)])
            B.stt(A2[:, l, :], adaT[:, l, 32:40], 1.0, gmlpT[:, l, :], ALU.add, ALU.mult,
                  [("ada", l), "gmlpT"], [("A2", l)])

        def ada(l):
            for j in range(6):
                ada_step(l, j)
            ada_finish(l)

        def norm_mod(l, which):
            Amat = A1 if which == 0 else A2
            sh0 = 0 if which == 0 else 24
            ak = ("A1", l) if which == 0 else ("A2", l)
            for n in range(2):
                tl = range(4 * n, 4 * n + 4)
                for c in range(8):
                    B.act(sq[:, c, :], yT[:, c, n * 512:(n + 1) * 512], AF.Square, yk(c, tl), [("sq", c)])
                for c in range(8):
                    B.mm(pf[2][:, :], onesd[:], sq[:, c, :], c == 0, c == 7, ["onesd", ("sq", c)], [("pf", 2)])
                B.act(sd[:], pf[2][:, :], AF.Ln, [("pf", 2), "epsc"], ["sd"], bias=epsc[:, 0:1])
                B.act(rstd[:], sd[:], AF.Exp, ["sd"], ["rstd"], scale=-0.5)
                for c in range(8):
                    t_ = tmpn[c % 2]
                    tk = ("tmpn", c % 2)
                    B.tt(t_[:], yT[:, c, n * 512:(n + 1) * 512], rstd[:], ALU.mult, yk(c, tl) + ["rstd"], [tk])
                    B.act(hT[:, c, n * 512:(n + 1) * 512], t_[:], AF.Identity, [tk, ak, ("ada", l)],
                          [("hT", c, n)], scale=Amat[:, l, c:c + 1], bias=adaT[:, l, sh0 + c:sh0 + c + 1])

        def mlp(l):
            h1 = big[:, :].rearrange("p (f t) -> p f t", f=32)
            cnt = 0
            for b in range(4):
                slot, sk = ws.acquire(LD["w1", l, b])
                for fl in range(8):
                    f = b * 8 + fl
                    for n in range(2):
                        pi = cnt % 2
                        cnt += 1
                        ps = pf[pi]
                        for kc in range(8):
                            B.mm(ps[:, :], slot[:, kc * 1024 + fl * 128: kc * 1024 + (fl + 1) * 128],
                                 hT[:, kc, n * 512:(n + 1) * 512], kc == 0, kc == 7,
                                 [sk, ("hT", kc, n)], [("pf", pi)])
                        B.act(rl[pi][:], ps[:, :], AF.Relu, [("pf", pi)], [("rl", pi)])
                        B.tt(h1[:, f, n * 512:(n + 1) * 512], rl[pi][:], rl[pi][:], ALU.mult,
                             [("rl", pi)], [("big", f, n)])
            for b in range(4):
                slot, sk = ws.acquire(LD["w2", l, b])
                for cl in range(2):
                    c = 2 * b + cl
                    for n in range(2):
                        pi = cnt % 2
                        cnt += 1
                        ps = pf[pi]
                        for fc in range(32):
                            B.mm(ps[:, :], slot[:, fc * 256 + cl * 128: fc * 256 + (cl + 1) * 128],
                                 h1[:, fc, n * 512:(n + 1) * 512], fc == 0, fc == 31,
                                 [sk, ("big", fc, n)], [("pf", pi)])
                        tl = range(4 * n, 4 * n + 4)
                        B.stt(yT[:, c, n * 512:(n + 1) * 512], ps[:, :], adaT[:, l, 40 + c:41 + c],
                              yT[:, c, n * 512:(n + 1) * 512], ALU.mult, ALU.add,
                              [("pf", pi), ("ada", l)] + yk(c, tl), yk(c, tl))


        def arena_barrier(old, new):
            B.memset(epsc[:], EPS, ["epsc"] + list(old) + list(new))

        def attention(l):
            arena.reset()
            qT = arena.alloc(8 * NT, BF16).rearrange("p (h t) -> p h t", h=8)
            kT = arena.alloc(2 * 1536, BF16).rearrange("p (h t) -> p h t", h=2)
            vtok = arena.alloc(12 * 256, BF16).rearrange("p (i f) -> p i f", i=12)
            qkv_tok = arena.alloc(1536, F32)
            qkn = arena.alloc(1280, F32).rearrange("p (h d) -> p h d", h=10)
            rt01 = arena.alloc(1280, F32)
            sqh = rt01.rearrange("p (h d) -> p h d", h=10)
            rt = [rt01[:, 0:640].rearrange("p (h d) -> p h d", h=10), rt01[:, 640:1280].rearrange("p (h d) -> p h d", h=10)] + \
                 [arena.alloc(640, F32).rearrange("p (h d) -> p h d", h=10) for _ in range(2)]
            gqk = arena.alloc(1280, F32).rearrange("p (h d) -> p h d", h=10)
            qr = arena.alloc(1280, BF16).rearrange("p (h d) -> p h d", h=10)
            PTall = arena.alloc(3072, BF16)
            PTa = PTall[:, 0:1024].rearrange("p (k t) -> p k t", k=2)
            PTb = PTall[:, 1024:2048].rearrange("p (k t) -> p k t", k=2)
            PTc = PTall[:, 2048:3072].rearrange("p (k t) -> p k t", k=2)
            sqflat = sq[:, :, :].rearrange("p c t -> p (c t)")
            qkv_toks = [qkv_tok, PTall.bitcast(F32)]
            qkv_keys = [["qkv_tok"], ["PT0", "PT1", "PT2"]]
            qkns = [qkn, sqflat[:, 0:2560].bitcast(F32).rearrange("p (h d) -> p h d", h=10)]
            qkn_keys = [["qkn"], keys("sq", range(5))]
            qrs = [qr, sqflat[:, 2560:3840].rearrange("p (h d) -> p h d", h=10)]
            qr_keys = [["qr"], keys("sq", range(5, 8))]
            rden = rstd
            ssq = arena.alloc(16, F32)
            rsq = arena.alloc(16, F32)
            ck_tok = sq[:, :, :].rearrange("p c t -> p (c t)").bitcast(F32)[:, 0:1024].rearrange("p (i f) -> p i f", i=4)
            CKK = keys("sq", range(8))
            AK = ["qT", "kTl", "kTc", "vtokc", "qkv_tok", "gqk", "qkn", "rt0", "rt1", "rt2", "rt3", "qr",
                  "PT0", "PT1", "PT2", "rstd", "ssq", "rsq"] + keys("vtok", range(8)) + keys("qTi", range(8)) + keys("kTi", range(8))
            arena_barrier(BIGK, AK)
            attnT = hT
            B.dma("sp", gqk[:, :, :], gqk_d[:, :].rearrange("p (h d) -> p h d", h=10), (), ["gqk"])
            B.dma("sp", cosT, cos_d[:, :, :], (), ["m01"])
            B.dma("sp", sinT, sin_d[:, :, :], (), ["m01"])

            B.dma("sp", ck_tok[:, :, :], ck_d[:, :].rearrange("(i p) f -> p i f", p=128), (), CKK)
            B.dma("pool", vtok[:, 8:12, :], cv_d[:, :].rearrange("(i p) f -> p i f", p=128), (), ["vtokc"])
            for kvh in range(2):
                ps = pf[kvh]
                for i in range(4):
                    B.tr(ps[:, i * 128:(i + 1) * 128], ck_tok[:, i, kvh * 128:(kvh + 1) * 128], ident[:],
                         CKK + ["ident"], [("pf", kvh)])
                B.copy(kT[:, kvh, 1024:1536], ps[:, :], [("pf", kvh)], ["kTc"], eng="act")

            slotA, skA = ws.acquire(LD["qkvA"])
            ws.pin(LD["qkvA"])
            slotB, skB = ws.acquire(LD["qkvB"])
            def pa_proj(i):
                    tsl = slice(i * 128, (i + 1) * 128)
                    n = i // 4
                    bsel = i % 2
                    qkv_t, qkv_k = qkv_toks[bsel], qkv_keys[bsel]
                    qkn_t, qkn_k = qkns[bsel], qkn_keys[bsel]
                    qr_t, qr_k = qrs[bsel], qr_keys[bsel]
                    for cb in range(3):
                        ps = pf[2 + cb]
                        pk = ("pf", 2 + cb)
                        for kc in range(8):
                            if cb < 2:
                                rhs = slotA[:, kc * 1024 + cb * 512: kc * 1024 + (cb + 1) * 512]
                                sk = skA
                            else:
                                rhs = slotB[:, kc * 512:(kc + 1) * 512]
                                sk = skB
                            B.mm(ps[:, :], hT[:, kc, tsl], rhs, kc == 0, kc == 7, [("hT", kc, n), sk], [pk])

            def pa_evac(i):
                    tsl = slice(i * 128, (i + 1) * 128)
                    bsel = i % 2
                    qkv_t, qkv_k = qkv_toks[bsel], qkv_keys[bsel]
                    for cb in range(3):
                        B.copy(qkv_t[:, cb * 512:(cb + 1) * 512], pf[2 + cb][:, :], [("pf", 2 + cb)], qkv_k, eng="act")
                    B.dma("sp", vout_d[tsl, :], qkv_t[:, 1280:1536], qkv_k, (), final=True)
                    B.copy(vtok[:, i, :], qkv_t[:, 1280:1536], qkv_k, [("vtok", i)], eng="act")

            def pa_rest(i):
                    tsl = slice(i * 128, (i + 1) * 128)
                    n = i // 4
                    bsel = i % 2
                    qkv_t, qkv_k = qkv_toks[bsel], qkv_keys[bsel]
                    qkn_t, qkn_k = qkns[bsel], qkn_keys[bsel]
                    qr_t, qr_k = qrs[bsel], qr_keys[bsel]
                    xq = qkv_t[:, 0:1280].rearrange("p (h d) -> p h d", h=10)
                    B.tt(sqh[:, :, :], xq, xq, ALU.mult, qkv_k, ["rt0", "rt1"])
                    B.P.add("dve", lambda h, o=ssq[:, 0:10], a=sqh[:, :, :]: h.tensor_reduce(out=o, in_=a, axis=AX.X, op=ALU.add),
                            ["rt0", "rt1"], ["ssq"])
                    B.act(rsq[:, 0:10], ssq[:, 0:10], AF.Sqrt, ["ssq", "epsc"], ["rsq"], scale=1.0 / 128.0, bias=epsc[:, 0:1])

            def pa_tail(i):
                    tsl = slice(i * 128, (i + 1) * 128)
                    n = i // 4
                    bsel = i % 2
                    qkv_t, qkv_k = qkv_toks[bsel], qkv_keys[bsel]
                    qkn_t, qkn_k = qkns[bsel], qkn_keys[bsel]
                    qr_t, qr_k = qrs[bsel], qr_keys[bsel]
                    xq = qkv_t[:, 0:1280].rearrange("p (h d) -> p h d", h=10)
                    B.recip(ssq[:, 0:10], rsq[:, 0:10], ["rsq"], ["ssq"])
                    B.tt(qkn_t[:, :, :], xq, ssq[:, 0:10].unsqueeze(2).to_broadcast([128, 10, 128]), ALU.mult,
                         qkv_k + ["ssq"], qkn_k)
                    B.tt(qkn_t[:, :, :], qkn_t[:, :, :], gqk[:, :, :], ALU.mult, qkn_k + ["gqk"], qkn_k)
                    B.dma("sp", kout_d[tsl, :], qkn_t[:, 8:10, :], qkn_k, (), final=True)
                    x1 = qkn_t[:, :, 0::2]
                    x2 = qkn_t[:, :, 1::2]
                    cb_ = cosT[:, i, :].unsqueeze(1).to_broadcast([128, 10, 64])
                    sb_ = sinT[:, i, :].unsqueeze(1).to_broadcast([128, 10, 64])
                    B.tt(rt[0][:, :, :], x1, cb_, ALU.mult, qkn_k + ["m01"], ["rt0"])
                    B.tt(rt[1][:, :, :], x2, sb_, ALU.mult, qkn_k + ["m01"], ["rt1"])
                    B.tt(qr_t[:, :, 0::2], rt[0][:, :, :], rt[1][:, :, :], ALU.subtract, ["rt0", "rt1"], qr_k)
                    B.tt(rt[2][:, :, :], x1, sb_, ALU.mult, qkn_k + ["m01"], ["rt2"])
                    B.tt(rt[3][:, :, :], x2, cb_, ALU.mult, qkn_k + ["m01"], ["rt3"])
                    B.tt(qr_t[:, :, 1::2], rt[2][:, :, :], rt[3][:, :, :], ALU.add, ["rt2", "rt3"], qr_k)
                    for hh in range(8):
                        B.tr(pb[0][:, hh * 128:(hh + 1) * 128], qr_t[:, hh, :], identb[:], qr_k + ["identb"], [("pb", 0)])
                    for hh in range(2):
                        B.tr(pb[1][:, hh * 128:(hh + 1) * 128], qr_t[:, 8 + hh, :], identb[:], qr_k + ["identb"], [("pb", 1)])
                    B.copy(qT[:, :, tsl], pb[0][:, :].rearrange("p (h t) -> p h t", h=8), [("pb", 0)], [("qTi", i)], eng="act")
                    B.copy(kT[:, :, tsl], pb[1][:, 0:256].rearrange("p (h t) -> p h t", h=2), [("pb", 1)], [("kTi", i)], eng="act")

            pa_proj(0)
            pa_evac(0)
            for i in range(8):
                if i + 1 < 8:
                    pa_proj(i + 1)
                pa_rest(i)
                if i + 1 < 8:
                    pa_evac(i + 1)
                pa_tail(i)

            ws.unpin(LD["qkvA"])
            sc = 1.0 / np.sqrt(128.0)
            QK_ALL = keys("qTi", range(8)) + keys("kTi", range(8)) + ["kTc"]
            PT2 = [PTa, PTb, PTc]
            units = [(h_, n, pr) for h_ in range(8) for n in range(2) for pr in range(6)]

            def score(g):
                h_, n, pr = units[g]
                kvh = h_ // 4
                nsl = slice(n * 512, (n + 1) * 512)
                bi = g % 2
                for k in range(2):
                    kc = 2 * pr + k
                    B.mm(pf[2 * bi + k][:, :], kT[:, kvh, kc * 128:(kc + 1) * 128], qT[:, h_, nsl], True, True,
                         QK_ALL, [("pf", 2 * bi + k)])
                ti = g % 3
                for qb in range(2):
                    col = (2 * pr) * 4 + n * 2 + qb
                    B.act(PT2[ti][:, :, qb * 256:(qb + 1) * 256],
                          ps2[bi][:, :].rearrange("p (k t) -> p k t", k=2)[:, :, qb * 256:(qb + 1) * 256], AF.Exp,
                          [("pf", 2 * bi), ("pf", 2 * bi + 1), "maskb"], ["PT%d" % ti], scale=sc,
                          bias=maskb[:, col:col + 1])

            def pv(g):
                h_, n, pr = units[g]
                kvh = h_ // 4
                nsl = slice(n * 512, (n + 1) * 512)
                ti = g % 3
                for k in range(2):
                    kc = 2 * pr + k
                    vk = ("vtok", kc) if kc < 8 else "vtokc"
                    B.mm(pf[4][:, :], vtok[:, kc, kvh * 128:(kvh + 1) * 128], PT2[ti][:, k, :], kc == 0, kc == 11,
                         [vk, "PT%d" % ti], [("pf", 4)])
                    B.mm(pf[5][:, :], onesb[:], PT2[ti][:, k, :], kc == 0, kc == 11, ["onesb", "PT%d" % ti], [("pf", 5)])
                if pr == 5:
                    B.copy(tmpn[0][:], pf[5][:, :], [("pf", 5)], [("tmpn", 0)])
                    B.copy(tmpn[1][:], pf[4][:, :], [("pf", 4)], [("tmpn", 1)])
                    B.recip(rden[:], tmpn[0][:], [("tmpn", 0)], ["rstd"])
                    B.tt(attnT[:, h_, nsl], tmpn[1][:], rden[:], ALU.mult, [("tmpn", 1), "rstd"], [("hT", h_, n)])

            score(0)
            score(1)
            for g in range(len(units)):
                if g + 2 < len(units):
                    score(g + 2)
                pv(g)

            slot, sk = ws.acquire(LD["wo"])
            cnt = 0
            for c in range(8):
                for n in range(2):
                    pi = cnt % 2
                    cnt += 1
                    for kc in range(8):
                        B.mm(pf[pi][:, :], slot[:, kc * 1024 + c * 128: kc * 1024 + (c + 1) * 128],
                             attnT[:, kc, n * 512:(n + 1) * 512], kc == 0, kc == 7, [sk, ("hT", kc, n)], [("pf", pi)])
                    tl = range(4 * n, 4 * n + 4)
                    B.stt(yT[:, c, n * 512:(n + 1) * 512], pf[pi][:, :], adaT[:, l, 16 + c:17 + c],
                          yT[:, c, n * 512:(n + 1) * 512], ALU.mult, ALU.add,
                          [("pf", pi), ("ada", l)] + yk(c, tl), yk(c, tl))
            arena_barrier(AK, BIGK)


        TWO_PI = float(2.0 * np.pi)
        PI_LO = 3.1415925

        def ar_view(off, nelem, dt):
            v = big[:, off // 2:(off // 2) + nelem * (2 if dt == F32 else 1)]
            if dt == F32:
                v = v.bitcast(F32)
            return v

        Toep = ar_view(0, 4096, BF16).rearrange("p (g m) -> p g m", g=32)
        WendT = ar_view(8192, 8192, BF16).rearrange("p (q r m) -> p q r m", q=4, r=16)
        HinB = WendT
        Wout = ar_view(24576, 8192, BF16).rearrange("p (r q m) -> p r q m", r=16, q=4)
        Ubuf = ar_view(40960, 4096, BF16).rearrange("p (g m) -> p g m", g=32)
        S5K = keys("TO", range(32)) + keys("WT", range(4), range(16)) + keys("WO", range(16)) + \
            keys("U", range(32)) + ["Dreg", "E0", "E1", "E2", "E3"]

        def s5_prep(hooks):
            Dv = ar_view(40960, 2048, F32)
            PR = Dv[:, 0:256].rearrange("p (r m) -> p r m", r=16)
            PIm = Dv[:, 256:512].rearrange("p (r m) -> p r m", r=16)
            tA = Dv[:, 512:768].rearrange("p (r m) -> p r m", r=16)
            tB = Dv[:, 768:1024].rearrange("p (r m) -> p r m", r=16)
            tI = itile[:, 0:256].rearrange("p (r m) -> p r m", r=16)
            Bbr = Dv[:, 1280:1536].rearrange("p (r c) -> p r c", r=16)
            Bbi = Dv[:, 1536:1792].rearrange("p (r c) -> p r c", r=16)
            sm_ = Dv[:, 1792:2048]
            sv = lambda i: sm_[:, i * 16:(i + 1) * 16]
            AFr = ar_view(49152, 2048, BF16).rearrange("p (r s c) -> p r s c", r=16, s=8)
            nAFi = ar_view(49152 + 4096, 2048, BF16).rearrange("p (r s c) -> p r s c", r=16, s=8)
            CFr = ar_view(49152 + 8192, 2048, BF16).rearrange("p (r s c) -> p r s c", r=16, s=8)
            CFi = ar_view(49152 + 12288, 2048, BF16).rearrange("p (r s c) -> p r s c", r=16, s=8)
            t1 = xin[0][:, :].rearrange("p (r s c) -> p r s c", r=8, s=8)
            t2 = xin[1][:, :].rearrange("p (r s c) -> p r s c", r=8, s=8)
            slot_in, SK_IN = ws.acquire(LD["s5in"])
            ws.pin(LD["s5in"])
            sqf = slot_in[:, 0:2784].bitcast(F32)
            lamP = sqf[:, 0:64].rearrange("p (x d r) -> p x d r", x=2, d=2)
            ldtP = sqf[:, 64:96].rearrange("p (d r) -> p d r", d=2)
            BPt = sqf[:, 96:608].rearrange("p (x r c) -> p x r c", x=2, r=16)
            CPt = sqf[:, 608:1120].rearrange("p (x r c) -> p x r c", x=2, r=16)
            nv = sqf[:, 1120:1136]
            Mf1 = sqf[:, 1136:1264]
            Mb1 = sqf[:, 1264:1392]
            SQK = [SK_IN]
            XK = [("xin", 0), ("xin", 1)]
            D_ = ["Dreg"]

            def reduce_sin(out, ang, shift, kk):
                B.ts(tB, ang, 1.0 / TWO_PI, ALU.mult, kk, D_, s2=shift / TWO_PI, op1=ALU.add)
                B.copy(tI, tB, D_, D_ + ["itile"])
                B.copy(tB, tI, D_ + ["itile"], D_)
                B.stt(tB, tB, -TWO_PI, ang, ALU.mult, ALU.add, D_ + kk, D_)
                B.ts(tB, tB, shift, ALU.add, D_, D_, s2=PI_LO, op1=ALU.min)
                B.ts(tB, tB, -PI_LO, ALU.max, D_, D_)
                B.act(out, tB, AF.Sin, D_, D_)

            B.tt(Toep[:, :, :], ident[:, :].unsqueeze(1).to_broadcast([128, 32, 128]),
                 dS[:, :].unsqueeze(2).to_broadcast([128, 32, 128]), ALU.mult, ["ident", "dS"], keys("TO", range(32)))

            for d in range(2):
                dt_, ar_, ai_ = sv(0), sv(1), sv(2)
                B.act(dt_, ldtP[:, d, :], AF.Exp, SQK, D_)
                B.tt(ar_, lamP[:, 0, d, :], dt_, ALU.mult, SQK + D_, D_)
                B.tt(ai_, lamP[:, 1, d, :], dt_, ALU.mult, SQK + D_, D_)
                nvb = nv.unsqueeze(1).to_broadcast([128, 16, 16])
                B.tt(tA, ai_.unsqueeze(2).to_broadcast([128, 16, 16]), nvb, ALU.mult, D_ + SQK, D_)
                reduce_sin(PIm, tA, 0.0, D_)
                reduce_sin(PR, tA, float(np.pi / 2), D_)
                B.tt(tA, ar_.unsqueeze(2).to_broadcast([128, 16, 16]), nvb, ALU.mult, D_ + SQK, D_)
                B.act(tA, tA, AF.Exp, D_, D_)
                B.tt(PR, PR, tA, ALU.mult, D_, D_)
                B.tt(PIm, PIm, tA, ALU.mult, D_, D_)
                if d == 0:
                    B.copy(P0r[:, :], PR[:, :, 0], D_, ["P0"])
                    B.copy(P0i[:, :], PIm[:, :, 0], D_, ["P0"])
                B.copy(rho8[:, d, :], tA[:, :, 15], D_, ["rho8"])
                B.copy(L8r[:, d, :], PR[:, :, 15], D_, ["L8"])
                B.copy(L8i[:, d, :], PIm[:, :, 15], D_, ["L8"])
                t8 = sv(3)
                B.ts(t8, ai_, 8.0, ALU.mult, D_, D_)
                B.ts(sv(4), t8, 1.0 / TWO_PI, ALU.mult, D_, D_)
                B.copy(tI[:, 0, :], sv(4), D_, D_ + ["itile"])
                B.copy(sv(4), tI[:, 0, :], D_ + ["itile"], D_)
                B.stt(th8[:, d, :], sv(4), -TWO_PI, t8, ALU.mult, ALU.add, D_, ["th8"])
                nr, ni, l2, kr, ki, tq = sv(5), sv(6), sv(7), sv(8), sv(9), sv(10)
                lr, li = lamP[:, 0, d, :], lamP[:, 1, d, :]
                B.ts(nr, PR[:, :, 8], -1.0, ALU.add, D_, D_)
                B.copy(ni, PIm[:, :, 8], D_, D_)
                B.tt(l2, lr, lr, ALU.mult, SQK, D_)
                B.tt(tq, li, li, ALU.mult, SQK, D_)
                B.tt(l2, l2, tq, ALU.add, D_, D_)
                B.recip(l2, l2, D_, D_)
                B.tt(kr, nr, lr, ALU.mult, D_ + SQK, D_)
                B.tt(tq, ni, li, ALU.mult, D_ + SQK, D_)
                B.tt(kr, kr, tq, ALU.add, D_, D_)
                B.tt(kr, kr, l2, ALU.mult, D_, D_)
                B.tt(ki, ni, lr, ALU.mult, D_ + SQK, D_)
                B.tt(tq, nr, li, ALU.mult, D_ + SQK, D_)
                B.tt(ki, ki, tq, ALU.subtract, D_, D_)
                B.tt(ki, ki, l2, ALU.mult, D_, D_)
                krb = kr.unsqueeze(2).to_broadcast([128, 16, 16])
                kib = ki.unsqueeze(2).to_broadcast([128, 16, 16])
                tAc = tA
                B.tt(Bbr, BPt[:, 0, :, :], krb, ALU.mult, SQK + D_, D_)
                B.tt(tAc, BPt[:, 1, :, :], kib, ALU.mult, SQK + D_, D_)
                B.tt(Bbr, Bbr, tAc, ALU.subtract, D_, D_)
                B.tt(Bbi, BPt[:, 1, :, :], krb, ALU.mult, SQK + D_, D_)
                B.tt(tAc, BPt[:, 0, :, :], kib, ALU.mult, SQK + D_, D_)
                B.tt(Bbi, Bbi, tAc, ALU.add, D_, D_)

                def cplx(outr, outi, Ar, Ai, msl, negi, ok):
                    for hf in range(2):
                        rs = slice(hf * 8, (hf + 1) * 8)
                        Arb = Ar[:, rs, :].unsqueeze(2).to_broadcast([128, 8, 8, 16])
                        Aib = Ai[:, rs, :].unsqueeze(2).to_broadcast([128, 8, 8, 16])
                        Prb = PR[:, rs, msl].unsqueeze(3).to_broadcast([128, 8, 8, 16])
                        Pib = PIm[:, rs, msl].unsqueeze(3).to_broadcast([128, 8, 8, 16])
                        B.tt(t1, Arb, Prb, ALU.mult, D_ + SQK, XK[0:1])
                        B.tt(t2, Aib, Pib, ALU.mult, D_ + SQK, XK[1:2])
                        B.tt(outr[:, rs, :, :], t1, t2, ALU.subtract, XK, ok)
                        B.tt(t1, Arb, Pib, ALU.mult, D_ + SQK, XK[0:1])
                        B.tt(t2, Aib, Prb, ALU.mult, D_ + SQK, XK[1:2])
                        if negi:
                            B.stt(outi[:, rs, :, :], t1, -1.0, t2, ALU.mult, ALU.subtract, XK, ok)
                        else:
                            B.tt(outi[:, rs, :, :], t1, t2, ALU.add, XK, ok)

                if d == 0:
                    m_af = slice(7, None, -1)
                    m_cf = slice(7, 15)
                    m_we = slice(14, 6, -1)
                    m_wo = slice(8, 16)
                else:
                    m_af = slice(7, 15)
                    m_cf = slice(7, None, -1)
                    m_we = slice(7, 15)
                    m_wo = slice(15, 7, -1)
                cplx(AFr, nAFi, Bbr, Bbi, m_af, True, ["E0", "E1"])
                cplx(CFr, CFi, CPt[:, 0, :, :], CPt[:, 1, :, :], m_cf, False, ["E2", "E3"])
                hooks[2 * d]()
                for gb in range(8):
                    pi = gb % 2
                    for k in range(4):
                        g = gb * 4 + k
                        half, pair = g // 16, g % 16
                        rs_ = slice(half * 64, (half + 1) * 64)
                        afr = AFr[rs_, pair, :, :].rearrange("p s c -> p (s c)")
                        afi = nAFi[rs_, pair, :, :].rearrange("p s c -> p (s c)")
                        cfr = CFr[rs_, pair, :, :].rearrange("p s c -> p (s c)")
                        cfi = CFi[rs_, pair, :, :].rearrange("p s c -> p (s c)")
                        B.mm(pf[pi][:, k * 128:(k + 1) * 128], afr, cfr, True, False, ["E0", "E2"], [("pf", pi)])
                        B.mm(pf[pi][:, k * 128:(k + 1) * 128], afi, cfi, False, True, ["E1", "E3"], [("pf", pi)])
                    tm = tmpn[pi]
                    B.tt(tm[:].rearrange("p (g m) -> p g m", g=4), pf[pi][:, :].rearrange("p (g m) -> p g m", g=4),
                         (Mf1 if d == 0 else Mb1).unsqueeze(1).to_broadcast([128, 4, 128]), ALU.mult,
                         [("pf", pi)] + SQK, [("tmpn", pi)])
                    tg = Toep[:, gb * 4:(gb + 1) * 4, :]
                    B.tt(tg, tg, tm[:].rearrange("p (g m) -> p g m", g=4), ALU.add,
                         [("tmpn", pi)] + keys("TO", range(gb * 4, gb * 4 + 4)), keys("TO", range(gb * 4, gb * 4 + 4)))
                hooks[2 * d + 1]()
                if d == 0:
                    cplx(AFr, nAFi, Bbr, Bbi, m_we, False, ["E0", "E1"])
                else:
                    B.ts(nAFi[:, :, :, :], nAFi[:, :, :, :], -1.0, ALU.mult, ["E1"], ["E1"])
                for x, src in enumerate((AFr, nAFi)):
                    q = d * 2 + x
                    for hb in range(2):
                        for k in range(8):
                            pair = hb * 8 + k
                            B.tr(pb[hb][:, k * 128:(k + 1) * 128], src[:, pair, :, :].rearrange("p s c -> p (s c)"),
                                 identb[:], ["E%d" % x, "identb"], [("pb", hb)])
                        B.copy(WendT[:, q, hb * 8:(hb + 1) * 8, :], pb[hb][:, :].rearrange("p (r m) -> p r m", r=8),
                               [("pb", hb)], keys("WT", q, range(hb * 8, hb * 8 + 8)), eng="act")
                cplx(Wout[:, :, d * 2 + 0, :].rearrange("p r (t c) -> p r t c", t=8),
                     Wout[:, :, d * 2 + 1, :].rearrange("p r (t c) -> p r t c", t=8),
                     CPt[:, 0, :, :], CPt[:, 1, :, :], m_wo, True, keys("WO", range(16)))

        def s5_phase(l):
            ws.unpin(LD["s5in"])
            uTs = sq[:, :, :].rearrange("p c t -> p (c t)").rearrange("p (cc s j) -> p cc s j", cc=4, s=8)
            E8v = E8[:, :].rearrange("p (a x) -> p a x", a=8)
            Xb = ar_view(49152, 1024, F32).rearrange("p (x k j) -> p x k j", x=2, k=4)
            tab = ar_view(49152 + 4096, 1024, F32).rearrange("p (x k j) -> p x k j", x=2, k=4)
            tmp = [ar_view(49152 + 8192 + 2048 * i, 512, F32).rearrange("p (k j) -> p k j", k=4) for i in range(4)]
            EK = ["E0", "E1", "E2", "E3"]
            for gb in range(8):
                pi = gb % 2
                for k in range(4):
                    g = gb * 4 + k
                    cc, gl = g // 8, g % 8
                    for s_ in range(8):
                        B.mm(pf[pi][:, k * 128:(k + 1) * 128], E8v[:, gl, 112 - s_ * 16: 112 - s_ * 16 + 128],
                             uTs[:, cc, s_, :], s_ == 0, s_ == 7, ["E8"] + keys("sq", [2 * cc, 2 * cc + 1]), [("pf", pi)])
                B.copy(Ubuf[:, gb * 4:(gb + 1) * 4, :], pf[pi][:, :].rearrange("p (g m) -> p g m", g=4), [("pf", pi)],
                       keys("U", range(gb * 4, gb * 4 + 4)), eng=("act" if gb % 2 == 0 else "dve"))
            Ue = ar_view(49152, 256, BF16)
            for g in range(32):
                cc, gl = g // 8, g % 8
                B.mm(pf[2][:, g * 8:g * 8 + 4], E8v[:, gl, 112:240], uTs[:, cc, 0, 0:128:32], True, True,
                     ["E8"] + keys("sq", [2 * cc, 2 * cc + 1]), [("pf", 2)])
                B.mm(pf[2][:, g * 8 + 4:g * 8 + 8], E8v[:, gl, 112:240], uTs[:, cc, 7, 31:128:32], True, True,
                     ["E8"] + keys("sq", [2 * cc, 2 * cc + 1]), [("pf", 2)])
            B.copy(Ue[:, :], pf[2][:, 0:256], [("pf", 2)], ["E0"])
            Uev = Ue[:, :].rearrange("p (g d s) -> p g d s", g=32, d=2)
            x0v = pf[3][:, :].rearrange("p (q r ab s) -> p q r ab s", q=4, r=16, ab=2)
            for q in range(4):
                d = q // 2
                for pair in range(16):
                    B.mm(x0v[:, q, pair, 0, :], WendT[0:32, q, pair, :], Uev[0:32, pair, d, :], True, True,
                         [("WT", q, pair), "E0"], [("pf", 3)])
                    B.mm(x0v[:, q, pair, 1, :], WendT[0:32, q, pair, :], Uev[0:32, 16 + pair, d, :], True, True,
                         [("WT", q, pair), "E0"], [("pf", 3)])
            xe = ar_view(49152 + 4096, 256, F32).rearrange("p (x d s r) -> p x d s r", x=2, d=2, s=4)
            for q in range(4):
                d, x = q // 2, q % 2
                for half in range(2):
                    rs_ = slice(half * 64, (half + 1) * 64)
                    B.copy(xe[rs_, x, d, :, :], x0v[rs_, q, :, half, :].rearrange("p r s -> p s r"), [("pf", 3)], ["E1"])
            hv0 = Hfin[:, 0, :].rearrange("p (d s r) -> p d s r", d=2, s=4)
            hv1 = Hfin[:, 1, :].rearrange("p (d s r) -> p d s r", d=2, s=4)
            B.copy(hv0[:, 1, :, :], xe[:, 0, 1, :, :], ["E1"], ["Hfin"])
            B.copy(hv1[:, 1, :, :], xe[:, 1, 1, :, :], ["E1"], ["Hfin"])
            p0r = P0r[:, :].unsqueeze(1).to_broadcast([128, 4, 16])
            p0i = P0i[:, :].unsqueeze(1).to_broadcast([128, 4, 16])
            e1 = ar_view(49152 + 8192, 64, F32).rearrange("p (s r) -> p s r", s=4)
            e2 = ar_view(49152 + 8192 + 2048, 64, F32).rearrange("p (s r) -> p s r", s=4)
            B.tt(e1, xe[:, 0, 0, :, :], p0r, ALU.mult, ["E1", "P0"], ["E2"])
            B.tt(e2, xe[:, 1, 0, :, :], p0i, ALU.mult, ["E1", "P0"], ["E2"])
            B.tt(hv0[:, 0, :, :], e1, e2, ALU.subtract, ["E2"], ["Hfin"])
            B.tt(e1, xe[:, 1, 0, :, :], p0r, ALU.mult, ["E1", "P0"], ["E2"])
            B.tt(e2, xe[:, 0, 0, :, :], p0i, ALU.mult, ["E1", "P0"], ["E2"])
            B.tt(hv1[:, 0, :, :], e1, e2, ALU.add, ["E2"], ["Hfin"])
            B.ts(th8n[:, :, :], th8[:, :, :], 1.0 / TWO_PI, ALU.mult, ["th8"], ["th8n"])
            for d in range(2):
                B.tt(LIr[:, d, :], L8r[:, d, :], initP[:, 0, d, :], ALU.mult, ["L8", "initP"], ["LI"])
                B.tt(LIt[:, :], L8i[:, d, :], initP[:, 1, d, :], ALU.mult, ["L8", "initP"], ["LIt"])
                B.tt(LIr[:, d, :], LIr[:, d, :], LIt[:, :], ALU.subtract, ["LI", "LIt"], ["LI"])
                B.tt(LIi[:, d, :], L8r[:, d, :], initP[:, 1, d, :], ALU.mult, ["L8", "initP"], ["LI"])
                B.tt(LIt[:, :], L8i[:, d, :], initP[:, 0, d, :], ALU.mult, ["L8", "initP"], ["LIt"])
                B.tt(LIi[:, d, :], LIi[:, d, :], LIt[:, :], ALU.add, ["LI", "LIt"], ["LI"])
            cntb = 0
            for d in range(2):
                for bt in range(4):
                    prs = range(bt * 4, bt * 4 + 4)
                    for x in range(2):
                        q = d * 2 + x
                        for k2 in range(2):
                            pi = 2 + (cntb % 2)
                            cntb += 1
                            for kk in range(2):
                                pair = bt * 4 + k2 * 2 + kk
                                B.mm(pf[pi][:, (2 * kk) * 128:(2 * kk + 1) * 128], WendT[:, q, pair, :], Ubuf[:, pair, :],
                                     True, True, [("WT", q, pair), ("U", pair)], [("pf", pi)])
                                B.mm(pf[pi][:, (2 * kk + 1) * 128:(2 * kk + 2) * 128], WendT[:, q, pair, :], Ubuf[:, 16 + pair, :],
                                     True, True, [("WT", q, pair), ("U", 16 + pair)], [("pf", pi)])
                            pv = pf[pi][:, :].rearrange("p (k ab j) -> p k ab j", k=2, ab=2)
                            B.copy(Xb[0:64, x, k2 * 2:k2 * 2 + 2, :], pv[0:64, :, 0, :], [("pf", pi)], ["E0"], eng="act")
                            B.copy(Xb[64:128, x, k2 * 2:k2 * 2 + 2, :], pv[64:128, :, 1, :], [("pf", pi)], ["E0"], eng="act")
                    j0 = 0 if d == 0 else 127
                    B.tt(Xb[:, 0, :, j0], Xb[:, 0, :, j0], LIr[:, d, bt * 4:bt * 4 + 4], ALU.add, ["E0", "LI"], ["E0"])
                    B.tt(Xb[:, 1, :, j0], Xb[:, 1, :, j0], LIi[:, d, bt * 4:bt * 4 + 4], ALU.add, ["E0", "LI"], ["E0"])
                    tq_ = tmp[3]
                    B.tt(tq_[:, :, :], th8n[:, d, bt * 4:bt * 4 + 4].unsqueeze(2).to_broadcast([128, 4, 128]),
                         jv[:, :].unsqueeze(1).to_broadcast([128, 4, 128]), ALU.mult, ["th8n", "jv"], ["E3"])
                    tb_ = tmp[2]
                    ti_ = itile[:, :].rearrange("p (k j) -> p k j", k=4)
                    B.copy(ti_, tq_[:, :, :], ["E3"], ["E2", "itile"])
                    B.copy(tb_[:, :, :], ti_, ["E2", "itile"], ["E3"])
                    B.tt(tb_[:, :, :], tq_[:, :, :], tb_[:, :, :], ALU.subtract, ["E3"], ["E3"])
                    B.act(tab[:, 1, :, :], tb_[:, :, :], AF.Sin, ["E3"], ["E1"], scale=TWO_PI * 0.99999)
                    B.stt(tb_[:, :, :], tb_[:, :, :], -1.0, tb_[:, :, :], ALU.mult, ALU.max, ["E3"], ["E3"])
                    B.act(tab[:, 0, :, :], tb_[:, :, :], AF.Sin, ["E3", "halfpi"], ["E1"], scale=-TWO_PI * 0.99999,
                          bias=halfpi[:, 0:1])
                    cs, sn = tab[:, 0, :, :], tab[:, 1, :, :]
                    Xr, Xi = Xb[:, 0, :, :], Xb[:, 1, :, :]
                    B.tt(tmp[0][:, :, :], Xr, cs, ALU.mult, ["E0", "E1"], ["E2"])
                    B.tt(tmp[1][:, :, :], Xi, sn, ALU.mult, ["E0", "E1"], ["E2"])
                    B.tt(tmp[2][:, :, :], Xi, cs, ALU.mult, ["E0", "E1"], ["E3"])
                    B.tt(tmp[3][:, :, :], Xr, sn, ALU.mult, ["E0", "E1"], ["E3"])
                    B.tt(Xr, tmp[0][:, :, :], tmp[1][:, :, :], ALU.add if d == 0 else ALU.subtract, ["E2"], ["E0"])
                    B.tt(Xi, tmp[2][:, :, :], tmp[3][:, :, :], ALU.subtract if d == 0 else ALU.add, ["E3"], ["E0"])
                    mult = tmp[0]
                    smk = smf if d == 0 else smb
                    B.tt(mult[:, :, :], smk[:, :].unsqueeze(1).to_broadcast([128, 4, 128]),
                         rho8[:, d, bt * 4:bt * 4 + 4].unsqueeze(2).to_broadcast([128, 4, 128]), ALU.mult,
                         ["smf", "smb", "rho8"], ["E2"])
                    mf = mult[:, :, :].rearrange("p k j -> p (k j)")
                    for x, dst in ((0, tmp[2]), (1, tmp[3])):
                        src = Xb[:, x, :, :].rearrange("p k j -> p (k j)")
                        dfl = dst[:, :, :].rearrange("p k j -> p (k j)")
                        if d == 0:
                            B.P.add("dve", lambda h, o=dfl, m=mf, s_=src: h.tensor_tensor_scan(
                                out=o, data0=m, data1=s_, initial=0.0, op0=ALU.mult, op1=ALU.add), ["E0", "E2"], ["E3"])
                        else:
                            B.P.add("dve", lambda h, o=dfl[:, ::-1], m=mf[:, ::-1], s_=src[:, ::-1]: h.tensor_tensor_scan(
                                out=o, data0=m, data1=s_, initial=0.0, op0=ALU.mult, op1=ALU.add), ["E0", "E2"], ["E3"])
                    kr_, ki_ = tmp[2][:, :, :], tmp[3][:, :, :]
                    B.tt(tmp[0][:, :, :], kr_, cs, ALU.mult, ["E3", "E1"], ["E2"])
                    B.tt(tmp[1][:, :, :], ki_, sn, ALU.mult, ["E3", "E1"], ["E2"])
                    B.tt(Xr, tmp[0][:, :, :], tmp[1][:, :, :], ALU.subtract if d == 0 else ALU.add, ["E2"], ["E0"])
                    B.tt(tmp[0][:, :, :], ki_, cs, ALU.mult, ["E3", "E1"], ["E2"])
                    B.tt(tmp[1][:, :, :], kr_, sn, ALU.mult, ["E3", "E1"], ["E2"])
                    B.tt(Xi, tmp[0][:, :, :], tmp[1][:, :, :], ALU.add if d == 0 else ALU.subtract, ["E2"], ["E0"])
                    for x in range(2):
                        q = d * 2 + x
                        wk = keys("WT", q, prs)
                        if d == 0:
                            B.tt(HinB[:, q, bt * 4:bt * 4 + 4, 1:128], Xb[:, x, :, 0:127],
                                 smf[:, 1:128].unsqueeze(1).to_broadcast([128, 4, 127]), ALU.mult, ["E0", "smf"], wk)
                            B.copy(HinB[:, q, bt * 4:bt * 4 + 4, 0], initP[:, x, d, bt * 4:bt * 4 + 4], ["initP"], wk)
                        else:
                            B.tt(HinB[:, q, bt * 4:bt * 4 + 4, 0:127], Xb[:, x, :, 1:128],
                                 smb[:, 0:127].unsqueeze(1).to_broadcast([128, 4, 127]), ALU.mult, ["E0", "smb"], wk)
                            B.copy(HinB[:, q, bt * 4:bt * 4 + 4, 127], initP[:, x, d, bt * 4:bt * 4 + 4], ["initP"], wk)
            ada(l + 1)
            for x, od in ((0, s5re_d), (1, s5im_d)):
                B.tr(pf[0][:, x * 128:(x + 1) * 128], Hfin[:, x, :], ident[:], ["Hfin", "ident"], [("pf", 0)])
            hst = tmpn[0]
            B.copy(hst[:, 0:256], pf[0][:, 0:256], [("pf", 0)], [("tmpn", 0)])
            for x, od in ((0, s5re_d), (1, s5im_d)):
                for d in range(2):
                    for sg in range(4):
                        r0 = d * 64 + sg * 16
                        B.dma("sp", od[sg, d, :, :].rearrange("(h r) p -> r h p", h=2),
                              hst[r0:r0 + 16, x * 128:(x + 1) * 128].rearrange("r (h p) -> r h p", h=2),
                              [("tmpn", 0)], (), final=True)
            for gb in range(8):
                pi = gb % 2
                for k in range(4):
                    g = gb * 4 + k
                    half, pair = g // 16, g % 16
                    rs_ = slice(half * 64, (half + 1) * 64)
                    B.mm(pf[pi][:, k * 128:(k + 1) * 128], Toep[:, g, :], Ubuf[:, g, :], True, False,
                         [("TO", g), ("U", g)], [("pf", pi)])
                    for q in range(4):
                        B.mm(pf[pi][:, k * 128:(k + 1) * 128], Wout[rs_, pair, q, :], HinB[rs_, q, pair, :], False, q == 3,
                             [("WO", pair), ("WT", q, pair)], [("pf", pi)])
                B.copy(Ubuf[:, gb * 4:(gb + 1) * 4, :], pf[pi][:, :].rearrange("p (g m) -> p g m", g=4), [("pf", pi)],
                       keys("U", range(gb * 4, gb * 4 + 4)), eng=("act" if gb % 2 == 0 else "dve"))
            y5 = ar_view(8192, 4096, F32).rearrange("p (c t) -> p c t", c=4)
            Y5K = keys("WT", range(4), range(16))
            gT = ar_view(24576, 4096, BF16).rearrange("p (c t) -> p c t", c=4)
            GTK = keys("WO", range(16))
            for cc in range(4):
                for hb in range(2):
                    pi = 2 + hb
                    for k in range(4):
                        tp = hb * 4 + k
                        for gl in range(8):
                            B.mm(pf[pi][:, k * 128:(k + 1) * 128], E8v[:, tp, 112 - gl * 16: 112 - gl * 16 + 128],
                                 Ubuf[:, cc * 8 + gl, :], gl == 0, gl == 7, ["E8", ("U", cc * 8 + gl)], [("pf", pi)])
                    B.copy(y5[:, cc, :].rearrange("p (j t) -> p t j", t=8)[:, hb * 4:(hb + 1) * 4, :],
                           pf[pi][:, :].rearrange("p (t j) -> p t j", t=4), [("pf", pi)], Y5K,
                           eng=("act" if hb == 0 else "dve"))
                B.act(gT[:, cc, :], y5[:, cc, :], AF.Gelu_apprx_tanh, Y5K, GTK)
            s5o = sq[:, :, :].rearrange("p c t -> p (c t)").rearrange("p (c t) -> p c t", c=4)
            slot, sk = ws.acquire(LD["wglu"])
            cg = 0
            for m in range(4):
                for n in range(2):
                    pi = cg % 2
                    cg += 1
                    for kc in range(4):
                        B.mm(pf[pi][:, :], slot[:, kc * 512 + m * 128: kc * 512 + (m + 1) * 128],
                             gT[:, kc, n * 512:(n + 1) * 512], kc == 0, kc == 3, [sk] + GTK, [("pf", pi)])
                    B.act(tmpn[pi][:], pf[pi][:, :], AF.Sigmoid, [("pf", pi), "bglu"], [("tmpn", pi)], bias=bglu[:, m:m + 1])
                    B.tt(s5o[:, m, n * 512:(n + 1) * 512], gT[:, m, n * 512:(n + 1) * 512], tmpn[pi][:], ALU.mult,
                         GTK + [("tmpn", pi)], keys("sq", [2 * m, 2 * m + 1]))

        cnt = [0]
        u_done = []

        def proj_fm(slot, sk, col0, evac):
            for n in range(2):
                pi = cnt[0] % 2
                cnt[0] += 1
                for kc in range(8):
                    B.mm(pf[pi][:, :], col0(kc), hT[:, kc, n * 512:(n + 1) * 512], kc == 0, kc == 7,
                         [sk, ("hT", kc, n)], [("pf", pi)])
                evac(pf[pi], ("pf", pi), n)

        def u_proj():
            uTs = sq[:, :, :].rearrange("p c t -> p (c t)").rearrange("p (cc s j) -> p cc s j", cc=4, s=8)
            slot, sk = ws.acquire(LD["winU"])
            for cc in range(4):
                def ev(ps, pk, n, cc=cc):
                    B.copy(uTs[:, cc, :, n * 64:(n + 1) * 64],
                           ps[:, :].rearrange("p (j s) -> p s j", s=8), [pk], keys("sq", [2 * cc, 2 * cc + 1]), eng="act")
                proj_fm(slot, sk, lambda kc, cc=cc, slot=slot: slot[:, kc * 512 + cc * 128: kc * 512 + (cc + 1) * 128], ev)
            u_done.append(1)

        def even_mixer(l):
            if not u_done:
                u_proj()
            s5_phase(l)
            s5o = sq[:, :, :].rearrange("p c t -> p (c t)").rearrange("p (c t) -> p c t", c=4)

            arena.reset()
            qT = arena.alloc(2 * NT, BF16).rearrange("p (h t) -> p h t", h=2)
            kT = arena.alloc(2 * NT, BF16).rearrange("p (h t) -> p h t", h=2)
            vtok = arena.alloc(8 * 512, BF16).rearrange("p (i f) -> p i f", i=8)
            srT = arena.alloc(4 * NT, BF16).rearrange("p (h t) -> p h t", h=4)
            glrT = arena.alloc(NT, BF16)
            qd = arena.alloc(4 * NT, BF16).rearrange("p (a t) -> p a t", a=4)
            kd = arena.alloc(4 * NT, BF16).rearrange("p (a t) -> p a t", a=4)
            Sin = arena.alloc(32 * 128, BF16).rearrange("p (a v) -> p a v", a=32)
            spt = arena.alloc(NT, F32)
            cbt = arena.alloc(NT, F32)
            ebt = spt
            kdT = arena.alloc(8 * 128, BF16).rearrange("p (n k) -> p n k", n=8)
            attb0 = arena.alloc(512, BF16)
            sqo = arena.alloc(512, BF16)
            t1o = arena.alloc(512, F32)
            GK = ["gq", "gk", "srT", "glrT", "spt", "cbt", "kdT", "att0", "sqo", "t1o"] + \
                keys("vtokg", range(8)) + keys("qd", range(4)) + keys("kd", range(4)) + keys("Sin", range(32))
            arena_barrier(S5K, GK)

            slot, sk = ws.acquire(LD["winQK"])
            for m in range(4):
                def ev(ps, pk, n, m=m):
                    if m < 2:
                        B.act(qT[:, m, n * 512:(n + 1) * 512], ps[:, :], AF.Copy, [pk], ["gq"], scale=0.125)
                    else:
                        B.copy(kT[:, m - 2, n * 512:(n + 1) * 512], ps[:, :], [pk], ["gk"], eng="act")
                proj_fm(slot, sk, lambda kc, m=m, slot=slot: slot[:, kc * 512 + m * 128: kc * 512 + (m + 1) * 128], ev)
            slot, sk = ws.acquire(LD["winVR"])
            for m in range(4):
                def ev(ps, pk, n, m=m):
                    B.act(srT[:, m, n * 512:(n + 1) * 512], ps[:, :], AF.Silu, [pk], ["srT"])
                proj_fm(slot, sk, lambda kc, m=m, slot=slot: slot[:, kc * 1024 + 512 + m * 128: kc * 1024 + 512 + (m + 1) * 128], ev)
            for i in range(8):
                pi = cnt[0] % 2
                cnt[0] += 1
                for kc in range(8):
                    B.mm(pf[pi][:, :], hT[:, kc, i * 128:(i + 1) * 128], slot[:, kc * 1024: kc * 1024 + 512],
                         kc == 0, kc == 7, [sk, ("hT", kc, i // 4)], [("pf", pi)])
                B.copy(vtok[:, i, :], pf[pi][:, :], [("pf", pi)], [("vtokg", i)], eng="act")
            slot, sk = ws.acquire(LD["winG"])
            for n in range(2):
                pi = cnt[0] % 2
                cnt[0] += 1
                for kc in range(8):
                    B.mm(pf[pi][0:32, :], slot[:, kc * 32:(kc + 1) * 32], hT[:, kc, n * 512:(n + 1) * 512],
                         kc == 0, kc == 7, [sk, ("hT", kc, n)], [("pf", pi)])
                B.copy(glrT[0:32, n * 512:(n + 1) * 512], pf[pi][0:32, :], [("pf", pi)], ["glrT"], eng="act")

            B.memset(m01[:], 1.0, ["m01"])
            B.memset(m01[:, 0::128], 0.0, ["m01"])
            for d in range(2):
                for hp in range(2):
                    a = d * 2 + hp
                    for n in range(2):
                        pi = cnt[0] % 2
                        cnt[0] += 1
                        B.mm(pf[pi][:, :], wg2[:, d * 256 + hp * 128: d * 256 + (hp + 1) * 128],
                             glrT[0:32, n * 512:(n + 1) * 512], True, True, ["wg2", "glrT"], [("pf", pi)])
                        B.act(ebt[:, n * 512:(n + 1) * 512], pf[pi][:, :], AF.Exp, [("pf", pi), "nbg"], ["spt"],
                              scale=-1.0, bias=nbg[:, a:a + 1])
                        B.act(spt[:, n * 512:(n + 1) * 512], ebt[:, n * 512:(n + 1) * 512], AF.Ln, ["spt", "onec"], ["spt"],
                              bias=onec[:, 0:1])
                    if d == 0:
                        B.P.add("dve", lambda h, o=cbt[:, :], m=m01[:, :], x=spt[:, :]: h.tensor_tensor_scan(
                            out=o, data0=m, data1=x, initial=0.0, op0=ALU.mult, op1=ALU.add), ["m01", "spt"], ["cbt"])
                    else:
                        B.P.add("dve", lambda h, o=cbt[:, ::-1], m=m01[:, :], x=spt[:, ::-1]: h.tensor_tensor_scan(
                            out=o, data0=m, data1=x, initial=0.0, op0=ALU.mult, op1=ALU.add), ["m01", "spt"], ["cbt"])
                    B.act(ebt[:, :], cbt[:, :], AF.Exp, ["cbt"], ["spt"], scale=-1.0 / 16.0)
                    B.tt(qd[:, a, :], qT[:, hp, :], ebt[:, :], ALU.mult, ["gq", "spt"], [("qd", a)])
                    last = 127 if d == 0 else 0
                    B.copy(ebl[:, a, :], ebt[:, last::128], ["spt"], [("ebl", a)])
                    B.act(ebt[:, :], cbt[:, :], AF.Exp, ["cbt"], ["spt"], scale=1.0 / 16.0)
                    B.tt(kd[:, a, :], kT[:, hp, :], ebt[:, :], ALU.mult, ["gk", "spt"], [("kd", a)])

            sptb = spt.bitcast(BF16)
            cbtb = cbt.bitcast(BF16)
            kdTs = [kdT,
                    sptb[:, 0:1024].rearrange("p (n k) -> p n k", n=8),
                    sptb[:, 1024:2048].rearrange("p (n k) -> p n k", n=8),
                    cbtb[:, 0:1024].rearrange("p (n k) -> p n k", n=8)]
            Sbuf = [[Sst[0][:, :], Sst[1][:, :]],
                    [cbt[:, 512:640], cbt[:, 640:768]],
                    [cbt[:, 768:896], cbt[:, 896:1024]],
                    [t1o[:, 0:128], t1o[:, 128:256]]]
            CHK = keys("kdTs", range(4)) + keys("Sc", range(4), range(2))
            arena_barrier(["spt", "cbt", "t1o", "kdT"] + keys("S", range(2)), CHK)
            for a in range(4):
                for n in range(8):
                    B.tr(pb[a % 2][:, n * 128:(n + 1) * 128], kd[:, a, n * 128:(n + 1) * 128], identb[:],
                         [("kd", a), "identb"], [("pb", a % 2)])
                B.copy(kdTs[a][:, :, :], pb[a % 2][:, :].rearrange("p (n k) -> p n k", n=8), [("pb", a % 2)],
                       [("kdTs", a)], eng="act")
                d, hp = a // 2, a % 2
                B.dma("sp", Sbuf[a][0], glainit_d[d, hp * 128:(hp + 1) * 128, :], (), [("Sc", a, 0)])
            cur = [0, 0, 0, 0]
            for idx in range(8):
                for a in range(4):
                    d, hp = a // 2, a % 2
                    n = idx if d == 0 else 7 - idx
                    c_ = cur[a]
                    Sc, Sn = Sbuf[a][c_], Sbuf[a][1 - c_]
                    pA = pf[4 + (a % 2)]
                    pk = ("pf", 4 + (a % 2))
                    if idx > 0 and idx % 2 == 0:
                        B.ts(Sc, Sc, cm[:, 0:1], ALU.mult, [("Sc", a, c_), "cm"], [("Sc", a, c_)])
                    B.copy(Sin[:, a * 8 + n, :], Sc, [("Sc", a, c_)], [("Sin", a * 8 + n)], eng="act")
                    for hh in range(2):
                        h_ = hp * 2 + hh
                        B.mm(pA[:, hh * 128:(hh + 1) * 128], kdTs[a][:, n, :], vtok[:, n, h_ * 128:(h_ + 1) * 128],
                             True, True, [("kdTs", a), ("vtokg", n)], [pk])
                    B.ts(Stmp[:], Sc, ebl[:, a, n:n + 1], ALU.mult, [("Sc", a, c_), ("ebl", a)], ["Stmp"])
                    for hh in range(2):
                        rs_ = slice(hh * 64, (hh + 1) * 64)
                        B.stt(Sn[rs_, :], pA[rs_, hh * 128:(hh + 1) * 128], ebl[rs_, a, n:n + 1], Stmp[rs_, :],
                              ALU.mult, ALU.add, [pk, ("ebl", a), "Stmp"], [("Sc", a, 1 - c_)])
                    cur[a] = 1 - c_
                    if idx % 2 == 1:
                        B.dma("sp", glaout_d[n // 2, d, hp * 128:(hp + 1) * 128, :], Sn, [("Sc", a, 1 - c_)], (),
                              final=True)
            arena_barrier(CHK, ["spt", "cbt", "t1o", "kdT"] + keys("S", range(2)))

            attb = [rl[0][:, 0:256], rl[1][:, 0:256], attb0[:, 0:256]]
            attk = [("rl", 0), ("rl", 1), "att0"]
            gunits = [(n, h_) for n in range(8) for h_ in range(4)]

            def g_att(u):
                n, h_ = gunits[u]
                hp, hh = h_ // 2, h_ % 2
                rs_ = slice(hh * 64, (hh + 1) * 64)
                csl = slice(n * 128, (n + 1) * 128)
                pi = u % 2
                for d in range(2):
                    a = d * 2 + hp
                    B.mm(pf[pi][:, d * 128:(d + 1) * 128], kd[rs_, a, csl], qd[rs_, a, csl], True, True,
                         [("kd", a), ("qd", a)], [("pf", pi)])
                B.tt(attb[u % 3], pf[pi][:, 0:256], tri2[:, :], ALU.mult, [("pf", pi), "tri2"], [attk[u % 3]])

            def g_out(u):
                n, h_ = gunits[u]
                hp, hh = h_ // 2, h_ % 2
                rs_ = slice(hh * 64, (hh + 1) * 64)
                csl = slice(n * 128, (n + 1) * 128)
                ob = 2 + (n % 2)
                for d in range(2):
                    a = d * 2 + hp
                    B.mm(pf[ob][:, h_ * 128:(h_ + 1) * 128], vtok[:, n, h_ * 128:(h_ + 1) * 128],
                         attb[u % 3][:, d * 128:(d + 1) * 128], d == 0, False, [("vtokg", n), attk[u % 3]], [("pf", ob)])
                    B.mm(pf[ob][:, h_ * 128:(h_ + 1) * 128], Sin[rs_, a * 8 + n, :], qd[rs_, a, csl],
                         False, d == 1, [("Sin", a * 8 + n), ("qd", a)], [("pf", ob)])
                if h_ == 3:
                    B.act(sqo[:], pf[ob][:, :], AF.Square, [("pf", ob)], ["sqo"])
                    B.mm(pf[4][:, :], onesdv[:], sqo[:], True, True, ["onesdv", "sqo"], [("pf", 4)])
                    B.act(sd[:], pf[4][:, :], AF.Ln, [("pf", 4), "epsc"], ["sd"], bias=epsc[:, 0:1])
                    B.act(rstd[:], sd[:], AF.Exp, ["sd"], ["rstd"], scale=-0.5)
                    B.tt(t1o[:], pf[ob][:, :], rstd[:], ALU.mult, [("pf", ob), "rstd"], ["t1o"])
                    B.stt(hT[:, 4:8, csl], t1o[:].rearrange("p (h t) -> p h t", h=4), gnorm[:, 0:1], srT[:, :, csl],
                          ALU.mult, ALU.mult, ["t1o", "gnorm", "srT"], keys("hT", range(4, 8), n // 4))

            g_att(0)
            g_att(1)
            for u in range(len(gunits)):
                if u + 2 < len(gunits):
                    g_att(u + 2)
                g_out(u)

            slot, sk = ws.acquire(LD["wout"])
            for c in range(8):
                for n in range(2):
                    pi = cnt[0] % 2
                    cnt[0] += 1
                    for kc in range(8):
                        if kc < 4:
                            rhs_, rk = s5o[:, kc, n * 512:(n + 1) * 512], keys("sq", [2 * kc, 2 * kc + 1])
                        else:
                            rhs_, rk = hT[:, kc, n * 512:(n + 1) * 512], [("hT", kc, n)]
                        B.mm(pf[pi][:, :], slot[:, kc * 1024 + c * 128: kc * 1024 + (c + 1) * 128],
                             rhs_, kc == 0, kc == 7, [sk] + rk, [("pf", pi)])
                    tl = range(4 * n, 4 * n + 4)
                    B.stt(yT[:, c, n * 512:(n + 1) * 512], pf[pi][:, :], adaT[:, l, 16 + c:17 + c],
                          yT[:, c, n * 512:(n + 1) * 512], ALU.mult, ALU.add,
                          [("pf", pi), ("ada", l)] + yk(c, tl), yk(c, tl))
            arena_barrier(GK, BIGK)

        if DBG_SKIP_EVEN:
            input_transposes()
        if not DBG_SKIP_EVEN:
            s5_prep([lambda: (input_transposes(), ada_step(0, 0), ada_step(0, 1)),
                     lambda: ada_step(0, 2),
                     lambda: (ada_step(0, 3), ada_step(0, 4), ada_step(0, 5), ada_finish(0)),
                     lambda: None])
            norm_mod(0, 0)
            u_proj()
        for l in range(DEPTH):
            if DBG_SKIP_EVEN:
                ada(l)
            if l != 0 or DBG_SKIP_EVEN:
                norm_mod(l, 0)
            if l % 2 == 0 and not DBG_SKIP_EVEN:
                even_mixer(l)
            if l % 2 == 1 and not DBG_SKIP_ODD:
                attention(l)
            norm_mod(l, 1)
            mlp(l)

        for i in range(8):
            xb_ = xin[i % 2]
            xk = ("xin", i % 2)
            for half in range(2):
                ps = pf[half]
                pk = ("pf", half)
                for cc in range(4):
                    c = half * 4 + cc
                    B.tr(ps[:, cc * 128:(cc + 1) * 128], yT[:, c, i * 128:(i + 1) * 128], ident[:],
                         yk(c, i) + ["ident"], [pk])
                B.copy(xb_[:, half * 512:(half + 1) * 512], ps[:, :], [pk], [xk],
                       eng=("act" if half == 0 else "dve"))
            B.dma("sp", y_d[i * 128:(i + 1) * 128, :], xb_[:], [xk], (), final=True)

        P.emit(nc, B.fin)
    return nc


_NC_CACHE = {}


def _get_program():
    if "nc" not in _NC_CACHE:
        _NC_CACHE["nc"] = build_program()
    return _NC_CACHE["nc"]


def _fm(v, ncol):
    return np.ascontiguousarray(np.asarray(v, np.float32).reshape(ncol, 128).T)


def kernel(x_prompt, x_sample, state_s5_re, state_s5_im, state_gla, cache_k, cache_v, c, c_ctx,
           norm_mix, norm_mlp, w_ada, b_ada, w_mlp_in, w_mlp_out, w_in_e, w_out_e,
           s5_lambda_re, s5_lambda_im, s5_log_dt, s5_b_re, s5_b_im, s5_c_re, s5_c_im, s5_d,
           s5_w_glu, s5_b_glu, gla_w_gate2, gla_b_gate, gla_norm, w_qkv_o, w_o_o, q_norm, k_norm):
    f32 = np.float32
    x_prompt = np.asarray(x_prompt, f32)
    x_sample = np.asarray(x_sample, f32)
    nc = _get_program()
    ident = np.eye(128, dtype=f32)
    b_adaT = np.ascontiguousarray(np.stack([_fm(np.asarray(b_ada)[l], 48) for l in range(DEPTH)], axis=1))
    gmixT = np.ascontiguousarray(np.stack([_fm(np.asarray(norm_mix)[l], 8) for l in range(DEPTH)], axis=1))
    gmlpT = np.ascontiguousarray(np.stack([_fm(np.asarray(norm_mlp)[l], 8) for l in range(DEPTH)], axis=1))
    shared = {
        "ident": ident,
        "w_ada": np.ascontiguousarray(np.asarray(w_ada, f32)),
        "b_adaT": b_adaT, "gmixT": gmixT, "gmlpT": gmlpT,
        "w_mlp_in": np.ascontiguousarray(np.asarray(w_mlp_in, f32)),
        "w_mlp_out": np.ascontiguousarray(np.asarray(w_mlp_out, f32)),
    }
    inv = (10000.0 ** (-np.arange(0, 64, 2, dtype=f32) / f32(64))).astype(f32)
    row = np.repeat(np.arange(16, dtype=f32), 64)
    col = np.tile(np.arange(64, dtype=f32), 16)
    ang = np.concatenate([row[:, None] * inv, col[:, None] * inv], axis=-1).astype(f32)
    tok_pm = lambda a: np.ascontiguousarray(a.reshape(8, 128, -1).transpose(1, 0, 2))
    cos_s, sin_s = tok_pm(np.cos(ang).astype(f32)), tok_pm(np.sin(ang).astype(f32))
    cos_p, sin_p = np.ones_like(cos_s), np.zeros_like(sin_s)
    mask_s = np.zeros((128, 48), f32)
    mask_p = np.full((128, 48), -30000.0, f32)
    for kc in range(8):
        mask_p[:, kc * 4 + kc // 2] = 0.0
    gqk = np.concatenate([np.tile(np.asarray(q_norm, f32).reshape(1, 128), (1, 8)),
                          np.tile(np.asarray(k_norm, f32).reshape(1, 128), (1, 2))], axis=1)
    shared.update({
        "w_qkv": np.ascontiguousarray(np.asarray(w_qkv_o, f32)[0]),
        "w_o": np.ascontiguousarray(np.asarray(w_o_o, f32)[0]),
        "gqk": np.ascontiguousarray(np.tile(gqk, (128, 1))),
    })
    wg2 = np.zeros((32, 2, 256), f32)
    for d in range(2):
        wg2[d * 16:(d + 1) * 16, d, :] = np.asarray(gla_w_gate2, f32)[0, d]
    bg = np.asarray(gla_b_gate, f32)[0]
    bgT = np.ascontiguousarray(bg.reshape(2, 2, 128).transpose(2, 0, 1).reshape(128, 4))
    ii = np.arange(128)
    shared.update({
        "w_in": np.ascontiguousarray(np.asarray(w_in_e, f32)[0]),
        "w_out": np.ascontiguousarray(np.asarray(w_out_e, f32)[0]),
        "wg2": wg2, "bgT": bgT,
        "gnorm": np.ascontiguousarray(np.asarray(gla_norm, f32)[0].reshape(128, 1)),
        "tri2": np.ascontiguousarray(np.concatenate([(ii[:, None] <= ii[None, :]).astype(f32),
                                                      (ii[:, None] >= ii[None, :]).astype(f32)], axis=1)),
    })
    def p_lay(a):
        a = np.asarray(a, f32)
        dd = a.shape[0]
        return np.ascontiguousarray(a.reshape(dd, 2, 16, 64).transpose(1, 3, 0, 2).reshape(128, dd, 16))
    lamP = np.ascontiguousarray(np.stack([p_lay(np.asarray(s5_lambda_re)[0]), p_lay(np.asarray(s5_lambda_im)[0])], axis=1))
    ldt = np.asarray(s5_log_dt, f32)[0]
    ldtP = np.ascontiguousarray(np.broadcast_to(ldt.reshape(2, 2, 1, 16).transpose(1, 2, 0, 3), (2, 64, 2, 16)).reshape(128, 2, 16))
    def bp_lay(a):
        return np.asarray(a, f32).reshape(2, 16, 64, 16).transpose(0, 2, 1, 3).reshape(128, 16, 16)
    def cp_lay(a):
        return np.asarray(a, f32).reshape(2, 16, 16, 64).transpose(0, 3, 1, 2).reshape(128, 16, 16)
    BPh = np.ascontiguousarray(np.stack([bp_lay(np.asarray(s5_b_re)[0]), bp_lay(np.asarray(s5_b_im)[0])], axis=1))
    CPh = np.ascontiguousarray(np.stack([cp_lay(np.asarray(s5_c_re)[0]), cp_lay(np.asarray(s5_c_im)[0])], axis=1))
    sc_i = np.arange(128)
    s_of, c_of = sc_i // 16, sc_i % 16
    dS = np.ascontiguousarray(np.asarray(s5_d, f32)[0].reshape(32, 16)[:, c_of].T)
    Mf = (s_of[None, :] >= s_of[:, None]).astype(f32)
    Mb = (s_of[:, None] >= s_of[None, :]).astype(f32)
    E8 = np.zeros((128, 8, 240), f32)
    for a_ in range(8):
        for cc_ in range(16):
            E8[a_ * 16 + cc_, a_, cc_ + 112] = 1.0
    s5in = np.concatenate([lamP.reshape(128, 64), ldtP.reshape(128, 32), BPh.reshape(128, 512), CPh.reshape(128, 512),
                           np.tile(np.arange(-7, 9, dtype=f32)[None, :], (128, 1)), Mf, Mb], axis=1).astype(f32)
    shared.update({
        "s5in": np.ascontiguousarray(s5in),
        "jv": np.ascontiguousarray(np.tile(np.arange(128, dtype=f32)[None, :], (128, 1))),
        "dS": dS,
        "E8": np.ascontiguousarray(E8.reshape(128, 8 * 240)),
        "w_glu": np.ascontiguousarray(np.asarray(s5_w_glu, f32)[0]),
        "bgluT": _fm(np.asarray(s5_b_glu)[0], 4),
    })
    state_s5_re = np.asarray(state_s5_re, f32)
    state_s5_im = np.asarray(state_s5_im, f32)
    state_gla = np.asarray(state_gla, f32)
    cache_k = np.asarray(cache_k, f32)
    cache_v = np.asarray(cache_v, f32)
    in_maps = []
    for core in range(8):
        m = dict(shared)
        if core < 4:
            m["x"] = np.ascontiguousarray(x_prompt[4 * core:4 * core + 4].reshape(NT, D))
            m["condT"] = _fm(c_ctx, 8)
            m["cache_k"] = np.zeros((512, 256), f32)
            m["cache_v"] = np.zeros((512, 256), f32)
            m["maskb"], m["ropecos"], m["ropesin"] = mask_p, cos_p, sin_p
            m["cm"] = np.zeros((128, 1), f32)
            m["initP"] = np.zeros((128, 2, 2, 16), f32)
            m["gla_init"] = np.zeros((2, 256, 128), f32)
        else:
            b = core - 4
            m["x"] = np.ascontiguousarray(x_sample[b])
            m["condT"] = _fm(np.asarray(c)[b], 8)
            m["cache_k"] = np.ascontiguousarray(cache_k[b, 0].reshape(512, 256))
            m["cache_v"] = np.ascontiguousarray(cache_v[b, 0].reshape(512, 256))
            m["maskb"], m["ropecos"], m["ropesin"] = mask_s, cos_s, sin_s
            m["cm"] = np.ones((128, 1), f32)
            m["initP"] = np.ascontiguousarray(np.stack([p_lay(state_s5_re[b, 0]), p_lay(state_s5_im[b, 0])], axis=1))
            m["gla_init"] = np.ascontiguousarray(state_gla[b, 0].reshape(2, 256, 128))
        in_maps.append(m)
    res = run_bass_kernel_spmd(nc, in_maps, core_ids=list(range(8)))
    r = res.results
    y_prompt = np.concatenate([r[i]["y"].reshape(4, 256, D) for i in range(4)], axis=0)
    y_sample = np.stack([r[4 + i]["y"] for i in range(4)], axis=0)
    new_k = np.concatenate([r[i]["k_out"].reshape(4, 1, 256, 2, 128) for i in range(4)], axis=0)
    new_v = np.concatenate([r[i]["v_out"].reshape(4, 1, 256, 2, 128) for i in range(4)], axis=0)
    new_gla = np.concatenate([r[i]["gla_out"].reshape(4, 1, 2, 4, 64, 128) for i in range(4)], axis=0)
    new_re = np.concatenate([r[i]["s5re_out"].reshape(4, 1, 2, 32, 64) for i in range(4)], axis=0)
    new_im = np.concatenate([r[i]["s5im_out"].reshape(4, 1, 2, 32, 64) for i in range(4)], axis=0)
    return (y_prompt, y_sample, new_re, new_im, new_gla, new_k, new_v)
```
